# Optimizing a Trainium2 kernel written in Bass

```python
import jax, jax.numpy as jnp
from jax import lax
import numpy as np

D_MODEL = 1024
BATCH = 16
SEQ = 2048
DEPTH = 4

A_HEADS = 4
A_DK = 128
A_DV = 128
A_KD = A_HEADS * A_DK
A_WIDTH = A_HEADS * A_DV
A_CHUNK = 64
B_Q_HEADS = 8
B_KV_HEADS = 2
B_HEAD_DIM = 64
B_WIDTH = B_Q_HEADS * B_HEAD_DIM
B_KV_WIDTH = B_KV_HEADS * B_HEAD_DIM
WINDOW = 128
ROPE_DIM = B_HEAD_DIM // 4
ROPE_THETA = 500000.0
AB_IN_WIDTH = 2 * A_KD + 2 * A_WIDTH + B_WIDTH + 2 * B_KV_WIDTH
AB_OUT_WIDTH = A_WIDTH + B_WIDTH
POOL_WINDOWS = (2, 4, 8, 16)
POOL_GROUPS = 4
POOL_GROUP = D_MODEL // POOL_GROUPS
D_FF = 2816
CONV_WIDTH = 3
N_EVEN = (DEPTH + 1) // 2
N_ODD = DEPTH // 2
ALPHA = (2.0 * DEPTH) ** 0.25
BETA = (8.0 * DEPTH) ** -0.25
LN_EPS = 1e-5
RMS_EPS = 1e-6

kernel_name = "hybrid_hgrn2_swa_pool_convffn_deepnorm"

F32 = jnp.float32


def _layer_norm(x, g, b):
    xf = x.astype(F32)
    mu = xf.mean(-1, keepdims=True)
    var = jnp.mean(jnp.square(xf - mu), -1, keepdims=True)
    return ((xf - mu) * lax.rsqrt(var + LN_EPS) * g + b).astype(x.dtype)


def _rope(x, cos, sin):
    half = ROPE_DIM // 2
    x1 = x[..., :half].astype(F32)
    x2 = x[..., half:ROPE_DIM].astype(F32)
    r1 = x1 * cos - x2 * sin
    r2 = x2 * cos + x1 * sin
    return jnp.concatenate([r1.astype(x.dtype), r2.astype(x.dtype), x[..., ROPE_DIM:]], -1)


def _hgrn2(q, fz, i, lb):
    Bn, T = q.shape[:2]
    fzf = fz.astype(F32)
    f = lb + (1.0 - lb) * jax.nn.sigmoid(fzf)
    logf = jnp.log(f)
    k = (1.0 - lb) * jax.nn.sigmoid(-fzf)
    nc = T // A_CHUNK

    def to_chunks(a):
        a = a.reshape(Bn, nc, A_CHUNK, A_HEADS, a.shape[-1])
        return a.transpose(1, 0, 3, 2, 4)

    qc, kc, vc, gc = (to_chunks(a) for a in (q.astype(F32), k, i.astype(F32), logf))
    causal = jnp.tril(jnp.ones((A_CHUNK, A_CHUNK), bool))[:, :, None]

    def step(S, inp):
        qb, kb, vb, gb = inp
        b = jnp.cumsum(gb, axis=2)
        o_inter = jnp.einsum('bhtk,bhkv->bhtv', qb * jnp.exp(b), S)
        diff = b[:, :, :, None, :] - b[:, :, None, :, :]
        decay = jnp.exp(jnp.where(causal, diff, -jnp.inf))
        att = jnp.einsum('bhtk,bhsk,bhtsk->bhts', qb, kb, decay)
        o = o_inter + jnp.einsum('bhts,bhsv->bhtv', att, vb)
        b_last = b[:, :, -1:, :]
        S = jnp.exp(b_last[:, :, 0])[..., None] * S + jnp.einsum(
            'bhsk,bhsv->bhkv', kb * jnp.exp(b_last - b), vb)
        return S, o

    S0 = jnp.zeros((Bn, A_HEADS, A_DK, A_DV), F32)
    _, o = lax.scan(step, S0, (qc, kc, vc, gc))
    return o.transpose(1, 0, 3, 2, 4).reshape(Bn, T, A_HEADS, A_DV)


def _swa(q, k, v, sinks):
    Bn, T = q.shape[:2]
    nb = T // WINDOW
    G = B_Q_HEADS // B_KV_HEADS
    qb = q.reshape(Bn, nb, WINDOW, B_KV_HEADS, G, B_HEAD_DIM)

    def band(a):
        a = a.reshape(Bn, nb, WINDOW, B_KV_HEADS, B_HEAD_DIM)
        prev = jnp.pad(a, ((0, 0), (1, 0), (0, 0), (0, 0), (0, 0)))[:, :-1]
        return jnp.concatenate([prev, a], axis=2)

    kb, vb = band(k), band(v)
    s = jnp.einsum('bnqhgd,bnkhd->bnhgqk', qb, kb).astype(F32) * (B_HEAD_DIM ** -0.5)
    qi = jnp.arange(WINDOW)[:, None] + WINDOW
    ki = jnp.arange(2 * WINDOW)[None, :]
    rel = qi - ki
    allowed = (rel >= 0) & (rel < WINDOW)
    blk = jnp.arange(nb)[:, None, None]
    valid = allowed[None] & ((blk > 0) | (ki[None] >= WINDOW))
    s = jnp.where(valid[None, :, None, None], s, -jnp.inf)
    sink = sinks.astype(F32).reshape(B_KV_HEADS, G)[None, None, :, :, None, None]
    m = jnp.maximum(s.max(-1, keepdims=True), sink)
    p = jnp.exp(s - m)
    p = p / (p.sum(-1, keepdims=True) + jnp.exp(sink - m))
    o = jnp.einsum('bnhgqk,bnkhd->bnqhgd', p.astype(v.dtype), vb)
    return o.reshape(Bn, T, B_WIDTH)


def _even_mixer(h, w_in, w_out, lb, norm_g, sinks, cos, sin):
    Bn, T, _ = h.shape
    z = h @ w_in
    widths = [A_KD, A_KD, A_WIDTH, A_WIDTH, B_WIDTH, B_KV_WIDTH, B_KV_WIDTH]
    idx = np.cumsum(widths)[:-1].tolist()
    aq, af, ai, ag, bq, bk, bv = jnp.split(z, idx, axis=-1)
    oa = _hgrn2(aq.reshape(Bn, T, A_HEADS, A_DK), af.reshape(Bn, T, A_HEADS, A_DK),
                ai.reshape(Bn, T, A_HEADS, A_DV), lb.reshape(A_HEADS, A_DK))
    oa = oa * lax.rsqrt(jnp.mean(oa * oa, -1, keepdims=True) + RMS_EPS) * norm_g
    oa = oa.reshape(Bn, T, A_WIDTH).astype(h.dtype) * jax.nn.silu(ag)
    q = _rope(bq.reshape(Bn, T, B_Q_HEADS, B_HEAD_DIM), cos, sin)
    k = _rope(bk.reshape(Bn, T, B_KV_HEADS, B_HEAD_DIM), cos, sin)
    v = bv.reshape(Bn, T, B_KV_HEADS, B_HEAD_DIM)
    ob = _swa(q, k, v, sinks).astype(h.dtype)
    return jnp.concatenate([oa, ob], axis=-1) @ w_out


def _pool_mixer(h, w_grp, scale):
    Bn, T, D = h.shape
    hf = h.astype(F32).reshape(Bn, T, POOL_GROUPS, POOL_GROUP)
    t = jnp.arange(T)
    outs = []
    for gi, w in enumerate(POOL_WINDOWS):
        xg = hf[:, :, gi]
        cs = jnp.cumsum(jnp.pad(xg, ((0, 0), (w, 0), (0, 0))), axis=1)
        win_sum = cs[:, w:] - cs[:, :T]
        cnt = jnp.minimum(t + 1, w).astype(F32)[None, :, None]
        outs.append(win_sum / cnt - xg)
    pooled = jnp.stack(outs, axis=2).astype(h.dtype)
    y = jnp.einsum('btgc,gcd->btgd', pooled, w_grp)
    return y.reshape(Bn, T, D) * scale


def _conv_ffn(h, w_up, conv_w, conv_b, w_down):
    T = h.shape[1]
    u, v = jnp.split(h @ w_up, 2, axis=-1)
    up = jnp.pad(u, ((0, 0), (CONV_WIDTH - 1, 0), (0, 0)))
    uc = conv_b
    for j in range(CONV_WIDTH):
        uc = uc + up[:, j:j + T] * conv_w[j]
    return (jax.nn.silu(uc) * v) @ w_down


def setup_inputs(seed: int = 0) -> dict:
    key = jax.random.key(seed)
    ks = jax.random.split(key, 20)
    n = jax.random.normal
    D = D_MODEL
    x = n(ks[0], (BATCH, SEQ, D), F32)
    c = n(ks[1], (BATCH, D), F32)
    offs = jax.random.randint(ks[2], (BATCH, 1), 0, 4096, dtype=jnp.int32)
    positions = offs + jnp.arange(SEQ, dtype=jnp.int32)[None, :]
    ada_w = n(ks[3], (DEPTH, D, 6 * D), F32) * D ** -0.5
    ada_b = 0.02 * n(ks[4], (DEPTH, 6 * D), F32)
    ln_g = 1.0 + 0.02 * n(ks[5], (DEPTH, 2, D), F32)
    ln_b = 0.02 * n(ks[6], (DEPTH, 2, D), F32)
    ab_w_in = n(ks[7], (N_EVEN, D, AB_IN_WIDTH), F32) * D ** -0.5
    ab_w_out = n(ks[8], (N_EVEN, AB_OUT_WIDTH, D), F32) * (AB_OUT_WIDTH ** -0.5 * BETA)
    hgrn_lb_raw = 0.5 * n(ks[9], (N_EVEN, A_KD), F32)
    hgrn_norm_g = 1.0 + 0.02 * n(ks[10], (N_EVEN, A_DV), F32)
    attn_sinks = 0.5 * n(ks[11], (N_EVEN, B_Q_HEADS), F32)
    pool_w = n(ks[12], (N_ODD, POOL_GROUPS, POOL_GROUP, POOL_GROUP), F32) * (POOL_GROUP ** -0.5 * BETA)
    pool_scale = 1.0 + 0.02 * n(ks[13], (N_ODD, D), F32)
    ffn_w_up = n(ks[14], (DEPTH, D, 2 * D_FF), F32) * D ** -0.5
    ffn_conv_w = n(ks[15], (DEPTH, CONV_WIDTH, D_FF), F32) * CONV_WIDTH ** -0.5
    ffn_conv_b = 0.02 * n(ks[16], (DEPTH, D_FF), F32)
    ffn_w_down = n(ks[17], (DEPTH, D_FF, D), F32) * (D_FF ** -0.5 * BETA)
    return {"x": x, "c": c, "positions": positions, "ada_w": ada_w, "ada_b": ada_b,
            "ln_g": ln_g, "ln_b": ln_b, "ab_w_in": ab_w_in, "ab_w_out": ab_w_out,
            "hgrn_lb_raw": hgrn_lb_raw, "hgrn_norm_g": hgrn_norm_g, "attn_sinks": attn_sinks,
            "pool_w": pool_w, "pool_scale": pool_scale, "ffn_w_up": ffn_w_up,
            "ffn_conv_w": ffn_conv_w, "ffn_conv_b": ffn_conv_b, "ffn_w_down": ffn_w_down}


def reference(x, c, positions, ada_w, ada_b, ln_g, ln_b, ab_w_in, ab_w_out,
              hgrn_lb_raw, hgrn_norm_g, attn_sinks, pool_w, pool_scale,
              ffn_w_up, ffn_conv_w, ffn_conv_b, ffn_w_down):
    lb_p = jax.nn.softmax(hgrn_lb_raw.astype(F32), axis=0)
    lower_bounds = jnp.cumsum(lb_p, axis=0) - lb_p[0]
    inv_freq = ROPE_THETA ** (-jnp.arange(0, ROPE_DIM, 2, dtype=F32) / ROPE_DIM)
    ang = positions.astype(F32)[..., None] * inv_freq
    cos = jnp.cos(ang)[:, :, None, :]
    sin = jnp.sin(ang)[:, :, None, :]
    c_act = jax.nn.silu(c)
    for l in range(DEPTH):
        mod = c_act @ ada_w[l] + ada_b[l]
        sh1, sc1, g1, sh2, sc2, g2 = [m[:, None, :] for m in jnp.split(mod, 6, axis=-1)]
        h = x * (1.0 + sc1) + sh1
        if l % 2 == 0:
            e = l // 2
            y = _even_mixer(h, ab_w_in[e], ab_w_out[e], lower_bounds[e], hgrn_norm_g[e],
                            attn_sinks[e], cos, sin)
        else:
            o = l // 2
            y = _pool_mixer(h, pool_w[o], pool_scale[o])
        x = _layer_norm(ALPHA * x + g1 * y, ln_g[l, 0], ln_b[l, 0])
        h = x * (1.0 + sc2) + sh2
        y = _conv_ffn(h, ffn_w_up[l], ffn_conv_w[l], ffn_conv_b[l], ffn_w_down[l])
        x = _layer_norm(ALPHA * x + g2 * y, ln_g[l, 1], ln_b[l, 1])
    return x
```

```python
import numpy as np
import concourse.bass as bass
import concourse.mybir as mybir
from concourse.bass_utils import run_bass_kernel_spmd

F32 = mybir.dt.float32
BF16 = mybir.dt.bfloat16
I32 = mybir.dt.int32
AF = mybir.ActivationFunctionType
OP = mybir.AluOpType
AX = mybir.AxisListType

D = 1024
T = 2048
DEPTH = 4
TW = 256
NT = T // TW
NFF = 22
ALPHA = (2.0 * DEPTH) ** 0.25
LN_EPS = 1e-5
RMS_EPS = 1e-6
R = 31
FF_GROUPS = [list(range(0, 4)), list(range(4, 8)), list(range(8, 12)), list(range(12, 16)),
             list(range(16, 20)), [20, 21]]
NCORES = 8
import os
OPT_LNPOOL = int(os.environ.get('K_LNPOOL', '1'))
OPT_EVAC = int(os.environ.get('K_EVAC', '1'))
OPT_POOLADD = int(os.environ.get('K_POOLADD', '1'))

CF_PERM, CF_MBD, CF_CMASK, CF_INVF, CF_SGN, CF_PI, CF_HPI, CF_2PI, CF_PINV, NCF = 0, 128, 256, 512, 513, 514, 515, 516, 520, 585
CF_INVF2, CF_PHS, CF_PHC, CF_ZERO = 517, 518, 519, 584
CW1 = 6.28125
CW2 = float(2.0 * np.pi - 6.28125)
CB_ID, CB_ONES, CB_MS, CB_MS0, NCB = 0, 128, 256, 512, 768
P_C, P_ADAB, P_LNG, P_LNB, P_LBR, P_NG, P_SINK, P_PSC, P_CW, P_CB = 0, 16, 208, 272, 336, 344, 346, 362, 378, 642
NPAR = 730


def stream_plan(n_layers=DEPTH, n_seq=2):
    plan = [("ada", 0, n) for n in range(48)]
    for s in range(n_seq):
        for l in range(n_layers):
            if l % 2 == 0:
                e = l // 2
                for h in range(4):
                    for base in (0, 512, 1024, 1536):
                        plan.append(("win", e, base + h * 128))
                for c in range(4):
                    plan.append(("win", e, 2048 + c * 128))
                for g in range(2):
                    plan.append(("bkdup", e, g))
                plan.append(("bv", e, 0))
                for n in range(8):
                    plan.append(("wout", e, n))
            else:
                o = l // 2
                plan.append(("pool", o, 0))
                plan.append(("pool", o, 1))
            for gidx, grp in enumerate(FF_GROUPS):
                for j in grp:
                    plan.append(("upu", l, j))
                    plan.append(("upv", l, j))
                    plan.append(("down", l, j))
                if s == 0 and l + 1 < n_layers:
                    plan += [("ada", l + 1, n) for n in range(gidx * 8, gidx * 8 + 8)]
    return plan


def _k1024(w):
    return np.ascontiguousarray(w.reshape(8, 128, 128).transpose(1, 0, 2)).reshape(128, 1024)


def materialize_stream(plan, inp):
    out = np.zeros((len(plan), 128, 1024), np.float32)
    cache = {}
    for i, u in enumerate(plan):
        if u in cache:
            out[i] = out[cache[u]]
            continue
        cache[u] = i
        kind, a, b = u
        if kind == "ada":
            out[i] = _k1024(inp["ada_w"][a][:, b * 128:(b + 1) * 128])
        elif kind == "win":
            out[i] = _k1024(inp["ab_w_in"][a][:, b:b + 128])
        elif kind == "bkdup":
            w = inp["ab_w_in"][a][:, 2560 + b * 64:2560 + (b + 1) * 64]
            out[i] = _k1024(np.concatenate([w, w], axis=1))
        elif kind == "bv":
            out[i] = _k1024(inp["ab_w_in"][a][:, 2688:2816])
        elif kind == "wout":
            out[i] = _k1024(inp["ab_w_out"][a][:, b * 128:(b + 1) * 128])
        elif kind == "pool":
            blk = np.zeros((128, 8, 128), np.float32)
            for gl in range(2):
                for m in range(2):
                    for kc in range(2):
                        blk[:, (gl * 2 + m) * 2 + kc, :] = inp["pool_w"][a][b * 2 + gl][kc * 128:(kc + 1) * 128, m * 128:(m + 1) * 128]
            out[i] = blk.reshape(128, 1024)
        elif kind == "upu":
            out[i] = _k1024(inp["ffn_w_up"][a][:, b * 128:(b + 1) * 128])
        elif kind == "upv":
            out[i] = _k1024(inp["ffn_w_up"][a][:, 2816 + b * 128:2816 + (b + 1) * 128])
        elif kind == "down":
            out[i] = inp["ffn_w_down"][a][b * 128:(b + 1) * 128, :]
    return out


def make_consts():
    cf = np.zeros((128, NCF), np.float32)
    cb = np.zeros((128, NCB), np.float32)
    p = np.arange(128)
    d = p % 64
    perm = np.zeros((128, 128), np.float32)
    for m in range(128):
        dm = m % 64
        if dm < 8:
            perm[m + 8, m] = 1.0
        elif dm < 16:
            perm[m - 8, m] = 1.0
    cf[:, CF_PERM:CF_PERM + 128] = perm
    s = p[:, None]
    t = p[None, :]
    cf[:, CF_MBD:CF_MBD + 128] = ((s // 32 == t // 32) & (s <= t)).astype(np.float32)
    cm = np.ones(256, np.float32)
    cm[::32] = 0.0
    cf[:, CF_CMASK:CF_CMASK + 256] = cm[None, :]
    invf = 500000.0 ** (-np.arange(0, 16, 2, dtype=np.float32) / 16.0)
    iv = np.zeros(128, np.float32)
    sg = np.zeros(128, np.float32)
    for r in range(128):
        dd = r % 64
        if dd < 8:
            iv[r] = invf[dd]
            sg[r] = -1.0
        elif dd < 16:
            iv[r] = invf[dd - 8]
            sg[r] = 1.0
    cf[:, CF_INVF] = iv
    cf[:, CF_SGN] = sg
    cf[:, CF_PI] = np.float32(np.pi)
    cf[:, CF_HPI] = np.float32(np.pi / 2)
    cf[:, CF_2PI] = np.float32(2 * np.pi)
    cf[:, CF_INVF2] = (iv.astype(np.float64) / (2 * np.pi)).astype(np.float32)
    cf[:, CF_PHS] = 0.5
    cf[:, CF_PHC] = 0.75
    for gi, w in enumerate((2, 4, 8, 16)):
        cnt = np.minimum(np.arange(16) + 1, w).astype(np.float32)
        cf[:, CF_PINV + gi * 16:CF_PINV + (gi + 1) * 16] = (1.0 / cnt)[None, :]
    cb[:, CB_ID:CB_ID + 128] = np.eye(128, dtype=np.float32)
    cb[:, CB_ONES:CB_ONES + 128] = 1.0
    q = p[:, None]
    ki = np.arange(256)[None, :]
    rel = q + 128 - ki
    allowed = (rel >= 0) & (rel < 128)
    cb[:, CB_MS:CB_MS + 256] = np.where(allowed, 0.0, -30000.0)
    cb[:, CB_MS0:CB_MS0 + 256] = np.where(allowed & (ki >= 128), 0.0, -30000.0)
    return cf, cb


def make_params(inp, core):
    pr = np.zeros((128, NPAR), np.float32)

    def fm(v):
        v = np.asarray(v)
        n = v.shape[-1] // 128
        v = v.reshape(v.shape[:-1] + (n, 128))
        return np.moveaxis(v, -1, 0)

    pr[:, P_C:P_C + 16] = fm(inp["c"][2 * core:2 * core + 2]).transpose(0, 2, 1).reshape(128, 16)
    pr[:, P_ADAB:P_ADAB + 192] = fm(inp["ada_b"]).reshape(128, 192)
    pr[:, P_LNG:P_LNG + 64] = fm(inp["ln_g"]).reshape(128, 64)
    pr[:, P_LNB:P_LNB + 64] = fm(inp["ln_b"]).reshape(128, 64)
    pr[:, P_LBR:P_LBR + 8] = fm(inp["hgrn_lb_raw"]).reshape(128, 8)
    pr[:, P_NG:P_NG + 2] = fm(inp["hgrn_norm_g"]).reshape(128, 2)
    pr[:, P_SINK:P_SINK + 16] = np.asarray(inp["attn_sinks"]).reshape(1, 16)
    pr[:, P_PSC:P_PSC + 16] = fm(inp["pool_scale"]).reshape(128, 16)
    pr[:, P_CW:P_CW + 264] = fm(inp["ffn_conv_w"]).reshape(128, 264)
    pr[:, P_CB:P_CB + 88] = fm(inp["ffn_conv_b"]).reshape(128, 88)
    return pr


class Tk:
    __slots__ = ("ap", "buf", "lo", "hi", "w", "r", "sem", "cnt", "ov", "ovv")

    def __init__(self, ap, buf, lo=0, hi=1):
        self.ap, self.buf, self.lo, self.hi = ap, buf, lo, hi
        self.w = None
        self.r = {}
        self.sem = None
        self.cnt = 0
        self.ov = None
        self.ovv = -1


class Eng:
    def __init__(self, k, eng, name, skip_own=False):
        self.k, self.eng, self.name, self.skip_own = k, eng, name, skip_own
        self.seen = {}
        self.pend = []
        self.sems = set()
        self.newsem()

    def newsem(self):
        self.sem = self.k.nc.alloc_semaphore(name="%s_s%d" % (self.name, self.k.nsem))
        self.k.nsem += 1
        self.cnt = 0
        self.sems.add(self.sem.num)


class Prog:
    def __init__(self, nc):
        self.nc = nc
        self.nsem = 0
        self.bufs = {}
        self.nbuf = 0
        self.sb_off = 16384
        self.PE = Eng(self, nc.tensor, "pe", skip_own=True)
        self.ACT = Eng(self, nc.scalar, "act")
        self.DVE = Eng(self, nc.vector, "dve")
        self.POOL = Eng(self, nc.gpsimd, "pool")
        self.SP = Eng(self, nc.sync, "sp")

    def sb(self, name, shape, dt, off=None):
        esz = 4 if dt in (F32, I32) else 2
        nbytes = int(np.prod(shape[1:])) * esz
        if off is None:
            off = self.sb_off
            self.sb_off += (nbytes + 31) // 32 * 32
        h = self.nc.alloc_sbuf_tensor_at(name, list(shape), dt, offset=off)
        return h, off, nbytes

    def tile(self, name, shape, dt):
        h, off, nb = self.sb(name, shape, dt)
        return self.reg(Tk(h[:], "sb_" + name, off, off + nb))

    def tile_at(self, name, shape, dt, off):
        h, off, nb = self.sb(name, shape, dt, off)
        return self.reg(Tk(h[:], "arena", off, off + nb))

    def reg(self, t):
        self.bufs.setdefault(t.buf, []).append(t)
        self.ver = getattr(self, "ver", 0) + 1
        return t

    def sub(self, parent_ap, buf, lo, hi):
        return self.reg(Tk(parent_ap, buf, lo, hi))

    def over(self, t):
        if t.ovv != self.ver:
            t.ov = [t2 for t2 in self.bufs[t.buf] if t2.lo < t.hi and t.lo < t2.hi]
            t.ovv = self.ver
        return t.ov

    def _need(self, reads, writes):
        need = {}

        def add(tok):
            if tok is None:
                return
            sem, val = tok
            if need.get(sem.num, (None, 0))[1] < val:
                need[sem.num] = (sem, val)
        for t in reads:
            psum = t.buf.startswith("pb")
            for t2 in self.over(t):
                add(t2.w)
                if psum:
                    for tok in t2.r.values():
                        add(tok)
        for t in writes:
            for t2 in self.over(t):
                add(t2.w)
                for tok in t2.r.values():
                    add(tok)
        return need

    def _wait(self, e, need):
        for key, (sem, val) in need.items():
            if e.skip_own and key in e.sems:
                continue
            if e.seen.get(key, 0) < val:
                e.eng.wait_ge(sem, val)
                e.seen[key] = val

    def _commit(self, tok, reads, writes):
        for t in writes:
            t.w = tok
            t.r = {}
        for t in reads:
            t.r[tok[0].num] = tok

    def op(self, e, fn, reads=(), writes=(), inc=True):
        self._wait(e, self._need(reads, writes))
        ins = fn()
        if not inc:
            e.pend.append((reads, writes))
            return
        if e.cnt >= 40000:
            e.newsem()
        e.cnt += 1
        ins.then_inc(e.sem, 1)
        tok = (e.sem, e.cnt)
        for (rs, ws) in e.pend:
            self._commit(tok, rs, ws)
        e.pend = []
        self._commit(tok, reads, writes)

    def dma(self, e, out_t, out_ap, in_ap, reads=(), semt=None):
        st = semt if semt is not None else (out_t[0] if isinstance(out_t, (list, tuple)) else out_t)
        if st.sem is None:
            st.sem = self.nc.alloc_semaphore(name="d%d" % self.nsem)
            self.nsem += 1
        writes = list(out_t) if isinstance(out_t, (list, tuple)) else ([out_t] if out_t is not None else [])
        if semt is None:
            st = writes[0]
            if st.sem is None:
                st.sem = self.nc.alloc_semaphore(name="d%d" % self.nsem)
                self.nsem += 1
        self._wait(e, self._need(reads, writes))
        e.eng.dma_start(out=out_ap, in_=in_ap).then_inc(st.sem, 16)
        st.cnt += 16
        self._commit((st.sem, st.cnt), reads, writes)


def build_program(n_layers=DEPTH, n_seq=2, nunits=None):
    nc = bass.Bass("TRN2", target_bir_lowering=False)
    plan = stream_plan(n_layers, n_seq)
    U = len(plan)
    x_d = nc.dram_tensor("xfm", [2, 128, 8, T], F32, kind="ExternalInput").ap()
    pos_d = nc.dram_tensor("pos", [2, T], I32, kind="ExternalInput").ap()
    cf_d = nc.dram_tensor("cf", [128, NCF], F32, kind="ExternalInput").ap()
    cb_d = nc.dram_tensor("cb", [128, NCB], F32, kind="ExternalInput").ap()
    par_d = nc.dram_tensor("par", [128, NPAR], F32, kind="ExternalInput").ap()
    ws_d = nc.dram_tensor("wstream", [U, 128, 1024], F32, kind="ExternalInput").ap()
    out_d = nc.dram_tensor("ofm", [2, 128, 8, T], F32, kind="ExternalOutput").ap()

    k = Prog(nc)
    PE, ACT, DVE, POOL, SP = k.PE, k.ACT, k.DVE, k.POOL, k.SP
    pe, act, dve, pool = nc.tensor, nc.scalar, nc.vector, nc.gpsimd
    LNB = {"b": 6}

    xa_h, xa_off, _ = k.sb("xa", [128, 8, T], F32)
    H_h, H_off, _ = k.sb("H", [128, 8, T], BF16)
    xa = [[k.reg(Tk(xa_h[:, kc, tt * TW:(tt + 1) * TW], "xa%d_%d" % (kc, tt))) for tt in range(NT)] for kc in range(8)]
    Hh = [[k.reg(Tk(H_h[:, kc, tt * TW:(tt + 1) * TW], "H%d_%d" % (kc, tt))) for tt in range(NT)] for kc in range(8)]
    ring = [k.tile("ring%d" % i, [128, 1024], BF16) for i in range(R)]
    cf = k.tile("cf", [128, NCF], F32)
    cb = k.tile("cb", [128, NCB], BF16)
    par = k.tile("par", [128, NPAR], F32)
    modT = k.tile("modT", [128, DEPTH, 48, 2], F32)
    coef = k.tile("coef", [128, 8, 8], F32)
    ctmp = k.tile("ctmp", [128, 8], F32)
    cact = k.tile("cact", [128, 8, 2], BF16)
    lbt = k.tile("lbt", [128, 2, 2, 4], F32)
    ngs = k.tile("ngs", [128, 2], F32)
    h1 = [k.tile("h1_%d" % kc, [128, TW], BF16) for kc in range(8)]
    cat = [k.tile("cat_%d" % kc, [128, TW], BF16) for kc in range(8)]
    S32 = [k.tile("S32_%d" % h, [128, 128], F32) for h in range(4)]
    Sbf = [k.tile("Sbf_%d" % h, [128, 128], BF16) for h in range(4)]
    kTc = [k.tile("kTc_%d" % g, [128, 128 + TW], BF16) for g in range(2)]
    vpad = [k.tile("vpad_%d" % g, [128, 3, 192], BF16) for g in range(2)]
    phalo = k.tile("phalo", [128, 8, 16], F32)
    arena0 = k.sb_off
    ARENA = 21 * 1024
    k.sb_off += ARENA
    assert k.sb_off <= 16384 + 212800, k.sb_off
    ar = {"off": arena0}
    acache = {}
    pcache = {}

    def a_reset():
        ar["off"] = arena0

    def a_tile(name, shape, dt):
        esz = 4 if dt in (F32, I32) else 2
        nb = int(np.prod(shape[1:])) * esz
        off = ar["off"]
        ar["off"] += (nb + 31) // 32 * 32
        assert ar["off"] <= arena0 + ARENA, (name, ar["off"] - arena0)
        key = (off, tuple(shape), str(dt))
        if key not in acache:
            acache[key] = k.tile_at("ar_%s_%d" % (name, k.nbuf_inc()), shape, dt, off)
        return acache[key]

    def nbuf_inc():
        k.nbuf += 1
        return k.nbuf
    k.nbuf_inc = nbuf_inc

    Bk = []
    for i in range(7):
        ph = nc.alloc_psum_tensor("pb%d" % i, [128, 512], F32)
        Bk.append(ph)
    B7h = nc.alloc_psum_tensor("pb7", [128, 1024], BF16)

    def pt(i, lo, hi):
        key = (i, lo, hi)
        if key not in pcache:
            if i == 7:
                pcache[key] = k.sub(B7h[:, lo:hi], "pb7", 0, 1)
            else:
                pcache[key] = k.sub(Bk[i][:, lo:hi], "pb%d" % i, 0, 1)
        return pcache[key]

    ws = {"cur": 0, "emitted": 0, "done": 0}

    def emit_dma(u):
        slot = ring[u % R]
        k.dma(POOL, slot, slot.ap, ws_d[u])

    def ws_prime():
        while ws["emitted"] < min(R, U):
            emit_dma(ws["emitted"])
            ws["emitted"] += 1

    def ws_get(n, kinds=None):
        res = []
        for i in range(n):
            u = ws["cur"]
            assert u < ws["emitted"], ("ring deadlock", u)
            if kinds is not None:
                assert plan[u][0] == kinds[i], (plan[u], kinds[i])
            res.append(ring[u % R])
            ws["cur"] += 1
        return res

    def ws_done(n):
        for i in range(n):
            ws["done"] += 1
            if ws["emitted"] < U:
                assert ws["emitted"] - ws["done"] < R
                emit_dma(ws["emitted"])
                ws["emitted"] += 1

    def w3(slot):
        return slot.ap.rearrange("p (k c) -> p k c", c=128)

    k.dma(SP, cf, cf.ap, cf_d)
    k.dma(SP, par, par.ap, par_d)
    k.dma(POOL, cb, cb.ap, cb_d)
    ws_prime()
    ident = cb.ap[:, CB_ID:CB_ID + 128]
    ones = cb.ap[:, CB_ONES:CB_ONES + 128]
    pc = lambda c0, n=1: par.ap[:, c0:c0 + n]
    cfc = lambda c0: cf.ap[:, c0:c0 + 1]

    k.op(ACT, lambda: act.activation(cact.ap.rearrange("p k b -> p (k b)"), par.ap[:, P_C:P_C + 16], AF.Silu), [par], [cact])
    lb2 = lbt.ap
    k.op(DVE, lambda: dve.memset(lb2[:, 0, 0, :], 0.0), [], [lbt])
    k.op(DVE, lambda: dve.tensor_tensor(out=lb2[:, 1, 1, :], in0=par.ap[:, P_LBR + 4:P_LBR + 8], in1=par.ap[:, P_LBR:P_LBR + 4], op=OP.subtract), [par], [lbt])
    k.op(ACT, lambda: act.activation(lb2[:, 1, 0, :], lb2[:, 1, 1, :], AF.Sigmoid), [lbt], [lbt])
    k.op(DVE, lambda: dve.memset(lb2[:, 0, 1, :], 1.0), [], [lbt])
    k.op(DVE, lambda: dve.tensor_scalar(out=lb2[:, 1, 1, :], in0=lb2[:, 1, 0, :], scalar1=-1.0, scalar2=1.0, op0=OP.mult, op1=OP.add), [lbt], [lbt])
    k.op(DVE, lambda: dve.tensor_scalar(out=ngs.ap, in0=par.ap[:, P_NG:P_NG + 2], scalar1=float(np.sqrt(128.0)), scalar2=None, op0=OP.mult), [par], [ngs])

    def compute_mod(l, n0=0, n1=48):
        nn = n1 - n0
        mps = pt(6, 0, 96)
        for n in range(n0, n1):
            (u,) = ws_get(1, ["ada"])
            assert plan[ws["cur"] - 1] == ("ada", l, n)
            c0 = (n - n0) * 2
            for kc in range(8):
                k.op(PE, lambda kc=kc: pe.matmul(mps.ap[:, c0:c0 + 2], w3(u)[:, kc, :], cact.ap[:, kc, :], start=(kc == 0), stop=(kc == 7)),
                     [u, cact], [mps], inc=(kc == 7))
            ws_done(1)
        k.op(DVE, lambda: dve.tensor_tensor(out=modT.ap[:, l, n0:n1, :], in0=mps.ap[:, 0:2 * nn].rearrange("p (n b) -> p n b", b=2),
                                            in1=par.ap[:, P_ADAB + l * 48 + n0:P_ADAB + l * 48 + n1].rearrange("p (n o) -> p n o", o=1).to_broadcast([128, nn, 2]),
                                            op=OP.add), [mps, par], [modT])

    compute_mod(0)

    CO_S1, CO_G1, CO_A1, CO_A0, CO_B1, CO_B0, CO_C1, CO_C0 = range(8)

    def compute_coefs(l, b, last):
        m = modT.ap[:, l]
        md = lambda i: m[:, i * 8:(i + 1) * 8, b]
        co = lambda i: coef.ap[:, i, :]
        lng = lambda j: par.ap[:, P_LNG + (l * 2 + j) * 8:P_LNG + (l * 2 + j) * 8 + 8]
        lnb = lambda j: par.ap[:, P_LNB + (l * 2 + j) * 8:P_LNB + (l * 2 + j) * 8 + 8]
        rw = [modT, par, coef, ctmp]
        k.op(DVE, lambda: dve.tensor_scalar(out=co(CO_S1), in0=md(1), scalar1=1.0 / ALPHA, scalar2=1.0 / ALPHA, op0=OP.mult, op1=OP.add), rw, [coef])
        if l % 2 == 1:
            o = l // 2
            k.op(DVE, lambda: dve.tensor_tensor(out=co(CO_G1), in0=md(2), in1=par.ap[:, P_PSC + o * 8:P_PSC + o * 8 + 8], op=OP.mult), rw, [coef])
        else:
            k.op(DVE, lambda: dve.tensor_copy(co(CO_G1), md(2)), rw, [coef])
        k.op(DVE, lambda: dve.tensor_scalar(out=co(CO_A1), in0=lng(0), scalar1=ALPHA, scalar2=None, op0=OP.mult), rw, [coef])
        k.op(DVE, lambda: dve.tensor_scalar(out=co(CO_A0), in0=lnb(0), scalar1=ALPHA, scalar2=None, op0=OP.mult), rw, [coef])
        k.op(DVE, lambda: dve.tensor_scalar(out=ctmp.ap, in0=md(4), scalar1=1.0, scalar2=None, op0=OP.add), rw, [ctmp])
        k.op(DVE, lambda: dve.tensor_tensor(out=co(CO_B1), in0=lng(0), in1=ctmp.ap, op=OP.mult), rw, [coef])
        k.op(DVE, lambda: dve.tensor_tensor(out=co(CO_B0), in0=lnb(0), in1=ctmp.ap, op=OP.mult), rw, [coef])
        k.op(DVE, lambda: dve.tensor_tensor(out=co(CO_B0), in0=co(CO_B0), in1=md(3), op=OP.add), rw, [coef])
        a2 = 1.0 if last else ALPHA
        k.op(DVE, lambda: dve.tensor_scalar(out=co(CO_C1), in0=lng(1), scalar1=a2, scalar2=None, op0=OP.mult), rw, [coef])
        k.op(DVE, lambda: dve.tensor_scalar(out=co(CO_C0), in0=lnb(1), scalar1=a2, scalar2=None, op0=OP.mult), rw, [coef])

    cc = lambda i, kc: coef.ap[:, i, kc:kc + 1]

    def layer_norm(tt, c1, c0, hb=None, out_dma=None):
        a_reset()
        rb = [a_tile("rb", [128, TW], BF16) for _ in range(3)]
        rsq = [a_tile("rsq", [128, TW], BF16) for _ in range(3)]
        mu = a_tile("mu", [128, TW], F32)
        msq = a_tile("msq", [128, TW], F32)
        rstd = a_tile("rstd", [128, TW], F32)
        nrm = [a_tile("nrm", [128, TW], F32) for _ in range(3)]
        psum_s = pt(LNB["b"], 0, TW)
        psum_q = pt(LNB["b"], TW, 2 * TW)
        for n in range(8):
            x_t = xa[n][tt]
            k.op(DVE, lambda: dve.tensor_copy(rb[n % 3].ap, x_t.ap), [x_t], [rb[n % 3]])
            k.op(ACT, lambda: act.activation(rsq[n % 3].ap, x_t.ap, AF.Square), [x_t], [rsq[n % 3]])
            k.op(PE, lambda: pe.matmul(psum_s.ap, ones, rb[n % 3].ap, start=(n == 0), stop=False), [cb, rb[n % 3]], [psum_s], inc=True)
            k.op(PE, lambda: pe.matmul(psum_q.ap, ones, rsq[n % 3].ap, start=False, stop=(n == 7)), [cb, rsq[n % 3]], [psum_q], inc=True)
        k.op(ACT, lambda: act.activation(mu.ap, psum_s.ap, AF.Identity, scale=1.0 / D), [psum_s], [mu])
        k.op(DVE, lambda: dve.tensor_tensor(out=msq.ap, in0=mu.ap, in1=mu.ap, op=OP.mult), [mu], [msq])
        k.op(DVE, lambda: dve.scalar_tensor_tensor(out=rstd.ap, in0=psum_q.ap, scalar=1.0 / D, in1=msq.ap, op0=OP.mult, op1=OP.subtract), [psum_q, msq], [rstd])
        k.op(DVE, lambda: dve.tensor_scalar(out=rstd.ap, in0=rstd.ap, scalar1=LN_EPS, scalar2=None, op0=OP.add), [rstd], [rstd])
        k.op(ACT, lambda: act.activation(rstd.ap, rstd.ap, AF.Ln), [rstd], [rstd])
        k.op(ACT, lambda: act.activation(rstd.ap, rstd.ap, AF.Exp, scale=-0.5), [rstd], [rstd])
        for n in range(8):
            x_t = xa[n][tt]
            nm = nrm[n % 3]
            k.op(DVE, lambda: dve.tensor_tensor(out=nm.ap, in0=x_t.ap, in1=mu.ap, op=OP.subtract), [x_t, mu], [nm])
            k.op(DVE, lambda: dve.tensor_tensor(out=nm.ap, in0=nm.ap, in1=rstd.ap, op=OP.mult), [nm, rstd], [nm])
            k.op(POOL, lambda: pool.tensor_scalar(out=x_t.ap, in0=nm.ap, scalar1=cc(c1, n), scalar2=cc(c0, n), op0=OP.mult, op1=OP.add), [nm, coef], [x_t])
            if hb is not None:
                k.op(ACT, lambda: act.activation(Hh[n][tt].ap, nm.ap, AF.Identity, scale=cc(hb[0], n), bias=cc(hb[1], n)), [nm, coef], [Hh[n][tt]])
        if out_dma is not None:
            b = out_dma
            semt = xa[0][tt]
            k.dma(SP, None, out_d[b, :, :, tt * TW:(tt + 1) * TW], xa_h[:, :, tt * TW:(tt + 1) * TW], reads=[xa[n][tt] for n in range(8)], semt=semt)
            out_waits.append((semt.sem, semt.cnt))

    out_waits = []

    def proj_head(uA, hd):
        pq, pf = pt(0, 0, TW), pt(0, TW, 2 * TW)
        pg, pv = pt(1, 0, TW), pt(1, TW, 2 * TW)
        uq, uf, ui, ug = uA[hd * 4:hd * 4 + 4]
        for (u, po) in ((uf, pf), (ug, pg), (uq, pq)):
            for kc in range(8):
                k.op(PE, lambda: pe.matmul(po.ap, w3(u)[:, kc, :], h1[kc].ap, start=(kc == 0), stop=(kc == 7)), [u, h1[kc]], [po], inc=(kc == 7))
        for j in range(2):
            for kc in range(8):
                k.op(PE, lambda: pe.matmul(pv.ap[:, j * 128:(j + 1) * 128], h1[kc].ap[:, j * 128:(j + 1) * 128], w3(ui)[:, kc, :], start=(kc == 0), stop=(kc == 7)),
                     [ui, h1[kc]], [pv], inc=(kc == 7))

    def even_prologue(tt, units, b):
        for kc in range(8):
            k.op(ACT, lambda: act.activation(h1[kc].ap, xa[kc][tt].ap, AF.Identity, scale=cc(CO_S1, kc), bias=modT.ap[:, lcur["l"], kc, b:b + 1]),
                 [xa[kc][tt], coef, modT], [h1[kc]])
        proj_head(units[0:16], 0)

    def even_mixer_tile(e, tt, units, b):
        uA = units[0:16]
        uBq = units[16:20]
        uBk = units[20:22]
        uBv = units[22]
        uWo = units[23:31]
        a_reset()
        sig = a_tile("sig", [128, TW], F32)
        lf = a_tile("lf", [128, TW], F32)
        kk = a_tile("kk", [128, TW], F32)
        bb = a_tile("bb", [128, TW], F32)
        bcl = a_tile("bcl", [128, TW], F32)
        eb = a_tile("eb", [128, TW], F32)
        ebe = [a_tile("ebe", [128, 8], F32) for _ in range(4)]
        sg = [a_tile("sg", [128, TW], F32) for _ in range(4)]
        osb, rs = sig, lf
        Qt = [a_tile("Qt", [128, TW], BF16) for _ in range(4)]
        Kt = a_tile("Kt", [128, TW], BF16)
        Kh = [a_tile("Kh", [128, TW], BF16) for _ in range(4)]
        vsb = [a_tile("vsb", [128, 2, 128], BF16) for _ in range(4)]
        KhT = [a_tile("KhT", [128, 128], BF16) for _ in range(4)]
        osq = Kt
        pq, pf = pt(0, 0, TW), pt(0, TW, 2 * TW)
        pg, pv = pt(1, 0, TW), pt(1, TW, 2 * TW)
        patt = pt(2, 0, 128)
        pss = pt(2, 256, 512)
        pkh = pt(7, 0, 128)
        pS = [pt(3 + h % 2, (h // 2) * 128, (h // 2 + 1) * 128) for h in range(4)]
        pO = [pt(5 + h // 2, (h % 2) * TW, (h % 2 + 1) * TW) for h in range(4)]
        qs = a_tile("qs", [128, TW], F32)
        xtra = a_tile("xtra", [128, TW], F32)

        def proj(hd):
            proj_head(uA, hd)

        def chain(hd):
            k.op(ACT, lambda: act.activation(sig.ap, pf.ap, AF.Exp, scale=-1.0), [pf], [sig])
            k.op(ACT, lambda: act.activation(sg[hd].ap, pg.ap, AF.Exp, scale=-1.0), [pg], [sg[hd]])
            k.op(ACT, lambda: act.activation(qs.ap, pq.ap, AF.Copy), [pq], [qs])
            k.op(ACT, lambda: act.activation(vsb[hd].ap.rearrange("p j d -> p (j d)"), pv.ap, AF.Copy), [pv], [vsb[hd]])
            k.op(DVE, lambda: dve.tensor_scalar(out=sig.ap, in0=sig.ap, scalar1=1.0, scalar2=None, op0=OP.add), [sig], [sig])
            k.op(DVE, lambda: dve.reciprocal(sig.ap, sig.ap), [sig], [sig])
            k.op(DVE, lambda: dve.tensor_scalar(out=sig.ap, in0=sig.ap, scalar1=lbt.ap[:, e, 1, hd:hd + 1], scalar2=lbt.ap[:, e, 0, hd:hd + 1], op0=OP.mult, op1=OP.add), [sig, lbt], [sig])
            k.op(DVE, lambda: dve.tensor_scalar(out=sg[hd].ap, in0=sg[hd].ap, scalar1=1.0, scalar2=None, op0=OP.add), [sg[hd]], [sg[hd]])
            k.op(DVE, lambda: dve.reciprocal(sg[hd].ap, sg[hd].ap), [sg[hd]], [sg[hd]])
            k.op(DVE, lambda: dve.tensor_tensor(out=sg[hd].ap, in0=sg[hd].ap, in1=pg.ap, op=OP.mult), [sg[hd], pg], [sg[hd]])
            k.op(ACT, lambda: act.activation(lf.ap, sig.ap, AF.Ln), [sig], [lf])
            k.op(DVE, lambda: dve.tensor_scalar(out=kk.ap, in0=sig.ap, scalar1=-1.0, scalar2=1.0, op0=OP.mult, op1=OP.add), [sig], [kk])
            k.op(DVE, lambda: dve.tensor_tensor_scan(bb.ap, cf.ap[:, CF_CMASK:CF_CMASK + TW], lf.ap, 0.0, OP.mult, OP.add), [cf, lf], [bb])
            k.op(DVE, lambda: dve.tensor_scalar(out=bcl.ap, in0=bb.ap, scalar1=-80.0, scalar2=None, op0=OP.max), [bb], [bcl])
            k.op(ACT, lambda: act.activation(eb.ap, bb.ap, AF.Exp), [bb], [eb])
            k.op(ACT, lambda: act.activation(bcl.ap, bcl.ap, AF.Exp, scale=-1.0), [bcl], [bcl])
            ebv = eb.ap.rearrange("p (c k) -> p c k", k=32)
            k.op(DVE, lambda: dve.tensor_tensor(out=Qt[hd].ap, in0=qs.ap, in1=eb.ap, op=OP.mult), [qs, eb], [Qt[hd]])
            k.op(DVE, lambda: dve.tensor_copy(ebe[hd].ap.rearrange("p (c o) -> p c o", o=1), ebv[:, :, 31:32]), [eb], [ebe[hd]])
            k.op(DVE, lambda: dve.tensor_tensor(out=Kt.ap, in0=kk.ap, in1=bcl.ap, op=OP.mult), [kk, bcl], [Kt])
            k.op(DVE, lambda: dve.tensor_tensor(out=Kh[hd].ap.rearrange("p (c k) -> p c k", k=32), in0=Kt.ap.rearrange("p (c k) -> p c k", k=32),
                                                in1=ebv[:, :, 31:32].to_broadcast([128, 8, 32]), op=OP.mult), [Kt, eb], [Kh[hd]])

        def scores(hd):
            for j in range(2):
                k.op(PE, lambda: pe.matmul(patt.ap, Kt.ap[:, j * 128:(j + 1) * 128], Qt[hd].ap[:, j * 128:(j + 1) * 128], start=True, stop=True), [Kt, Qt[hd]], [patt])
                att_j = attT2[hd][j]
                k.op(DVE, lambda: dve.tensor_tensor(out=att_j.ap, in0=patt.ap, in1=cf.ap[:, CF_MBD:CF_MBD + 128], op=OP.mult), [patt, cf], [att_j])

        for hd in range(4):
            chain(hd)
            if hd + 1 < 4:
                proj(hd + 1)
            scores(hd)
        for j in range(2):
            for hd in range(4):
                k.op(PE, lambda: pe.transpose(pkh.ap, Kh[hd].ap[:, j * 128:(j + 1) * 128], ident), [Kh[hd], cb], [pkh])
                k.op(ACT, lambda: act.activation(KhT[hd].ap, pkh.ap, AF.Copy), [pkh], [KhT[hd]])
                k.op(PE, lambda: pe.matmul(pO[hd].ap[:, j * 128:(j + 1) * 128], vsb[hd].ap[:, j, :], attT2[hd][j].ap, start=(j == 0 and hd % 2 == 0), stop=False), [vsb[hd], attT2[hd][j]], [pO[hd]], inc=False)
            for c in range(4):
                col = j * 128 + c * 32
                ci = j * 4 + c
                for hd in range(4):
                    k.op(PE, lambda: pe.matmul(pO[hd].ap[:, col:col + 32], Sbf[hd].ap, Qt[hd].ap[:, col:col + 32], start=False, stop=(c == 3 and j == 1 and hd % 2 == 1)), [Sbf[hd], Qt[hd]], [pO[hd]], inc=True)
                    k.op(PE, lambda: pe.matmul(pS[hd].ap, KhT[hd].ap[c * 32:(c + 1) * 32, :], vsb[hd].ap[c * 32:(c + 1) * 32, j, :], start=True, stop=True, tile_position=(c * 32, 0)),
                         [KhT[hd], vsb[hd]], [pS[hd]])
                    k.op(DVE, lambda: dve.scalar_tensor_tensor(out=S32[hd].ap, in0=S32[hd].ap, scalar=ebe[hd].ap[:, ci:ci + 1], in1=pS[hd].ap, op0=OP.mult, op1=OP.add), [S32[hd], ebe[hd], pS[hd]], [S32[hd]])
                    k.op(ACT, lambda: act.activation(Sbf[hd].ap, S32[hd].ap, AF.Copy), [S32[hd]], [Sbf[hd]])
        osbs = [sig, lf, kk, bb]
        rss = [bcl, eb, qs, xtra]
        osqs = Kh
        psss = [pt(h // 2, (h % 2) * TW, (h % 2 + 1) * TW) for h in range(4)]
        for hd in range(4):
            k.op(ACT, lambda: act.activation(osbs[hd].ap, pO[hd].ap, AF.Copy), [pO[hd]], [osbs[hd]])
            k.op(ACT, lambda: act.activation(osqs[hd].ap, pO[hd].ap, AF.Square), [pO[hd]], [osqs[hd]])
            k.op(PE, lambda: pe.matmul(psss[hd].ap, ones, osqs[hd].ap, start=True, stop=True), [cb, osqs[hd]], [psss[hd]])
        for hd in range(4):
            k.op(DVE, lambda: dve.tensor_scalar(out=rss[hd].ap, in0=psss[hd].ap, scalar1=128.0 * RMS_EPS, scalar2=None, op0=OP.add), [psss[hd]], [rss[hd]])
        for hd in range(4):
            k.op(ACT, lambda: act.activation(rss[hd].ap, rss[hd].ap, AF.Ln), [rss[hd]], [rss[hd]])
        for hd in range(4):
            k.op(ACT, lambda: act.activation(rss[hd].ap, rss[hd].ap, AF.Exp, scale=-0.5), [rss[hd]], [rss[hd]])
        for hd in range(4):
            k.op(DVE, lambda: dve.tensor_tensor(out=osbs[hd].ap, in0=osbs[hd].ap, in1=rss[hd].ap, op=OP.mult), [osbs[hd], rss[hd]], [osbs[hd]])
            k.op(DVE, lambda: dve.scalar_tensor_tensor(out=cat[hd].ap, in0=osbs[hd].ap, scalar=ngs.ap[:, e:e + 1], in1=sg[hd].ap, op0=OP.mult, op1=OP.mult), [osbs[hd], ngs, sg[hd]], [cat[hd]])

        a_reset()
        qr = [[a_tile("qr", [128, TW], BF16) for _ in range(2)] for _ in range(2)]
        b_mark = ar["off"]
        xs = [a_tile("xs", [128, TW], F32) for _ in range(2)]
        cosF = a_tile("cosF", [128, TW], F32)
        sinF = a_tile("sinF", [128, TW], F32)
        posf = a_tile("posf", [128, TW], F32)
        posi = a_tile("posi", [128, TW], I32)
        t1 = a_tile("t1", [128, TW], F32)
        t2 = a_tile("t2", [128, TW], F32)
        nint = a_tile("nint", [128, TW], I32)
        k.dma(SP, posi, posi.ap, pos_d[b:b + 1, tt * TW:(tt + 1) * TW].to_broadcast([128, TW]))
        k.op(DVE, lambda: dve.tensor_copy(posf.ap, posi.ap), [posi], [posf])

        def sin_table(dst, phcol, addcol):
            PI = float(np.pi)
            k.op(DVE, lambda: dve.tensor_scalar(out=t1.ap, in0=posf.ap, scalar1=cfc(CF_INVF2), scalar2=cfc(phcol), op0=OP.mult, op1=OP.add), [posf, cf], [t1])
            k.op(DVE, lambda: dve.tensor_copy(nint.ap, t1.ap), [t1], [nint])
            k.op(DVE, lambda: dve.tensor_copy(t1.ap, nint.ap), [nint], [t1])
            k.op(DVE, lambda: dve.tensor_scalar(out=t2.ap, in0=posf.ap, scalar1=cfc(CF_INVF), scalar2=cfc(addcol), op0=OP.mult, op1=OP.add), [posf, cf], [t2])
            k.op(DVE, lambda: dve.scalar_tensor_tensor(out=t2.ap, in0=t1.ap, scalar=-CW1, in1=t2.ap, op0=OP.mult, op1=OP.add), [t1, t2], [t2])
            k.op(DVE, lambda: dve.scalar_tensor_tensor(out=t2.ap, in0=t1.ap, scalar=-CW2, in1=t2.ap, op0=OP.mult, op1=OP.add), [t1, t2], [t2])
            k.op(DVE, lambda: dve.tensor_scalar(out=t1.ap, in0=t2.ap, scalar1=-PI, scalar2=2.0 * PI, op0=OP.is_lt, op1=OP.mult), [t2], [t1])
            k.op(DVE, lambda: dve.tensor_tensor(out=t2.ap, in0=t2.ap, in1=t1.ap, op=OP.add), [t1, t2], [t2])
            k.op(DVE, lambda: dve.tensor_scalar(out=t1.ap, in0=t2.ap, scalar1=PI, scalar2=-2.0 * PI, op0=OP.is_gt, op1=OP.mult), [t2], [t1])
            k.op(DVE, lambda: dve.tensor_tensor(out=t2.ap, in0=t2.ap, in1=t1.ap, op=OP.add), [t1, t2], [t2])
            k.op(DVE, lambda: dve.tensor_scalar(out=t2.ap, in0=t2.ap, scalar1=-PI, scalar2=PI, op0=OP.max, op1=OP.min), [t2], [t2])
            k.op(ACT, lambda: act.activation(dst.ap, t2.ap, AF.Sin), [t2], [dst])

        sin_table(sinF, CF_PHS, CF_ZERO)
        k.op(DVE, lambda: dve.tensor_scalar(out=sinF.ap, in0=sinF.ap, scalar1=cfc(CF_SGN), scalar2=None, op0=OP.mult), [sinF, cf], [sinF])
        sin_table(cosF, CF_PHC, CF_HPI)
        prot = pt(4, 0, TW)

        def rope(u, pacc, xsb, dst_t, dst_ap, scale):
            for kc in range(8):
                k.op(PE, lambda: pe.matmul(pacc.ap, w3(u)[:, kc, :], h1[kc].ap, start=(kc == 0), stop=(kc == 7)), [u, h1[kc]], [pacc], inc=(kc == 7))
            k.op(ACT, lambda: act.activation(xsb.ap, pacc.ap, AF.Identity, scale=scale), [pacc], [xsb])
            k.op(PE, lambda: pe.matmul(prot.ap, cf.ap[:, CF_PERM:CF_PERM + 128], xsb.ap, start=True, stop=True), [cf, xsb], [prot])
            k.op(DVE, lambda: dve.tensor_tensor(out=t1.ap, in0=xsb.ap, in1=cosF.ap, op=OP.mult), [xsb, cosF], [t1])
            k.op(DVE, lambda: dve.tensor_tensor(out=t2.ap, in0=prot.ap, in1=sinF.ap, op=OP.mult), [prot, sinF], [t2])
            k.op(DVE, lambda: dve.tensor_tensor(out=dst_ap, in0=t1.ap, in1=t2.ap, op=OP.add), [t1, t2], [dst_t])

        pacc = [pt(0, 0, TW), pt(2, 0, TW)]
        pvv = pt(1, 0, 128)
        pob = [pt(1, 256, 384), pt(1, 384, 512)]
        ps4s = [[pt(5, 0, 512), pt(6, 0, 512)], [pt(2, 0, 512), pt(3, 0, 512)]]
        ppt = pt(7, 0, 1024)
        for g in range(2):
            if tt == 0:
                k.op(DVE, lambda: dve.memset(kTc[g].ap[:, 0:128], 0.0), [], [kTc[g]])
                k.op(DVE, lambda: dve.memset(vpad[g].ap, 0.0), [], [vpad[g]])
            else:
                k.op(DVE, lambda: dve.tensor_copy(kTc[g].ap[:, 0:128], kTc[g].ap[:, TW:TW + 128]), [kTc[g]], [kTc[g]])
                k.op(DVE, lambda: dve.tensor_copy(vpad[g].ap[:, 0, 64:128], vpad[g].ap[:, 2, 64:128]), [vpad[g]], [vpad[g]])
            rope(uBk[g], pacc[0], xs[0], kTc[g], kTc[g].ap[:, 128:128 + TW], 1.0)
            for c in range(2):
                rope(uBq[g * 2 + c], pacc[(c + 1) % 2], xs[(c + 1) % 2], qr[g][c], qr[g][c].ap, 0.125)
            for j in range(2):
                for kc in range(8):
                    k.op(PE, lambda: pe.matmul(pvv.ap[:, j * 64:(j + 1) * 64], h1[kc].ap[:, j * 128:(j + 1) * 128], w3(uBv)[:, kc, g * 64:(g + 1) * 64], start=(kc == 0), stop=(kc == 7)),
                         [uBv, h1[kc]], [pvv], inc=(kc == 7))
            k.op(ACT, lambda: act.activation(vpad[g].ap[:, 1:3, 64:128], pvv.ap.rearrange("p (j d) -> p j d", d=64), AF.Copy), [pvv], [vpad[g]])
        ar["off"] = b_mark
        pexs = [a_tile("pex", [128, 4, 256], F32) for _ in range(2)]
        pns = [a_tile("pn", [128, 4, 256], BF16) for _ in range(2)]
        pT = a_tile("pT", [128, 8, 128], BF16)
        smalls = [[a_tile("sml", [128, 4], F32) for _ in range(4)] for _ in range(2)]
        blks = [(g, j) for g in range(2) for j in range(2)]

        def stageA(bi):
            g, j = blks[bi]
            ps4 = ps4s[bi % 2]
            pex = pexs[bi % 2]
            mx, ngm, sm, es = smalls[bi % 2]
            gblk = tt * 2 + j
            msk = cb.ap[:, CB_MS0:CB_MS0 + 256] if gblk == 0 else cb.ap[:, CB_MS:CB_MS + 256]
            for hl in range(4):
                c, rbp = hl // 2, (hl % 2) * 64
                so = ps4[hl // 2].ap[:, (hl % 2) * 256:(hl % 2 + 1) * 256]
                k.op(PE, lambda: pe.matmul(so, qr[g][c].ap[rbp:rbp + 64, j * 128:(j + 1) * 128], kTc[g].ap[rbp:rbp + 64, j * 128:j * 128 + 256], start=True, stop=False),
                     [qr[g][c], kTc[g]], [ps4[hl // 2]], inc=False)
                k.op(PE, lambda: pe.matmul(so, ident, msk, start=False, stop=True), [cb], [ps4[hl // 2]], inc=True)
            for pr in range(2):
                k.op(DVE, lambda: dve.tensor_reduce(out=mx.ap[:, pr * 2:pr * 2 + 2], in_=ps4[pr].ap.rearrange("p (h k) -> p h k", k=256), axis=AX.X, op=OP.max), [ps4[pr]], [mx])
            snk = par.ap[:, P_SINK + e * 8 + g * 4:P_SINK + e * 8 + g * 4 + 4]
            k.op(DVE, lambda: dve.tensor_tensor(out=mx.ap, in0=mx.ap, in1=snk, op=OP.max), [mx, par], [mx])
            k.op(DVE, lambda: dve.tensor_scalar(out=ngm.ap, in0=mx.ap, scalar1=-1.0, scalar2=None, op0=OP.mult), [mx], [ngm])
            k.op(DVE, lambda: dve.memset(sm.ap, 0.0), [], [sm])
            for hl in range(4):
                so = ps4[hl // 2].ap[:, (hl % 2) * 256:(hl % 2 + 1) * 256]
                k.op(ACT, lambda: act.activation(pex.ap[:, hl, :], so, AF.Exp, bias=ngm.ap[:, hl:hl + 1], accum_out=sm.ap[:, hl:hl + 1]), [ps4[hl // 2], ngm], [pex, sm])
            k.op(DVE, lambda: dve.tensor_tensor(out=es.ap, in0=snk, in1=ngm.ap, op=OP.add), [par, ngm], [es])
            k.op(ACT, lambda: act.activation(es.ap, es.ap, AF.Exp), [es], [es])

        def stageB(bi):
            g, j = blks[bi]
            pex, pn = pexs[bi % 2], pns[bi % 2]
            mx, ngm, sm, es = smalls[bi % 2]
            k.op(DVE, lambda: dve.tensor_tensor(out=es.ap, in0=es.ap, in1=sm.ap, op=OP.add), [es, sm], [es])
            k.op(DVE, lambda: dve.reciprocal(es.ap, es.ap), [es], [es])
            k.op(DVE, lambda: dve.tensor_tensor(out=pn.ap, in0=pex.ap, in1=es.ap.rearrange("p (h o) -> p h o", o=1).to_broadcast([128, 4, 256]), op=OP.mult), [pex, es], [pn])
            for hl in range(4):
                for hf in range(2):
                    i8 = hl * 2 + hf
                    k.op(PE, lambda: pe.transpose(ppt.ap[:, i8 * 128:(i8 + 1) * 128], pn.ap[:, hl, hf * 128:(hf + 1) * 128], ident), [pn, cb], [ppt], inc=(i8 == 7))
            k.op(ACT, lambda: act.activation(pT.ap.rearrange("p a q -> p (a q)"), ppt.ap, AF.Copy), [ppt], [pT])
            for pr in range(2):
                n4 = 0
                for hl in (2 * pr, 2 * pr + 1):
                    off = 64 if hl % 2 == 0 else 0
                    for hf in range(2):
                        k.op(PE, lambda: pe.matmul(pob[pr].ap, vpad[g].ap[:, j + hf, off:off + 128], pT.ap[:, hl * 2 + hf, :], start=(n4 == 0), stop=(n4 == 3)),
                             [vpad[g], pT], [pob[pr]], inc=(n4 == 3))
                        n4 += 1
                ct = cat[4 + g * 2 + pr]
                k.op(ACT, lambda: act.activation(ct.ap[:, j * 128:(j + 1) * 128], pob[pr].ap, AF.Copy), [pob[pr]], [ct])

        stageA(0)
        for bi in range(4):
            if bi + 1 < 4:
                stageA(bi + 1)
            stageB(bi)
        py = [pt(2, 0, TW), pt(3, 0, TW)]
        for n in range(8):
            po = py[n % 2]
            for kc in range(8):
                k.op(PE, lambda: pe.matmul(po.ap, w3(uWo[n])[:, kc, :], cat[kc].ap, start=(kc == 0), stop=(kc == 7)), [uWo[n], cat[kc]], [po], inc=(kc == 7))
            x_t = xa[n][tt]
            k.op(DVE, lambda: dve.scalar_tensor_tensor(out=x_t.ap, in0=po.ap, scalar=cc(CO_G1, n), in1=x_t.ap, op0=OP.mult, op1=OP.add), [po, coef, x_t], [x_t])

    attT2 = [[k.tile("attT2_%d_%d" % (h, j), [128, 128], BF16) for j in range(2)] for h in range(4)]
    assert k.sb_off - 16384 <= 210784, k.sb_off - 16384
    print("sbuf bytes used", k.sb_off - 16384, "of 212863")

    def odd_mixer_tile(o, tt, units, b):
        a_reset()
        hf = [a_tile("hf", [128, TW], F32) for _ in range(2)]
        Pb = [a_tile("Pb", [128, 16 + TW], F32) for _ in range(2)]
        wsm = [a_tile("wsm", [128, TW], F32) for _ in range(2)]
        pl = [a_tile("pl", [128, TW], F32) for _ in range(2)]
        fx = a_tile("fx", [128, 16], F32)
        onesb = cf.ap[:, CF_CMASK + 1:CF_CMASK + 2].to_broadcast([128, TW])
        def st1(kc):
            hh = hf[kc % 2]
            k.op(ACT, lambda: act.activation(hh.ap, xa[kc][tt].ap, AF.Identity, scale=cc(CO_S1, kc), bias=modT.ap[:, lcur["l"], kc, b:b + 1]),
                 [xa[kc][tt], coef, modT], [hh])

        def st2(kc):
            gi = kc // 2
            w = (2, 4, 8, 16)[gi]
            hh, P, wv, pp = hf[kc % 2], Pb[kc % 2], wsm[kc % 2], pl[kc % 2]
            if tt == 0:
                k.op(DVE, lambda: dve.memset(P.ap[:, 0:16], 0.0), [], [P])
            else:
                k.op(DVE, lambda: dve.tensor_copy(P.ap[:, 0:16], phalo.ap[:, kc, :]), [phalo], [P])
            init = 0.0 if tt == 0 else phalo.ap[:, kc, 15:16]
            k.op(DVE, lambda: dve.tensor_tensor_scan(P.ap[:, 16:16 + TW], onesb, hh.ap, init, OP.mult, OP.add), [cf, hh, phalo], [P])
            k.op(DVE, lambda: dve.tensor_copy(phalo.ap[:, kc, :], P.ap[:, TW:TW + 16]), [P], [phalo])
            k.op(POOL, lambda: pool.tensor_tensor(out=wv.ap, in0=P.ap[:, 16:16 + TW], in1=P.ap[:, 16 - w:16 - w + TW], op=OP.subtract), [P], [wv])
            k.op(DVE, lambda: dve.scalar_tensor_tensor(out=pp.ap, in0=wv.ap, scalar=1.0 / w, in1=hh.ap, op0=OP.mult, op1=OP.subtract), [wv, hh], [pp])
            if tt == 0:
                k.op(DVE, lambda: dve.tensor_tensor(out=fx.ap, in0=wv.ap[:, 0:16], in1=cf.ap[:, CF_PINV + gi * 16:CF_PINV + (gi + 1) * 16], op=OP.mult), [wv, cf], [fx])
                k.op(DVE, lambda: dve.tensor_tensor(out=pp.ap[:, 0:16], in0=fx.ap, in1=hh.ap[:, 0:16], op=OP.subtract), [fx, hh], [pp])

        def st3(kc):
            pp = pl[kc % 2]
            k.op(ACT, lambda: act.activation(h1[kc].ap, pp.ap, AF.Copy), [pp], [h1[kc]])

        st1(0)
        for kc in range(8):
            st2(kc)
            if kc + 1 < 8:
                st1(kc + 1)
            st3(kc)
        py = [pt(2, 0, TW), pt(3, 0, TW)]
        for n in range(8):
            gi, mm_ = n // 2, n % 2
            u = units[gi // 2]
            gl = gi % 2
            po = py[n % 2]
            for kc in range(2):
                k.op(PE, lambda: pe.matmul(po.ap, w3(u)[:, (gl * 2 + mm_) * 2 + kc, :], h1[gi * 2 + kc].ap, start=(kc == 0), stop=(kc == 1)), [u, h1[gi * 2 + kc]], [po], inc=(kc == 1))
            x_t = xa[n][tt]
            k.op(DVE, lambda: dve.scalar_tensor_tensor(out=x_t.ap, in0=po.ap, scalar=cc(CO_G1, n), in1=x_t.ap, op0=OP.mult, op1=OP.add), [po, coef, x_t], [x_t])

    def ffn_phase(l, b, last):
        for gidx, grp in enumerate(FF_GROUPS):
            ng = len(grp)
            units = ws_get(3 * ng, ["upu", "upv", "down"] * ng)
            lastg = gidx == len(FF_GROUPS) - 1
            a_reset()
            if lastg:
                ar["off"] = arena0 + 9216
            DP = 2 if lastg else 3
            ub = [[a_tile("ub", [128, 2 + TW], F32) for _ in range(2)] for _ in range(ng)]
            acc = [a_tile("acc", [128, TW], F32) for _ in range(DP)]
            gt = [a_tile("gt", [128, TW], BF16) for _ in range(DP)]
            puv = [(pt(4 + d, 0, TW), pt(4 + d, TW, 2 * TW)) for d in range(DP)]
            py = [pt(m // 2, (m % 2) * TW, (m % 2 + 1) * TW) for m in range(8)]
            items = [(tt, jl) for tt in range(NT) for jl in range(ng)]

            def up(i):
                tt, jl = items[i]
                uu, uv = units[3 * jl], units[3 * jl + 1]
                pu, pvv = puv[i % DP]
                for (u, po) in ((uu, pu), (uv, pvv)):
                    for kc in range(8):
                        k.op(PE, lambda: pe.matmul(po.ap, w3(u)[:, kc, :], Hh[kc][tt].ap, start=(kc == 0), stop=(kc == 7)), [u, Hh[kc][tt]], [po], inc=(kc == 7))

            def mid(i, stage):
                tt, jl = items[i]
                j = grp[jl]
                pu, pvv = puv[i % DP]
                u_c = ub[jl][tt % 2]
                u_n = ub[jl][(tt + 1) % 2]
                a_ = acc[i % DP]
                cw = lambda t3: par.ap[:, P_CW + (l * 3 + t3) * 22 + j:P_CW + (l * 3 + t3) * 22 + j + 1]
                cbias = par.ap[:, P_CB + l * 22 + j:P_CB + l * 22 + j + 1]
                if stage == 1:
                    if tt == 0:
                        k.op(DVE, lambda: dve.memset(u_c.ap[:, 0:2], 0.0), [], [u_c])
                    k.op(ACT, lambda: act.activation(u_c.ap[:, 2:2 + TW], pu.ap, AF.Copy), [pu], [u_c])
                    k.op(ACT, lambda: act.activation(a_.ap, pu.ap, AF.Identity, scale=cw(2), bias=cbias), [pu, par], [a_])
                elif stage == 2:
                    k.op(DVE, lambda: dve.scalar_tensor_tensor(out=a_.ap, in0=u_c.ap[:, 0:TW], scalar=cw(0), in1=a_.ap, op0=OP.mult, op1=OP.add), [u_c, par, a_], [a_])
                    k.op(DVE, lambda: dve.scalar_tensor_tensor(out=a_.ap, in0=u_c.ap[:, 1:1 + TW], scalar=cw(1), in1=a_.ap, op0=OP.mult, op1=OP.add), [u_c, par, a_], [a_])
                else:
                    k.op(ACT, lambda: act.activation(u_n.ap[:, 0:2], u_c.ap[:, TW:TW + 2], AF.Copy), [u_c], [u_n])

            def midB(i):
                pu, pvv = puv[i % DP]
                a_ = acc[i % DP]
                k.op(ACT, lambda: act.activation(a_.ap, a_.ap, AF.Silu), [a_], [a_])
                k.op(DVE, lambda: dve.tensor_tensor(out=gt[i % DP].ap, in0=a_.ap, in1=pvv.ap, op=OP.mult), [a_, pvv], [gt[i % DP]])

            def down(i):
                tt, jl = items[i]
                ud = units[3 * jl + 2]
                for m in range(8):
                    k.op(PE, lambda: pe.matmul(py[m].ap, ud.ap[:, m * 128:(m + 1) * 128], gt[i % DP].ap, start=(jl == 0 and m % 2 == 0), stop=(jl == ng - 1 and m % 2 == 1)), [ud, gt[i % DP]], [py[m]], inc=(m == 7 or (jl == ng - 1 and m % 2 == 1)))
                if jl == ng - 1:
                    for m in range(8):
                        x_t = xa[m][tt]
                        g2c = modT.ap[:, l, 40 + m, b:b + 1]
                        if m < 4 and OPT_EVAC:
                            tm = ytmp[m]
                            k.op(ACT, lambda: act.activation(tm.ap, py[m].ap, AF.Identity, scale=g2c, bias=cfc(CF_ZERO)), [py[m], modT, cf], [tm])
                            pend.append((x_t, tm))
                        else:
                            k.op(DVE, lambda: dve.scalar_tensor_tensor(out=x_t.ap, in0=py[m].ap, scalar=g2c, in1=x_t.ap, op0=OP.mult, op1=OP.add),
                                 [py[m], modT, x_t], [x_t])
                    if lastg:
                        flush_pend()
                    if lastg:
                        save = ar["off"]
                        layer_norm(tt, CO_C1, CO_C0, hb=None, out_dma=(outb["b"] if last else None))
                        ar["off"] = save

            ytmp = [a_tile("ytmp", [128, TW], F32) for _ in range(4)]
            pend = []

            def flush_pend():
                for (x_t, tm) in pend:
                    if OPT_POOLADD:
                        k.op(POOL, lambda: pool.tensor_tensor(out=x_t.ap, in0=x_t.ap, in1=tm.ap, op=OP.add), [x_t, tm], [x_t])
                    else:
                        k.op(DVE, lambda: dve.tensor_tensor(out=x_t.ap, in0=x_t.ap, in1=tm.ap, op=OP.add), [x_t, tm], [x_t])
                del pend[:]

            n_it = len(items)
            for d in range(DP - 1):
                if d < n_it:
                    up(d)
            HOIST = DP >= 3
            if HOIST:
                mid(0, 1)
            LAG = 2 if HOIST else 1
            for i in range(n_it + LAG):
                if i < n_it:
                    if not HOIST:
                        mid(i, 1)
                    mid(i, 2)
                    if HOIST and i + 1 < n_it:
                        mid(i + 1, 1)
                    mid(i, 3)
                    midB(i)
                flush_pend()
                if i + DP - 1 < n_it:
                    up(i + DP - 1)
                if i >= LAG:
                    down(i - LAG)
            flush_pend()
            ws_done(3 * ng)
            if b == 0 and l + 1 < n_layers:
                compute_mod(l + 1, gidx * 8, gidx * 8 + 8)

    outb = {"b": 0}
    lcur = {"l": 0}

    for s in range(n_seq):
        outb["b"] = s
        for tt in range(NT):
            k.dma(SP, [xa[kc][tt] for kc in range(8)], xa_h[:, :, tt * TW:(tt + 1) * TW], x_d[s, :, :, tt * TW:(tt + 1) * TW], reads=[], semt=xa[0][tt])
        for tt in range(NT):
            for kc in range(8):
                k.op(ACT if kc % 2 else DVE,
                     (lambda kc=kc, tt=tt: act.activation(xa[kc][tt].ap, xa[kc][tt].ap, AF.Identity, scale=ALPHA)) if kc % 2 else
                     (lambda kc=kc, tt=tt: dve.tensor_scalar(out=xa[kc][tt].ap, in0=xa[kc][tt].ap, scalar1=ALPHA, scalar2=None, op0=OP.mult)),
                     [xa[kc][tt]], [xa[kc][tt]])
        for l in range(n_layers):
            lcur["l"] = l
            last = l == n_layers - 1
            compute_coefs(l, s, last)
            if l % 2 == 0:
                e = l // 2
                for hd in range(4):
                    k.op(DVE, lambda: dve.memset(S32[hd].ap, 0.0), [], [S32[hd]])
                    k.op(DVE, lambda: dve.memset(Sbf[hd].ap, 0.0), [], [Sbf[hd]])
                units = ws_get(31)
                even_prologue(0, units, s)
                for tt in range(NT):
                    even_mixer_tile(e, tt, units, s)
                    if tt + 1 < NT:
                        even_prologue(tt + 1, units, s)
                    layer_norm(tt, CO_A1, CO_A0, hb=(CO_B1, CO_B0))
                ws_done(31)
            else:
                o = l // 2
                units = ws_get(2, ["pool", "pool"])
                for tt in range(NT):
                    odd_mixer_tile(o, tt, units, s)
                    layer_norm(tt, CO_A1, CO_A0, hb=(CO_B1, CO_B0))
                ws_done(2)
            ffn_phase(l, s, last)
    for (sem, val) in out_waits:
        nc.sync.wait_ge(sem, val)
    assert ws["cur"] == U, (ws["cur"], U)
    return nc, plan


_CACHE = {}


def kernel(**inputs):
    inp = {kk: np.asarray(v) for kk, v in inputs.items()}
    if "prog" not in _CACHE:
        _CACHE["prog"] = build_program()
    nc, plan = _CACHE["prog"]
    wstream = materialize_stream(plan, inp)
    cf, cb = make_consts()
    x = inp["x"].astype(np.float32)
    pos = inp["positions"].astype(np.int32)
    in_maps = []
    for c in range(NCORES):
        xc = x[2 * c:2 * c + 2]
        xfm = np.ascontiguousarray(xc.transpose(0, 2, 1).reshape(2, 8, 128, T).transpose(0, 2, 1, 3))
        in_maps.append({"xfm": xfm, "pos": np.ascontiguousarray(pos[2 * c:2 * c + 2]), "cf": cf, "cb": cb,
                        "par": make_params(inp, c), "wstream": wstream})
    res = run_bass_kernel_spmd(nc, in_maps, core_ids=list(range(NCORES)))
    out = np.zeros((16, T, D), np.float32)
    for c in range(NCORES):
        o = np.asarray(res.results[c]["ofm"])
        out[2 * c:2 * c + 2] = o.transpose(0, 2, 1, 3).reshape(2, D, T).transpose(0, 2, 1)
    return out
```

```python
import numpy as np
import concourse.bass as bass
import concourse.mybir as mybir
from concourse.bass_utils import run_bass_kernel_spmd

F32 = mybir.dt.float32
BF16 = mybir.dt.bfloat16
I32 = mybir.dt.int32
AF = mybir.ActivationFunctionType
OP = mybir.AluOpType
AX = mybir.AxisListType

D = 1024
T = 2048
DEPTH = 4
TW = 256
NT = T // TW
NFF = 22
ALPHA = (2.0 * DEPTH) ** 0.25
LN_EPS = 1e-5
RMS_EPS = 1e-6
R = 31
FF_GROUPS = [list(range(0, 4)), list(range(4, 8)), list(range(8, 12)), list(range(12, 16)),
             list(range(16, 20)), [20, 21]]
NCORES = 8
import os
OPT_LNPOOL = int(os.environ.get('K_LNPOOL', '1'))
OPT_EVAC = int(os.environ.get('K_EVAC', '1'))
OPT_POOLADD = int(os.environ.get('K_POOLADD', '1'))

CF_PERM, CF_MBD, CF_CMASK, CF_INVF, CF_SGN, CF_PI, CF_HPI, CF_2PI, CF_PINV, NCF = 0, 128, 256, 512, 513, 514, 515, 516, 520, 585
CF_INVF2, CF_PHS, CF_PHC, CF_ZERO = 517, 518, 519, 584
CW1 = 6.28125
CW2 = float(2.0 * np.pi - 6.28125)
CB_ID, CB_ONES, CB_MS, CB_MS0, NCB = 0, 128, 256, 512, 768
P_C, P_ADAB, P_LNG, P_LNB, P_LBR, P_NG, P_SINK, P_PSC, P_CW, P_CB = 0, 16, 208, 272, 336, 344, 346, 362, 378, 642
NPAR = 730


def stream_plan(n_layers=DEPTH, n_seq=2):
    plan = [("ada", 0, n) for n in range(48)]
    for s in range(n_seq):
        for l in range(n_layers):
            if l % 2 == 0:
                e = l // 2
                for h in range(4):
                    for base in (0, 512, 1024, 1536):
                        plan.append(("win", e, base + h * 128))
                for c in range(4):
                    plan.append(("win", e, 2048 + c * 128))
                for g in range(2):
                    plan.append(("bkdup", e, g))
                plan.append(("bv", e, 0))
                for n in range(8):
                    plan.append(("wout", e, n))
            else:
                o = l // 2
                plan.append(("pool", o, 0))
                plan.append(("pool", o, 1))
            for gidx, grp in enumerate(FF_GROUPS):
                for j in grp:
                    plan.append(("upu", l, j))
                    plan.append(("upv", l, j))
                    plan.append(("down", l, j))
                if s == 0 and l + 1 < n_layers:
                    plan += [("ada", l + 1, n) for n in range(gidx * 8, gidx * 8 + 8)]
    return plan


def _k1024(w):
    return np.ascontiguousarray(w.reshape(8, 128, 128).transpose(1, 0, 2)).reshape(128, 1024)


def materialize_stream(plan, inp):
    out = np.zeros((len(plan), 128, 1024), np.float32)
    cache = {}
    for i, u in enumerate(plan):
        if u in cache:
            out[i] = out[cache[u]]
            continue
        cache[u] = i
        kind, a, b = u
        if kind == "ada":
            out[i] = _k1024(inp["ada_w"][a][:, b * 128:(b + 1) * 128])
        elif kind == "win":
            out[i] = _k1024(inp["ab_w_in"][a][:, b:b + 128])
        elif kind == "bkdup":
            w = inp["ab_w_in"][a][:, 2560 + b * 64:2560 + (b + 1) * 64]
            out[i] = _k1024(np.concatenate([w, w], axis=1))
        elif kind == "bv":
            out[i] = _k1024(inp["ab_w_in"][a][:, 2688:2816])
        elif kind == "wout":
            out[i] = _k1024(inp["ab_w_out"][a][:, b * 128:(b + 1) * 128])
        elif kind == "pool":
            blk = np.zeros((128, 8, 128), np.float32)
            for gl in range(2):
                for m in range(2):
                    for kc in range(2):
                        blk[:, (gl * 2 + m) * 2 + kc, :] = inp["pool_w"][a][b * 2 + gl][kc * 128:(kc + 1) * 128, m * 128:(m + 1) * 128]
            out[i] = blk.reshape(128, 1024)
        elif kind == "upu":
            out[i] = _k1024(inp["ffn_w_up"][a][:, b * 128:(b + 1) * 128])
        elif kind == "upv":
            out[i] = _k1024(inp["ffn_w_up"][a][:, 2816 + b * 128:2816 + (b + 1) * 128])
        elif kind == "down":
            out[i] = inp["ffn_w_down"][a][b * 128:(b + 1) * 128, :]
    return out


def make_consts():
    cf = np.zeros((128, NCF), np.float32)
    cb = np.zeros((128, NCB), np.float32)
    p = np.arange(128)
    d = p % 64
    perm = np.zeros((128, 128), np.float32)
    for m in range(128):
        dm = m % 64
        if dm < 8:
            perm[m + 8, m] = 1.0
        elif dm < 16:
            perm[m - 8, m] = 1.0
    cf[:, CF_PERM:CF_PERM + 128] = perm
    s = p[:, None]
    t = p[None, :]
    cf[:, CF_MBD:CF_MBD + 128] = ((s // 32 == t // 32) & (s <= t)).astype(np.float32)
    cm = np.ones(256, np.float32)
    cm[::32] = 0.0
    cf[:, CF_CMASK:CF_CMASK + 256] = cm[None, :]
    invf = 500000.0 ** (-np.arange(0, 16, 2, dtype=np.float32) / 16.0)
    iv = np.zeros(128, np.float32)
    sg = np.zeros(128, np.float32)
    for r in range(128):
        dd = r % 64
        if dd < 8:
            iv[r] = invf[dd]
            sg[r] = -1.0
        elif dd < 16:
            iv[r] = invf[dd - 8]
            sg[r] = 1.0
    cf[:, CF_INVF] = iv
    cf[:, CF_SGN] = sg
    cf[:, CF_PI] = np.float32(np.pi)
    cf[:, CF_HPI] = np.float32(np.pi / 2)
    cf[:, CF_2PI] = np.float32(2 * np.pi)
    cf[:, CF_INVF2] = (iv.astype(np.float64) / (2 * np.pi)).astype(np.float32)
    cf[:, CF_PHS] = 0.5
    cf[:, CF_PHC] = 0.75
    for gi, w in enumerate((2, 4, 8, 16)):
        cnt = np.minimum(np.arange(16) + 1, w).astype(np.float32)
        cf[:, CF_PINV + gi * 16:CF_PINV + (gi + 1) * 16] = (1.0 / cnt)[None, :]
    cb[:, CB_ID:CB_ID + 128] = np.eye(128, dtype=np.float32)
    cb[:, CB_ONES:CB_ONES + 128] = 1.0
    q = p[:, None]
    ki = np.arange(256)[None, :]
    rel = q + 128 - ki
    allowed = (rel >= 0) & (rel < 128)
    cb[:, CB_MS:CB_MS + 256] = np.where(allowed, 0.0, -30000.0)
    cb[:, CB_MS0:CB_MS0 + 256] = np.where(allowed & (ki >= 128), 0.0, -30000.0)
    return cf, cb


def make_params(inp, core):
    pr = np.zeros((128, NPAR), np.float32)

    def fm(v):
        v = np.asarray(v)
        n = v.shape[-1] // 128
        v = v.reshape(v.shape[:-1] + (n, 128))
        return np.moveaxis(v, -1, 0)

    pr[:, P_C:P_C + 16] = fm(inp["c"][2 * core:2 * core + 2]).transpose(0, 2, 1).reshape(128, 16)
    pr[:, P_ADAB:P_ADAB + 192] = fm(inp["ada_b"]).reshape(128, 192)
    pr[:, P_LNG:P_LNG + 64] = fm(inp["ln_g"]).reshape(128, 64)
    pr[:, P_LNB:P_LNB + 64] = fm(inp["ln_b"]).reshape(128, 64)
    pr[:, P_LBR:P_LBR + 8] = fm(inp["hgrn_lb_raw"]).reshape(128, 8)
    pr[:, P_NG:P_NG + 2] = fm(inp["hgrn_norm_g"]).reshape(128, 2)
    pr[:, P_SINK:P_SINK + 16] = np.asarray(inp["attn_sinks"]).reshape(1, 16)
    pr[:, P_PSC:P_PSC + 16] = fm(inp["pool_scale"]).reshape(128, 16)
    pr[:, P_CW:P_CW + 264] = fm(inp["ffn_conv_w"]).reshape(128, 264)
    pr[:, P_CB:P_CB + 88] = fm(inp["ffn_conv_b"]).reshape(128, 88)
    return pr


class Tk:
    __slots__ = ("ap", "buf", "lo", "hi", "w", "r", "sem", "cnt", "ov", "ovv")

    def __init__(self, ap, buf, lo=0, hi=1):
        self.ap, self.buf, self.lo, self.hi = ap, buf, lo, hi
        self.w = None
        self.r = {}
        self.sem = None
        self.cnt = 0
        self.ov = None
        self.ovv = -1


class Eng:
    def __init__(self, k, eng, name, skip_own=False):
        self.k, self.eng, self.name, self.skip_own = k, eng, name, skip_own
        self.seen = {}
        self.pend = []
        self.sems = set()
        self.newsem()

    def newsem(self):
        self.sem = self.k.nc.alloc_semaphore(name="%s_s%d" % (self.name, self.k.nsem))
        self.k.nsem += 1
        self.cnt = 0
        self.sems.add(self.sem.num)


class Prog:
    def __init__(self, nc):
        self.nc = nc
        self.nsem = 0
        self.bufs = {}
        self.nbuf = 0
        self.sb_off = 16384
        self.PE = Eng(self, nc.tensor, "pe", skip_own=True)
        self.ACT = Eng(self, nc.scalar, "act")
        self.DVE = Eng(self, nc.vector, "dve")
        self.POOL = Eng(self, nc.gpsimd, "pool")
        self.SP = Eng(self, nc.sync, "sp")

    def sb(self, name, shape, dt, off=None):
        esz = 4 if dt in (F32, I32) else 2
        nbytes = int(np.prod(shape[1:])) * esz
        if off is None:
            off = self.sb_off
            self.sb_off += (nbytes + 31) // 32 * 32
        h = self.nc.alloc_sbuf_tensor_at(name, list(shape), dt, offset=off)
        return h, off, nbytes

    def tile(self, name, shape, dt):
        h, off, nb = self.sb(name, shape, dt)
        return self.reg(Tk(h[:], "sb_" + name, off, off + nb))

    def tile_at(self, name, shape, dt, off):
        h, off, nb = self.sb(name, shape, dt, off)
        return self.reg(Tk(h[:], "arena", off, off + nb))

    def reg(self, t):
        self.bufs.setdefault(t.buf, []).append(t)
        self.ver = getattr(self, "ver", 0) + 1
        return t

    def sub(self, parent_ap, buf, lo, hi):
        return self.reg(Tk(parent_ap, buf, lo, hi))

    def over(self, t):
        if t.ovv != self.ver:
            t.ov = [t2 for t2 in self.bufs[t.buf] if t2.lo < t.hi and t.lo < t2.hi]
            t.ovv = self.ver
        return t.ov

    def _need(self, reads, writes):
        need = {}

        def add(tok):
            if tok is None:
                return
            sem, val = tok
            if need.get(sem.num, (None, 0))[1] < val:
                need[sem.num] = (sem, val)
        for t in reads:
            psum = t.buf.startswith("pb")
            for t2 in self.over(t):
                add(t2.w)
                if psum:
                    for tok in t2.r.values():
                        add(tok)
        for t in writes:
            for t2 in self.over(t):
                add(t2.w)
                for tok in t2.r.values():
                    add(tok)
        return need

    def _wait(self, e, need):
        for key, (sem, val) in need.items():
            if e.skip_own and key in e.sems:
                continue
            if e.seen.get(key, 0) < val:
                e.eng.wait_ge(sem, val)
                e.seen[key] = val

    def _commit(self, tok, reads, writes):
        for t in writes:
            t.w = tok
            t.r = {}
        for t in reads:
            t.r[tok[0].num] = tok

    def op(self, e, fn, reads=(), writes=(), inc=True):
        self._wait(e, self._need(reads, writes))
        ins = fn()
        if not inc:
            e.pend.append((reads, writes))
            return
        if e.cnt >= 40000:
            e.newsem()
        e.cnt += 1
        ins.then_inc(e.sem, 1)
        tok = (e.sem, e.cnt)
        for (rs, ws) in e.pend:
            self._commit(tok, rs, ws)
        e.pend = []
        self._commit(tok, reads, writes)

    def dma(self, e, out_t, out_ap, in_ap, reads=(), semt=None):
        st = semt if semt is not None else (out_t[0] if isinstance(out_t, (list, tuple)) else out_t)
        if st.sem is None:
            st.sem = self.nc.alloc_semaphore(name="d%d" % self.nsem)
            self.nsem += 1
        writes = list(out_t) if isinstance(out_t, (list, tuple)) else ([out_t] if out_t is not None else [])
        if semt is None:
            st = writes[0]
            if st.sem is None:
                st.sem = self.nc.alloc_semaphore(name="d%d" % self.nsem)
                self.nsem += 1
        self._wait(e, self._need(reads, writes))
        e.eng.dma_start(out=out_ap, in_=in_ap).then_inc(st.sem, 16)
        st.cnt += 16
        self._commit((st.sem, st.cnt), reads, writes)


def build_program(n_layers=DEPTH, n_seq=2, nunits=None):
    nc = bass.Bass("TRN2", target_bir_lowering=False)
    plan = stream_plan(n_layers, n_seq)
    U = len(plan)
    x_d = nc.dram_tensor("xfm", [2, 128, 8, T], F32, kind="ExternalInput").ap()
    pos_d = nc.dram_tensor("pos", [2, T], I32, kind="ExternalInput").ap()
    cf_d = nc.dram_tensor("cf", [128, NCF], F32, kind="ExternalInput").ap()
    cb_d = nc.dram_tensor("cb", [128, NCB], F32, kind="ExternalInput").ap()
    par_d = nc.dram_tensor("par", [128, NPAR], F32, kind="ExternalInput").ap()
    ws_d = nc.dram_tensor("wstream", [U, 128, 1024], F32, kind="ExternalInput").ap()
    out_d = nc.dram_tensor("ofm", [2, 128, 8, T], F32, kind="ExternalOutput").ap()

    k = Prog(nc)
    PE, ACT, DVE, POOL, SP = k.PE, k.ACT, k.DVE, k.POOL, k.SP
    pe, act, dve, pool = nc.tensor, nc.scalar, nc.vector, nc.gpsimd
    LNB = {"b": 6}

    xa_h, xa_off, _ = k.sb("xa", [128, 8, T], F32)
    H_h, H_off, _ = k.sb("H", [128, 8, T], BF16)
    xa = [[k.reg(Tk(xa_h[:, kc, tt * TW:(tt + 1) * TW], "xa%d_%d" % (kc, tt))) for tt in range(NT)] for kc in range(8)]
    Hh = [[k.reg(Tk(H_h[:, kc, tt * TW:(tt + 1) * TW], "H%d_%d" % (kc, tt))) for tt in range(NT)] for kc in range(8)]
    ring = [k.tile("ring%d" % i, [128, 1024], BF16) for i in range(R)]
    cf = k.tile("cf", [128, NCF], F32)
    cb = k.tile("cb", [128, NCB], BF16)
    par = k.tile("par", [128, NPAR], F32)
    modT = k.tile("modT", [128, DEPTH, 48, 2], F32)
    coef = k.tile("coef", [128, 8, 8], F32)
    ctmp = k.tile("ctmp", [128, 8], F32)
    cact = k.tile("cact", [128, 8, 2], BF16)
    lbt = k.tile("lbt", [128, 2, 2, 4], F32)
    ngs = k.tile("ngs", [128, 2], F32)
    h1 = [k.tile("h1_%d" % kc, [128, TW], BF16) for kc in range(8)]
    cat = [k.tile("cat_%d" % kc, [128, TW], BF16) for kc in range(8)]
    S32 = [k.tile("S32_%d" % h, [128, 128], F32) for h in range(4)]
    Sbf = [k.tile("Sbf_%d" % h, [128, 128], BF16) for h in range(4)]
    kTc = [k.tile("kTc_%d" % g, [128, 128 + TW], BF16) for g in range(2)]
    vpad = [k.tile("vpad_%d" % g, [128, 3, 192], BF16) for g in range(2)]
    phalo = k.tile("phalo", [128, 8, 16], F32)
    arena0 = k.sb_off
    ARENA = 21 * 1024
    k.sb_off += ARENA
    assert k.sb_off <= 16384 + 212800, k.sb_off
    ar = {"off": arena0}
    acache = {}
    pcache = {}

    def a_reset():
        ar["off"] = arena0

    def a_tile(name, shape, dt):
        esz = 4 if dt in (F32, I32) else 2
        nb = int(np.prod(shape[1:])) * esz
        off = ar["off"]
        ar["off"] += (nb + 31) // 32 * 32
        assert ar["off"] <= arena0 + ARENA, (name, ar["off"] - arena0)
        key = (off, tuple(shape), str(dt))
        if key not in acache:
            acache[key] = k.tile_at("ar_%s_%d" % (name, k.nbuf_inc()), shape, dt, off)
        return acache[key]

    def nbuf_inc():
        k.nbuf += 1
        return k.nbuf
    k.nbuf_inc = nbuf_inc

    Bk = []
    for i in range(7):
        ph = nc.alloc_psum_tensor("pb%d" % i, [128, 512], F32)
        Bk.append(ph)
    B7h = nc.alloc_psum_tensor("pb7", [128, 1024], BF16)

    def pt(i, lo, hi):
        key = (i, lo, hi)
        if key not in pcache:
            if i == 7:
                pcache[key] = k.sub(B7h[:, lo:hi], "pb7", 0, 1)
            else:
                pcache[key] = k.sub(Bk[i][:, lo:hi], "pb%d" % i, 0, 1)
        return pcache[key]

    ws = {"cur": 0, "emitted": 0, "done": 0}

    def emit_dma(u):
        slot = ring[u % R]
        k.dma(POOL, slot, slot.ap, ws_d[u])

    def ws_prime():
        while ws["emitted"] < min(R, U):
            emit_dma(ws["emitted"])
            ws["emitted"] += 1

    def ws_get(n, kinds=None):
        res = []
        for i in range(n):
            u = ws["cur"]
            assert u < ws["emitted"], ("ring deadlock", u)
            if kinds is not None:
                assert plan[u][0] == kinds[i], (plan[u], kinds[i])
            res.append(ring[u % R])
            ws["cur"] += 1
        return res

    def ws_done(n):
        for i in range(n):
            ws["done"] += 1
            if ws["emitted"] < U:
                assert ws["emitted"] - ws["done"] < R
                emit_dma(ws["emitted"])
                ws["emitted"] += 1

    def w3(slot):
        return slot.ap.rearrange("p (k c) -> p k c", c=128)

    k.dma(SP, cf, cf.ap, cf_d)
    k.dma(SP, par, par.ap, par_d)
    k.dma(POOL, cb, cb.ap, cb_d)
    ws_prime()
    ident = cb.ap[:, CB_ID:CB_ID + 128]
    ones = cb.ap[:, CB_ONES:CB_ONES + 128]
    pc = lambda c0, n=1: par.ap[:, c0:c0 + n]
    cfc = lambda c0: cf.ap[:, c0:c0 + 1]

    k.op(ACT, lambda: act.activation(cact.ap.rearrange("p k b -> p (k b)"), par.ap[:, P_C:P_C + 16], AF.Silu), [par], [cact])
    lb2 = lbt.ap
    k.op(DVE, lambda: dve.memset(lb2[:, 0, 0, :], 0.0), [], [lbt])
    k.op(DVE, lambda: dve.tensor_tensor(out=lb2[:, 1, 1, :], in0=par.ap[:, P_LBR + 4:P_LBR + 8], in1=par.ap[:, P_LBR:P_LBR + 4], op=OP.subtract), [par], [lbt])
    k.op(ACT, lambda: act.activation(lb2[:, 1, 0, :], lb2[:, 1, 1, :], AF.Sigmoid), [lbt], [lbt])
    k.op(DVE, lambda: dve.memset(lb2[:, 0, 1, :], 1.0), [], [lbt])
    k.op(DVE, lambda: dve.tensor_scalar(out=lb2[:, 1, 1, :], in0=lb2[:, 1, 0, :], scalar1=-1.0, scalar2=1.0, op0=OP.mult, op1=OP.add), [lbt], [lbt])
    k.op(DVE, lambda: dve.tensor_scalar(out=lb2[:, :, 1, :], in0=lb2[:, :, 1, :], scalar1=0.5, scalar2=None, op0=OP.mult), [lbt], [lbt])
    k.op(DVE, lambda: dve.tensor_tensor(out=lb2[:, :, 0, :], in0=lb2[:, :, 0, :], in1=lb2[:, :, 1, :], op=OP.add), [lbt], [lbt])
    k.op(DVE, lambda: dve.tensor_scalar(out=ngs.ap, in0=par.ap[:, P_NG:P_NG + 2], scalar1=float(np.sqrt(128.0)), scalar2=None, op0=OP.mult), [par], [ngs])

    def compute_mod(l, n0=0, n1=48):
        nn = n1 - n0
        mps = pt(6, 0, 96)
        for n in range(n0, n1):
            (u,) = ws_get(1, ["ada"])
            assert plan[ws["cur"] - 1] == ("ada", l, n)
            c0 = (n - n0) * 2
            for kc in range(8):
                k.op(PE, lambda kc=kc: pe.matmul(mps.ap[:, c0:c0 + 2], w3(u)[:, kc, :], cact.ap[:, kc, :], start=(kc == 0), stop=(kc == 7)),
                     [u, cact], [mps], inc=(kc == 7))
            ws_done(1)
        k.op(DVE, lambda: dve.tensor_tensor(out=modT.ap[:, l, n0:n1, :], in0=mps.ap[:, 0:2 * nn].rearrange("p (n b) -> p n b", b=2),
                                            in1=par.ap[:, P_ADAB + l * 48 + n0:P_ADAB + l * 48 + n1].rearrange("p (n o) -> p n o", o=1).to_broadcast([128, nn, 2]),
                                            op=OP.add), [mps, par], [modT])

    compute_mod(0)

    CO_S1, CO_G1, CO_A1, CO_A0, CO_B1, CO_B0, CO_C1, CO_C0 = range(8)

    def compute_coefs(l, b, last):
        m = modT.ap[:, l]
        md = lambda i: m[:, i * 8:(i + 1) * 8, b]
        co = lambda i: coef.ap[:, i, :]
        lng = lambda j: par.ap[:, P_LNG + (l * 2 + j) * 8:P_LNG + (l * 2 + j) * 8 + 8]
        lnb = lambda j: par.ap[:, P_LNB + (l * 2 + j) * 8:P_LNB + (l * 2 + j) * 8 + 8]
        rw = [modT, par, coef, ctmp]
        k.op(DVE, lambda: dve.tensor_scalar(out=co(CO_S1), in0=md(1), scalar1=1.0 / ALPHA, scalar2=1.0 / ALPHA, op0=OP.mult, op1=OP.add), rw, [coef])
        if l % 2 == 1:
            o = l // 2
            k.op(DVE, lambda: dve.tensor_tensor(out=co(CO_G1), in0=md(2), in1=par.ap[:, P_PSC + o * 8:P_PSC + o * 8 + 8], op=OP.mult), rw, [coef])
        else:
            k.op(DVE, lambda: dve.tensor_copy(co(CO_G1), md(2)), rw, [coef])
        k.op(DVE, lambda: dve.tensor_scalar(out=co(CO_A1), in0=lng(0), scalar1=ALPHA, scalar2=None, op0=OP.mult), rw, [coef])
        k.op(DVE, lambda: dve.tensor_scalar(out=co(CO_A0), in0=lnb(0), scalar1=ALPHA, scalar2=None, op0=OP.mult), rw, [coef])
        k.op(DVE, lambda: dve.tensor_scalar(out=ctmp.ap, in0=md(4), scalar1=1.0, scalar2=None, op0=OP.add), rw, [ctmp])
        k.op(DVE, lambda: dve.tensor_tensor(out=co(CO_B1), in0=lng(0), in1=ctmp.ap, op=OP.mult), rw, [coef])
        k.op(DVE, lambda: dve.tensor_tensor(out=co(CO_B0), in0=lnb(0), in1=ctmp.ap, op=OP.mult), rw, [coef])
        k.op(DVE, lambda: dve.tensor_tensor(out=co(CO_B0), in0=co(CO_B0), in1=md(3), op=OP.add), rw, [coef])
        a2 = 1.0 if last else ALPHA
        k.op(DVE, lambda: dve.tensor_scalar(out=co(CO_C1), in0=lng(1), scalar1=a2, scalar2=None, op0=OP.mult), rw, [coef])
        k.op(DVE, lambda: dve.tensor_scalar(out=co(CO_C0), in0=lnb(1), scalar1=a2, scalar2=None, op0=OP.mult), rw, [coef])

    cc = lambda i, kc: coef.ap[:, i, kc:kc + 1]

    def layer_norm(tt, c1, c0, hb=None, out_dma=None):
        a_reset()
        rb = [a_tile("rb", [128, TW], BF16) for _ in range(3)]
        rsq = [a_tile("rsq", [128, TW], BF16) for _ in range(3)]
        mu = a_tile("mu", [128, TW], F32)
        msq = a_tile("msq", [128, TW], F32)
        rstd = a_tile("rstd", [128, TW], F32)
        nrm = [a_tile("nrm", [128, TW], F32) for _ in range(3)]
        psum_s = pt(LNB["b"], 0, TW)
        psum_q = pt(LNB["b"], TW, 2 * TW)
        for n in range(8):
            x_t = xa[n][tt]
            k.op(DVE, lambda: dve.tensor_copy(rb[n % 3].ap, x_t.ap), [x_t], [rb[n % 3]])
            k.op(ACT, lambda: act.activation(rsq[n % 3].ap, x_t.ap, AF.Square), [x_t], [rsq[n % 3]])
            k.op(PE, lambda: pe.matmul(psum_s.ap, ones, rb[n % 3].ap, start=(n == 0), stop=False), [cb, rb[n % 3]], [psum_s], inc=True)
            k.op(PE, lambda: pe.matmul(psum_q.ap, ones, rsq[n % 3].ap, start=False, stop=(n == 7)), [cb, rsq[n % 3]], [psum_q], inc=True)
        k.op(ACT, lambda: act.activation(mu.ap, psum_s.ap, AF.Identity, scale=1.0 / D), [psum_s], [mu])
        k.op(ACT, lambda: act.activation(msq.ap, psum_s.ap, AF.Square, scale=1.0 / D), [psum_s], [msq])
        k.op(DVE, lambda: dve.scalar_tensor_tensor(out=rstd.ap, in0=psum_q.ap, scalar=1.0 / D, in1=msq.ap, op0=OP.mult, op1=OP.subtract), [psum_q, msq], [rstd])
        k.op(DVE, lambda: dve.tensor_scalar(out=rstd.ap, in0=rstd.ap, scalar1=LN_EPS, scalar2=None, op0=OP.add), [rstd], [rstd])
        k.op(ACT, lambda: act.activation(rstd.ap, rstd.ap, AF.Ln), [rstd], [rstd])
        k.op(ACT, lambda: act.activation(rstd.ap, rstd.ap, AF.Exp, scale=-0.5), [rstd], [rstd])
        for n in range(8):
            x_t = xa[n][tt]
            nm = nrm[n % 3]
            k.op(DVE, lambda: dve.tensor_tensor(out=nm.ap, in0=x_t.ap, in1=mu.ap, op=OP.subtract), [x_t, mu], [nm])
            k.op(DVE, lambda: dve.tensor_tensor(out=nm.ap, in0=nm.ap, in1=rstd.ap, op=OP.mult), [nm, rstd], [nm])
            k.op(POOL, lambda: pool.tensor_scalar(out=x_t.ap, in0=nm.ap, scalar1=cc(c1, n), scalar2=cc(c0, n), op0=OP.mult, op1=OP.add), [nm, coef], [x_t])
            if hb is not None:
                k.op(ACT, lambda: act.activation(Hh[n][tt].ap, nm.ap, AF.Identity, scale=cc(hb[0], n), bias=cc(hb[1], n)), [nm, coef], [Hh[n][tt]])
        if out_dma is not None:
            b = out_dma
            semt = xa[0][tt]
            k.dma(SP, None, out_d[b, :, :, tt * TW:(tt + 1) * TW], xa_h[:, :, tt * TW:(tt + 1) * TW], reads=[xa[n][tt] for n in range(8)], semt=semt)
            out_waits.append((semt.sem, semt.cnt))

    out_waits = []

    def proj_head(uA, hd):
        pq, pf = pt(0, 0, TW), pt(0, TW, 2 * TW)
        pg, pv = pt(1, 0, TW), pt(1, TW, 2 * TW)
        uq, uf, ui, ug = uA[hd * 4:hd * 4 + 4]
        for (u, po) in ((uf, pf), (ug, pg), (uq, pq)):
            for kc in range(8):
                k.op(PE, lambda: pe.matmul(po.ap, w3(u)[:, kc, :], h1[kc].ap, start=(kc == 0), stop=(kc == 7)), [u, h1[kc]], [po], inc=(kc == 7))
        for j in range(2):
            for kc in range(8):
                k.op(PE, lambda: pe.matmul(pv.ap[:, j * 128:(j + 1) * 128], h1[kc].ap[:, j * 128:(j + 1) * 128], w3(ui)[:, kc, :], start=(kc == 0), stop=(kc == 7)),
                     [ui, h1[kc]], [pv], inc=(kc == 7))

    def even_prologue(tt, units, b):
        for kc in range(8):
            k.op(ACT, lambda: act.activation(h1[kc].ap, xa[kc][tt].ap, AF.Identity, scale=cc(CO_S1, kc), bias=modT.ap[:, lcur["l"], kc, b:b + 1]),
                 [xa[kc][tt], coef, modT], [h1[kc]])
        proj_head(units[0:16], 0)

    def even_mixer_tile(e, tt, units, b):
        uA = units[0:16]
        uBq = units[16:20]
        uBk = units[20:22]
        uBv = units[22]
        uWo = units[23:31]
        a_reset()
        sig = a_tile("sig", [128, TW], F32)
        lf = a_tile("lf", [128, TW], F32)
        kk = a_tile("kk", [128, TW], F32)
        bb = a_tile("bb", [128, TW], F32)
        bcl = a_tile("bcl", [128, TW], F32)
        eb = a_tile("eb", [128, TW], F32)
        ebe = [a_tile("ebe", [128, 8], F32) for _ in range(4)]
        sg = [a_tile("sg", [128, TW], F32) for _ in range(4)]
        osb, rs = sig, lf
        Qt = [a_tile("Qt", [128, TW], BF16) for _ in range(4)]
        Kt = a_tile("Kt", [128, TW], BF16)
        Kh = [a_tile("Kh", [128, TW], BF16) for _ in range(4)]
        vsb = [a_tile("vsb", [128, 2, 128], BF16) for _ in range(4)]
        KhT = [a_tile("KhT", [128, 128], BF16) for _ in range(4)]
        osq = Kt
        pq, pf = pt(0, 0, TW), pt(0, TW, 2 * TW)
        pg, pv = pt(1, 0, TW), pt(1, TW, 2 * TW)
        patt = pt(2, 0, 128)
        pss = pt(2, 256, 512)
        pkh = pt(7, 0, 128)
        pS = [pt(3 + h % 2, (h // 2) * 128, (h // 2 + 1) * 128) for h in range(4)]
        pO = [pt(5 + h // 2, (h % 2) * TW, (h % 2 + 1) * TW) for h in range(4)]
        qs = a_tile("qs", [128, TW], F32)
        xtra = a_tile("xtra", [128, TW], F32)

        def proj(hd):
            proj_head(uA, hd)

        def chain(hd):
            k.op(ACT, lambda: act.activation(sig.ap, pf.ap, AF.Tanh, scale=0.5), [pf], [sig])
            k.op(ACT, lambda: act.activation(sg[hd].ap, pg.ap, AF.Silu), [pg], [sg[hd]])
            k.op(ACT, lambda: act.activation(qs.ap, pq.ap, AF.Copy), [pq], [qs])
            k.op(ACT, lambda: act.activation(vsb[hd].ap.rearrange("p j d -> p (j d)"), pv.ap, AF.Copy), [pv], [vsb[hd]])
            k.op(DVE, lambda: dve.tensor_scalar(out=sig.ap, in0=sig.ap, scalar1=lbt.ap[:, e, 1, hd:hd + 1], scalar2=lbt.ap[:, e, 0, hd:hd + 1], op0=OP.mult, op1=OP.add), [sig, lbt], [sig])
            k.op(ACT, lambda: act.activation(lf.ap, sig.ap, AF.Ln), [sig], [lf])
            k.op(DVE, lambda: dve.tensor_scalar(out=kk.ap, in0=sig.ap, scalar1=-1.0, scalar2=1.0, op0=OP.mult, op1=OP.add), [sig], [kk])
            k.op(DVE, lambda: dve.tensor_tensor_scan(bb.ap, cf.ap[:, CF_CMASK:CF_CMASK + TW], lf.ap, 0.0, OP.mult, OP.add), [cf, lf], [bb])
            k.op(DVE, lambda: dve.tensor_scalar(out=bcl.ap, in0=bb.ap, scalar1=-80.0, scalar2=None, op0=OP.max), [bb], [bcl])
            k.op(ACT, lambda: act.activation(eb.ap, bb.ap, AF.Exp), [bb], [eb])
            k.op(ACT, lambda: act.activation(bcl.ap, bcl.ap, AF.Exp, scale=-1.0), [bcl], [bcl])
            ebv = eb.ap.rearrange("p (c k) -> p c k", k=32)
            k.op(DVE, lambda: dve.tensor_tensor(out=Qt[hd].ap, in0=qs.ap, in1=eb.ap, op=OP.mult), [qs, eb], [Qt[hd]])
            k.op(DVE, lambda: dve.tensor_copy(ebe[hd].ap.rearrange("p (c o) -> p c o", o=1), ebv[:, :, 31:32]), [eb], [ebe[hd]])
            k.op(DVE, lambda: dve.tensor_tensor(out=Kt.ap, in0=kk.ap, in1=bcl.ap, op=OP.mult), [kk, bcl], [Kt])
            k.op(DVE, lambda: dve.tensor_tensor(out=Kh[hd].ap.rearrange("p (c k) -> p c k", k=32), in0=Kt.ap.rearrange("p (c k) -> p c k", k=32),
                                                in1=ebv[:, :, 31:32].to_broadcast([128, 8, 32]), op=OP.mult), [Kt, eb], [Kh[hd]])

        def scores(hd):
            for j in range(2):
                k.op(PE, lambda: pe.matmul(patt.ap, Kt.ap[:, j * 128:(j + 1) * 128], Qt[hd].ap[:, j * 128:(j + 1) * 128], start=True, stop=True), [Kt, Qt[hd]], [patt])
                att_j = attT2[hd][j]
                k.op(DVE, lambda: dve.tensor_tensor(out=att_j.ap, in0=patt.ap, in1=cf.ap[:, CF_MBD:CF_MBD + 128], op=OP.mult), [patt, cf], [att_j])

        for hd in range(4):
            chain(hd)
            if hd + 1 < 4:
                proj(hd + 1)
            scores(hd)
        for j in range(2):
            for hd in range(4):
                k.op(PE, lambda: pe.transpose(pkh.ap, Kh[hd].ap[:, j * 128:(j + 1) * 128], ident), [Kh[hd], cb], [pkh])
                k.op(ACT, lambda: act.activation(KhT[hd].ap, pkh.ap, AF.Copy), [pkh], [KhT[hd]])
                k.op(PE, lambda: pe.matmul(pO[hd].ap[:, j * 128:(j + 1) * 128], vsb[hd].ap[:, j, :], attT2[hd][j].ap, start=(j == 0 and hd % 2 == 0), stop=False), [vsb[hd], attT2[hd][j]], [pO[hd]], inc=False)
            for c in range(4):
                col = j * 128 + c * 32
                ci = j * 4 + c
                for hd in range(4):
                    k.op(PE, lambda: pe.matmul(pO[hd].ap[:, col:col + 32], Sbf[hd].ap, Qt[hd].ap[:, col:col + 32], start=False, stop=(c == 3 and j == 1 and hd % 2 == 1)), [Sbf[hd], Qt[hd]], [pO[hd]], inc=True)
                    k.op(PE, lambda: pe.matmul(pS[hd].ap, KhT[hd].ap[c * 32:(c + 1) * 32, :], vsb[hd].ap[c * 32:(c + 1) * 32, j, :], start=True, stop=True, tile_position=(c * 32, 0)),
                         [KhT[hd], vsb[hd]], [pS[hd]])
                    k.op(DVE, lambda: dve.scalar_tensor_tensor(out=S32[hd].ap, in0=S32[hd].ap, scalar=ebe[hd].ap[:, ci:ci + 1], in1=pS[hd].ap, op0=OP.mult, op1=OP.add), [S32[hd], ebe[hd], pS[hd]], [S32[hd]])
                    k.op(ACT, lambda: act.activation(Sbf[hd].ap, S32[hd].ap, AF.Copy), [S32[hd]], [Sbf[hd]])
        osbs = [sig, lf, kk, bb]
        rss = [bcl, eb, qs, xtra]
        osqs = Kh
        psss = [pt(h // 2, (h % 2) * TW, (h % 2 + 1) * TW) for h in range(4)]
        for hd in range(4):
            k.op(ACT, lambda: act.activation(osbs[hd].ap, pO[hd].ap, AF.Copy), [pO[hd]], [osbs[hd]])
            k.op(ACT, lambda: act.activation(osqs[hd].ap, pO[hd].ap, AF.Square), [pO[hd]], [osqs[hd]])
            k.op(PE, lambda: pe.matmul(psss[hd].ap, ones, osqs[hd].ap, start=True, stop=True), [cb, osqs[hd]], [psss[hd]])
        for hd in range(4):
            k.op(DVE, lambda: dve.tensor_scalar(out=rss[hd].ap, in0=psss[hd].ap, scalar1=128.0 * RMS_EPS, scalar2=None, op0=OP.add), [psss[hd]], [rss[hd]])
        for hd in range(4):
            k.op(ACT, lambda: act.activation(rss[hd].ap, rss[hd].ap, AF.Ln), [rss[hd]], [rss[hd]])
        for hd in range(4):
            k.op(ACT, lambda: act.activation(rss[hd].ap, rss[hd].ap, AF.Exp, scale=-0.5), [rss[hd]], [rss[hd]])
        for hd in range(4):
            k.op(DVE, lambda: dve.tensor_tensor(out=osbs[hd].ap, in0=osbs[hd].ap, in1=rss[hd].ap, op=OP.mult), [osbs[hd], rss[hd]], [osbs[hd]])
            k.op(DVE, lambda: dve.scalar_tensor_tensor(out=cat[hd].ap, in0=osbs[hd].ap, scalar=ngs.ap[:, e:e + 1], in1=sg[hd].ap, op0=OP.mult, op1=OP.mult), [osbs[hd], ngs, sg[hd]], [cat[hd]])

        a_reset()
        qr = [[a_tile("qr", [128, TW], BF16) for _ in range(2)] for _ in range(2)]
        b_mark = ar["off"]
        xs = [a_tile("xs", [128, TW], F32) for _ in range(2)]
        cosF = a_tile("cosF", [128, TW], F32)
        sinF = a_tile("sinF", [128, TW], F32)
        posf = a_tile("posf", [128, TW], F32)
        posi = a_tile("posi", [128, TW], I32)
        t1 = a_tile("t1", [128, TW], F32)
        t2 = a_tile("t2", [128, TW], F32)
        nint = a_tile("nint", [128, TW], I32)
        k.dma(SP, posi, posi.ap, pos_d[b:b + 1, tt * TW:(tt + 1) * TW].to_broadcast([128, TW]))
        k.op(DVE, lambda: dve.tensor_copy(posf.ap, posi.ap), [posi], [posf])

        def sin_table(dst, phcol, addcol):
            PI = float(np.pi)
            k.op(DVE, lambda: dve.tensor_scalar(out=t1.ap, in0=posf.ap, scalar1=cfc(CF_INVF2), scalar2=cfc(phcol), op0=OP.mult, op1=OP.add), [posf, cf], [t1])
            k.op(DVE, lambda: dve.tensor_copy(nint.ap, t1.ap), [t1], [nint])
            k.op(DVE, lambda: dve.tensor_copy(t1.ap, nint.ap), [nint], [t1])
            k.op(DVE, lambda: dve.tensor_scalar(out=t2.ap, in0=posf.ap, scalar1=cfc(CF_INVF), scalar2=cfc(addcol), op0=OP.mult, op1=OP.add), [posf, cf], [t2])
            k.op(DVE, lambda: dve.scalar_tensor_tensor(out=t2.ap, in0=t1.ap, scalar=-CW1, in1=t2.ap, op0=OP.mult, op1=OP.add), [t1, t2], [t2])
            k.op(DVE, lambda: dve.scalar_tensor_tensor(out=t2.ap, in0=t1.ap, scalar=-CW2, in1=t2.ap, op0=OP.mult, op1=OP.add), [t1, t2], [t2])
            k.op(DVE, lambda: dve.tensor_scalar(out=t1.ap, in0=t2.ap, scalar1=-PI, scalar2=2.0 * PI, op0=OP.is_lt, op1=OP.mult), [t2], [t1])
            k.op(DVE, lambda: dve.tensor_tensor(out=t2.ap, in0=t2.ap, in1=t1.ap, op=OP.add), [t1, t2], [t2])
            k.op(DVE, lambda: dve.tensor_scalar(out=t1.ap, in0=t2.ap, scalar1=PI, scalar2=-2.0 * PI, op0=OP.is_gt, op1=OP.mult), [t2], [t1])
            k.op(DVE, lambda: dve.tensor_tensor(out=t2.ap, in0=t2.ap, in1=t1.ap, op=OP.add), [t1, t2], [t2])
            k.op(DVE, lambda: dve.tensor_scalar(out=t2.ap, in0=t2.ap, scalar1=-PI, scalar2=PI, op0=OP.max, op1=OP.min), [t2], [t2])
            k.op(ACT, lambda: act.activation(dst.ap, t2.ap, AF.Sin), [t2], [dst])

        sin_table(sinF, CF_PHS, CF_ZERO)
        k.op(DVE, lambda: dve.tensor_scalar(out=sinF.ap, in0=sinF.ap, scalar1=cfc(CF_SGN), scalar2=None, op0=OP.mult), [sinF, cf], [sinF])
        sin_table(cosF, CF_PHC, CF_HPI)
        prot = pt(4, 0, TW)

        def rope(u, pacc, xsb, dst_t, dst_ap, scale):
            for kc in range(8):
                k.op(PE, lambda: pe.matmul(pacc.ap, w3(u)[:, kc, :], h1[kc].ap, start=(kc == 0), stop=(kc == 7)), [u, h1[kc]], [pacc], inc=(kc == 7))
            k.op(ACT, lambda: act.activation(xsb.ap, pacc.ap, AF.Identity, scale=scale), [pacc], [xsb])
            k.op(PE, lambda: pe.matmul(prot.ap, cf.ap[:, CF_PERM:CF_PERM + 128], xsb.ap, start=True, stop=True), [cf, xsb], [prot])
            k.op(DVE, lambda: dve.tensor_tensor(out=t1.ap, in0=xsb.ap, in1=cosF.ap, op=OP.mult), [xsb, cosF], [t1])
            k.op(DVE, lambda: dve.tensor_tensor(out=t2.ap, in0=prot.ap, in1=sinF.ap, op=OP.mult), [prot, sinF], [t2])
            k.op(DVE, lambda: dve.tensor_tensor(out=dst_ap, in0=t1.ap, in1=t2.ap, op=OP.add), [t1, t2], [dst_t])

        pacc = [pt(0, 0, TW), pt(2, 0, TW)]
        pvv = pt(1, 0, 128)
        pob = [pt(1, 256, 384), pt(1, 384, 512)]
        ps4s = [[pt(5, 0, 512), pt(6, 0, 512)], [pt(2, 0, 512), pt(3, 0, 512)]]
        ppt = pt(7, 0, 1024)
        for g in range(2):
            if tt == 0:
                k.op(DVE, lambda: dve.memset(kTc[g].ap[:, 0:128], 0.0), [], [kTc[g]])
                k.op(DVE, lambda: dve.memset(vpad[g].ap, 0.0), [], [vpad[g]])
            else:
                k.op(DVE, lambda: dve.tensor_copy(kTc[g].ap[:, 0:128], kTc[g].ap[:, TW:TW + 128]), [kTc[g]], [kTc[g]])
                k.op(DVE, lambda: dve.tensor_copy(vpad[g].ap[:, 0, 64:128], vpad[g].ap[:, 2, 64:128]), [vpad[g]], [vpad[g]])
            rope(uBk[g], pacc[0], xs[0], kTc[g], kTc[g].ap[:, 128:128 + TW], 1.0)
            for c in range(2):
                rope(uBq[g * 2 + c], pacc[(c + 1) % 2], xs[(c + 1) % 2], qr[g][c], qr[g][c].ap, 0.125)
            for j in range(2):
                for kc in range(8):
                    k.op(PE, lambda: pe.matmul(pvv.ap[:, j * 64:(j + 1) * 64], h1[kc].ap[:, j * 128:(j + 1) * 128], w3(uBv)[:, kc, g * 64:(g + 1) * 64], start=(kc == 0), stop=(kc == 7)),
                         [uBv, h1[kc]], [pvv], inc=(kc == 7))
            k.op(ACT, lambda: act.activation(vpad[g].ap[:, 1:3, 64:128], pvv.ap.rearrange("p (j d) -> p j d", d=64), AF.Copy), [pvv], [vpad[g]])
        ar["off"] = b_mark
        pexs = [a_tile("pex", [128, 4, 256], F32) for _ in range(2)]
        pns = [a_tile("pn", [128, 4, 256], BF16) for _ in range(2)]
        pT = a_tile("pT", [128, 8, 128], BF16)
        smalls = [[a_tile("sml", [128, 4], F32) for _ in range(4)] for _ in range(2)]
        blks = [(g, j) for g in range(2) for j in range(2)]

        def stageA(bi):
            g, j = blks[bi]
            ps4 = ps4s[bi % 2]
            pex = pexs[bi % 2]
            mx, ngm, sm, es = smalls[bi % 2]
            gblk = tt * 2 + j
            msk = cb.ap[:, CB_MS0:CB_MS0 + 256] if gblk == 0 else cb.ap[:, CB_MS:CB_MS + 256]
            for hl in range(4):
                c, rbp = hl // 2, (hl % 2) * 64
                so = ps4[hl // 2].ap[:, (hl % 2) * 256:(hl % 2 + 1) * 256]
                k.op(PE, lambda: pe.matmul(so, qr[g][c].ap[rbp:rbp + 64, j * 128:(j + 1) * 128], kTc[g].ap[rbp:rbp + 64, j * 128:j * 128 + 256], start=True, stop=False),
                     [qr[g][c], kTc[g]], [ps4[hl // 2]], inc=False)
                k.op(PE, lambda: pe.matmul(so, ident, msk, start=False, stop=True), [cb], [ps4[hl // 2]], inc=True)
            for pr in range(2):
                k.op(DVE, lambda: dve.tensor_reduce(out=mx.ap[:, pr * 2:pr * 2 + 2], in_=ps4[pr].ap.rearrange("p (h k) -> p h k", k=256), axis=AX.X, op=OP.max), [ps4[pr]], [mx])
            snk = par.ap[:, P_SINK + e * 8 + g * 4:P_SINK + e * 8 + g * 4 + 4]
            k.op(DVE, lambda: dve.tensor_tensor(out=mx.ap, in0=mx.ap, in1=snk, op=OP.max), [mx, par], [mx])
            k.op(DVE, lambda: dve.tensor_scalar(out=ngm.ap, in0=mx.ap, scalar1=-1.0, scalar2=None, op0=OP.mult), [mx], [ngm])
            k.op(DVE, lambda: dve.memset(sm.ap, 0.0), [], [sm])
            for hl in range(4):
                so = ps4[hl // 2].ap[:, (hl % 2) * 256:(hl % 2 + 1) * 256]
                k.op(ACT, lambda: act.activation(pex.ap[:, hl, :], so, AF.Exp, bias=ngm.ap[:, hl:hl + 1], accum_out=sm.ap[:, hl:hl + 1]), [ps4[hl // 2], ngm], [pex, sm])
            k.op(DVE, lambda: dve.tensor_tensor(out=es.ap, in0=snk, in1=ngm.ap, op=OP.add), [par, ngm], [es])
            k.op(ACT, lambda: act.activation(es.ap, es.ap, AF.Exp), [es], [es])

        def stageB(bi):
            g, j = blks[bi]
            pex, pn = pexs[bi % 2], pns[bi % 2]
            mx, ngm, sm, es = smalls[bi % 2]
            k.op(DVE, lambda: dve.tensor_tensor(out=es.ap, in0=es.ap, in1=sm.ap, op=OP.add), [es, sm], [es])
            k.op(DVE, lambda: dve.reciprocal(es.ap, es.ap), [es], [es])
            k.op(DVE, lambda: dve.tensor_tensor(out=pn.ap, in0=pex.ap, in1=es.ap.rearrange("p (h o) -> p h o", o=1).to_broadcast([128, 4, 256]), op=OP.mult), [pex, es], [pn])
            for hl in range(4):
                for hf in range(2):
                    i8 = hl * 2 + hf
                    k.op(PE, lambda: pe.transpose(ppt.ap[:, i8 * 128:(i8 + 1) * 128], pn.ap[:, hl, hf * 128:(hf + 1) * 128], ident), [pn, cb], [ppt], inc=(i8 == 7))
            k.op(ACT, lambda: act.activation(pT.ap.rearrange("p a q -> p (a q)"), ppt.ap, AF.Copy), [ppt], [pT])
            for pr in range(2):
                n4 = 0
                for hl in (2 * pr, 2 * pr + 1):
                    off = 64 if hl % 2 == 0 else 0
                    for hf in range(2):
                        k.op(PE, lambda: pe.matmul(pob[pr].ap, vpad[g].ap[:, j + hf, off:off + 128], pT.ap[:, hl * 2 + hf, :], start=(n4 == 0), stop=(n4 == 3)),
                             [vpad[g], pT], [pob[pr]], inc=(n4 == 3))
                        n4 += 1
                ct = cat[4 + g * 2 + pr]
                k.op(ACT, lambda: act.activation(ct.ap[:, j * 128:(j + 1) * 128], pob[pr].ap, AF.Copy), [pob[pr]], [ct])

        stageA(0)
        for bi in range(4):
            if bi + 1 < 4:
                stageA(bi + 1)
            stageB(bi)
        py = [pt(2, 0, TW), pt(3, 0, TW)]
        for n in range(8):
            po = py[n % 2]
            for kc in range(8):
                k.op(PE, lambda: pe.matmul(po.ap, w3(uWo[n])[:, kc, :], cat[kc].ap, start=(kc == 0), stop=(kc == 7)), [uWo[n], cat[kc]], [po], inc=(kc == 7))
            x_t = xa[n][tt]
            k.op(DVE, lambda: dve.scalar_tensor_tensor(out=x_t.ap, in0=po.ap, scalar=cc(CO_G1, n), in1=x_t.ap, op0=OP.mult, op1=OP.add), [po, coef, x_t], [x_t])

    attT2 = [[k.tile("attT2_%d_%d" % (h, j), [128, 128], BF16) for j in range(2)] for h in range(4)]
    assert k.sb_off - 16384 <= 210784, k.sb_off - 16384
    print("sbuf bytes used", k.sb_off - 16384, "of 212863")

    def odd_mixer_tile(o, tt, units, b):
        a_reset()
        hf = [a_tile("hf", [128, TW], F32) for _ in range(2)]
        Pb = [a_tile("Pb", [128, 16 + TW], F32) for _ in range(2)]
        wsm = [a_tile("wsm", [128, TW], F32) for _ in range(2)]
        pl = [a_tile("pl", [128, TW], F32) for _ in range(2)]
        fx = a_tile("fx", [128, 16], F32)
        onesb = cf.ap[:, CF_CMASK + 1:CF_CMASK + 2].to_broadcast([128, TW])
        def st1(kc):
            hh = hf[kc % 2]
            k.op(ACT, lambda: act.activation(hh.ap, xa[kc][tt].ap, AF.Identity, scale=cc(CO_S1, kc), bias=modT.ap[:, lcur["l"], kc, b:b + 1]),
                 [xa[kc][tt], coef, modT], [hh])

        def st2(kc):
            gi = kc // 2
            w = (2, 4, 8, 16)[gi]
            hh, P, wv, pp = hf[kc % 2], Pb[kc % 2], wsm[kc % 2], pl[kc % 2]
            if tt == 0:
                k.op(DVE, lambda: dve.memset(P.ap[:, 0:16], 0.0), [], [P])
            else:
                k.op(DVE, lambda: dve.tensor_copy(P.ap[:, 0:16], phalo.ap[:, kc, :]), [phalo], [P])
            init = 0.0 if tt == 0 else phalo.ap[:, kc, 15:16]
            k.op(DVE, lambda: dve.tensor_tensor_scan(P.ap[:, 16:16 + TW], onesb, hh.ap, init, OP.mult, OP.add), [cf, hh, phalo], [P])
            k.op(DVE, lambda: dve.tensor_copy(phalo.ap[:, kc, :], P.ap[:, TW:TW + 16]), [P], [phalo])
            k.op(POOL, lambda: pool.tensor_tensor(out=wv.ap, in0=P.ap[:, 16:16 + TW], in1=P.ap[:, 16 - w:16 - w + TW], op=OP.subtract), [P], [wv])
            k.op(DVE, lambda: dve.scalar_tensor_tensor(out=pp.ap, in0=wv.ap, scalar=1.0 / w, in1=hh.ap, op0=OP.mult, op1=OP.subtract), [wv, hh], [pp])
            if tt == 0:
                k.op(DVE, lambda: dve.tensor_tensor(out=fx.ap, in0=wv.ap[:, 0:16], in1=cf.ap[:, CF_PINV + gi * 16:CF_PINV + (gi + 1) * 16], op=OP.mult), [wv, cf], [fx])
                k.op(DVE, lambda: dve.tensor_tensor(out=pp.ap[:, 0:16], in0=fx.ap, in1=hh.ap[:, 0:16], op=OP.subtract), [fx, hh], [pp])

        def st3(kc):
            pp = pl[kc % 2]
            k.op(ACT, lambda: act.activation(h1[kc].ap, pp.ap, AF.Copy), [pp], [h1[kc]])

        st1(0)
        for kc in range(8):
            st2(kc)
            if kc + 1 < 8:
                st1(kc + 1)
            st3(kc)
        py = [pt(2, 0, TW), pt(3, 0, TW)]
        for n in range(8):
            gi, mm_ = n // 2, n % 2
            u = units[gi // 2]
            gl = gi % 2
            po = py[n % 2]
            for kc in range(2):
                k.op(PE, lambda: pe.matmul(po.ap, w3(u)[:, (gl * 2 + mm_) * 2 + kc, :], h1[gi * 2 + kc].ap, start=(kc == 0), stop=(kc == 1)), [u, h1[gi * 2 + kc]], [po], inc=(kc == 1))
            x_t = xa[n][tt]
            k.op(DVE, lambda: dve.scalar_tensor_tensor(out=x_t.ap, in0=po.ap, scalar=cc(CO_G1, n), in1=x_t.ap, op0=OP.mult, op1=OP.add), [po, coef, x_t], [x_t])

    def ffn_phase(l, b, last):
        for gidx, grp in enumerate(FF_GROUPS):
            ng = len(grp)
            units = ws_get(3 * ng, ["upu", "upv", "down"] * ng)
            lastg = gidx == len(FF_GROUPS) - 1
            a_reset()
            if lastg:
                ar["off"] = arena0 + 9216
            DP = 2 if lastg else 3
            ub = [[a_tile("ub", [128, 2 + TW], F32) for _ in range(2)] for _ in range(ng)]
            acc = [a_tile("acc", [128, TW], F32) for _ in range(DP)]
            NG = DP + 1
            gt = [a_tile("gt", [128, TW], BF16) for _ in range(NG)]
            puv = [(pt(4 + d, 0, TW), pt(4 + d, TW, 2 * TW)) for d in range(DP)]
            py = [pt(m // 2, (m % 2) * TW, (m % 2 + 1) * TW) for m in range(8)]
            items = [(tt, jl) for tt in range(NT) for jl in range(ng)]

            def up(i):
                tt, jl = items[i]
                uu, uv = units[3 * jl], units[3 * jl + 1]
                pu, pvv = puv[i % DP]
                for (u, po) in ((uu, pu), (uv, pvv)):
                    for kc in range(8):
                        k.op(PE, lambda: pe.matmul(po.ap, w3(u)[:, kc, :], Hh[kc][tt].ap, start=(kc == 0), stop=(kc == 7)), [u, Hh[kc][tt]], [po], inc=(kc == 7))

            def mid(i, stage):
                tt, jl = items[i]
                j = grp[jl]
                pu, pvv = puv[i % DP]
                u_c = ub[jl][tt % 2]
                u_n = ub[jl][(tt + 1) % 2]
                a_ = acc[i % DP]
                cw = lambda t3: par.ap[:, P_CW + (l * 3 + t3) * 22 + j:P_CW + (l * 3 + t3) * 22 + j + 1]
                cbias = par.ap[:, P_CB + l * 22 + j:P_CB + l * 22 + j + 1]
                if stage == 1:
                    if tt == 0:
                        k.op(DVE, lambda: dve.memset(u_c.ap[:, 0:2], 0.0), [], [u_c])
                    k.op(ACT, lambda: act.activation(u_c.ap[:, 2:2 + TW], pu.ap, AF.Copy), [pu], [u_c])
                    k.op(ACT, lambda: act.activation(a_.ap, pu.ap, AF.Identity, scale=cw(2), bias=cbias), [pu, par], [a_])
                elif stage == 2:
                    k.op(DVE, lambda: dve.scalar_tensor_tensor(out=a_.ap, in0=u_c.ap[:, 0:TW], scalar=cw(0), in1=a_.ap, op0=OP.mult, op1=OP.add), [u_c, par, a_], [a_])
                    k.op(DVE, lambda: dve.scalar_tensor_tensor(out=a_.ap, in0=u_c.ap[:, 1:1 + TW], scalar=cw(1), in1=a_.ap, op0=OP.mult, op1=OP.add), [u_c, par, a_], [a_])
                else:
                    k.op(ACT, lambda: act.activation(u_n.ap[:, 0:2], u_c.ap[:, TW:TW + 2], AF.Copy), [u_c], [u_n])

            def midB(i):
                pu, pvv = puv[i % DP]
                a_ = acc[i % DP]
                k.op(ACT, lambda: act.activation(a_.ap, a_.ap, AF.Silu), [a_], [a_])
                k.op(DVE, lambda: dve.tensor_tensor(out=gt[i % NG].ap, in0=a_.ap, in1=pvv.ap, op=OP.mult), [a_, pvv], [gt[i % NG]])

            def down(i):
                tt, jl = items[i]
                ud = units[3 * jl + 2]
                for m in range(8):
                    k.op(PE, lambda: pe.matmul(py[m].ap, ud.ap[:, m * 128:(m + 1) * 128], gt[i % NG].ap, start=(jl == 0 and m % 2 == 0), stop=(jl == ng - 1 and m % 2 == 1)), [ud, gt[i % NG]], [py[m]], inc=(m == 7 or (jl == ng - 1 and m % 2 == 1)))
                if jl == ng - 1:
                    for m in range(8):
                        x_t = xa[m][tt]
                        g2c = modT.ap[:, l, 40 + m, b:b + 1]
                        if m < 4 and OPT_EVAC:
                            tm = ytmp[m]
                            k.op(ACT, lambda: act.activation(tm.ap, py[m].ap, AF.Identity, scale=g2c, bias=cfc(CF_ZERO)), [py[m], modT, cf], [tm])
                            pend.append((x_t, tm))
                        else:
                            k.op(DVE, lambda: dve.scalar_tensor_tensor(out=x_t.ap, in0=py[m].ap, scalar=g2c, in1=x_t.ap, op0=OP.mult, op1=OP.add),
                                 [py[m], modT, x_t], [x_t])
                    if lastg:
                        flush_pend()
                    if lastg:
                        save = ar["off"]
                        layer_norm(tt, CO_C1, CO_C0, hb=None, out_dma=(outb["b"] if last else None))
                        ar["off"] = save

            ytmp = [a_tile("ytmp", [128, TW], F32) for _ in range(4)]
            pend = []

            def flush_pend():
                for (x_t, tm) in pend:
                    if OPT_POOLADD:
                        k.op(POOL, lambda: pool.tensor_tensor(out=x_t.ap, in0=x_t.ap, in1=tm.ap, op=OP.add), [x_t, tm], [x_t])
                    else:
                        k.op(DVE, lambda: dve.tensor_tensor(out=x_t.ap, in0=x_t.ap, in1=tm.ap, op=OP.add), [x_t, tm], [x_t])
                del pend[:]

            n_it = len(items)
            for d in range(DP - 1):
                if d < n_it:
                    up(d)
            HOIST = DP >= 3
            if HOIST:
                mid(0, 1)
            LAG = 3 if HOIST else 1
            for i in range(n_it + LAG):
                if i < n_it:
                    if not HOIST:
                        mid(i, 1)
                    mid(i, 2)
                    if HOIST and i + 1 < n_it:
                        mid(i + 1, 1)
                    mid(i, 3)
                    midB(i)
                flush_pend()
                if i + DP - 1 < n_it:
                    up(i + DP - 1)
                if i >= LAG:
                    down(i - LAG)
            flush_pend()
            ws_done(3 * ng)
            if b == 0 and l + 1 < n_layers:
                compute_mod(l + 1, gidx * 8, gidx * 8 + 8)

    outb = {"b": 0}
    lcur = {"l": 0}

    for s in range(n_seq):
        outb["b"] = s
        for tt in range(NT):
            k.dma(SP, [xa[kc][tt] for kc in range(8)], xa_h[:, :, tt * TW:(tt + 1) * TW], x_d[s, :, :, tt * TW:(tt + 1) * TW], reads=[], semt=xa[0][tt])
        for tt in range(NT):
            for kc in range(8):
                k.op(ACT if kc % 2 else DVE,
                     (lambda kc=kc, tt=tt: act.activation(xa[kc][tt].ap, xa[kc][tt].ap, AF.Identity, scale=ALPHA)) if kc % 2 else
                     (lambda kc=kc, tt=tt: dve.tensor_scalar(out=xa[kc][tt].ap, in0=xa[kc][tt].ap, scalar1=ALPHA, scalar2=None, op0=OP.mult)),
                     [xa[kc][tt]], [xa[kc][tt]])
        for l in range(n_layers):
            lcur["l"] = l
            last = l == n_layers - 1
            compute_coefs(l, s, last)
            if l % 2 == 0:
                e = l // 2
                for hd in range(4):
                    k.op(DVE, lambda: dve.memset(S32[hd].ap, 0.0), [], [S32[hd]])
                    k.op(DVE, lambda: dve.memset(Sbf[hd].ap, 0.0), [], [Sbf[hd]])
                units = ws_get(31)
                even_prologue(0, units, s)
                for tt in range(NT):
                    even_mixer_tile(e, tt, units, s)
                    if tt + 1 < NT:
                        even_prologue(tt + 1, units, s)
                    layer_norm(tt, CO_A1, CO_A0, hb=(CO_B1, CO_B0))
                ws_done(31)
            else:
                o = l // 2
                units = ws_get(2, ["pool", "pool"])
                for tt in range(NT):
                    odd_mixer_tile(o, tt, units, s)
                    layer_norm(tt, CO_A1, CO_A0, hb=(CO_B1, CO_B0))
                ws_done(2)
            ffn_phase(l, s, last)
    for (sem, val) in out_waits:
        nc.sync.wait_ge(sem, val)
    assert ws["cur"] == U, (ws["cur"], U)
    return nc, plan


_CACHE = {}


def kernel(**inputs):
    inp = {kk: np.asarray(v) for kk, v in inputs.items()}
    if "prog" not in _CACHE:
        _CACHE["prog"] = build_program()
    nc, plan = _CACHE["prog"]
    wstream = materialize_stream(plan, inp)
    cf, cb = make_consts()
    x = inp["x"].astype(np.float32)
    pos = inp["positions"].astype(np.int32)
    in_maps = []
    for c in range(NCORES):
        xc = x[2 * c:2 * c + 2]
        xfm = np.ascontiguousarray(xc.transpose(0, 2, 1).reshape(2, 8, 128, T).transpose(0, 2, 1, 3))
        in_maps.append({"xfm": xfm, "pos": np.ascontiguousarray(pos[2 * c:2 * c + 2]), "cf": cf, "cb": cb,
                        "par": make_params(inp, c), "wstream": wstream})
    res = run_bass_kernel_spmd(nc, in_maps, core_ids=list(range(NCORES)))
    out = np.zeros((16, T, D), np.float32)
    for c in range(NCORES):
        o = np.asarray(res.results[c]["ofm"])
        out[2 * c:2 * c + 2] = o.transpose(0, 2, 1, 3).reshape(2, D, T).transpose(0, 2, 1)
    return out
```

```python
import numpy as np
import concourse.bass as bass
import concourse.mybir as mybir
from concourse.bass_utils import run_bass_kernel_spmd

F32 = mybir.dt.float32
BF16 = mybir.dt.bfloat16
I32 = mybir.dt.int32
AF = mybir.ActivationFunctionType
OP = mybir.AluOpType
AX = mybir.AxisListType

D = 1024
T = 2048
DEPTH = 4
TW = 256
NT = T // TW
NFF = 22
ALPHA = (2.0 * DEPTH) ** 0.25
LN_EPS = 1e-5
RMS_EPS = 1e-6
R = 31
FF_GROUPS = [list(range(0, 4)), list(range(4, 8)), list(range(8, 12)), list(range(12, 16)),
             list(range(16, 20)), [20, 21]]
NCORES = 8
import os
OPT_LNPOOL = int(os.environ.get('K_LNPOOL', '1'))
OPT_EVAC = int(os.environ.get('K_EVAC', '1'))
OPT_POOLADD = int(os.environ.get('K_POOLADD', '1'))

CF_PERM, CF_MBD, CF_CMASK, CF_INVF, CF_SGN, CF_PI, CF_HPI, CF_2PI, CF_PINV, NCF = 0, 128, 256, 512, 513, 514, 515, 516, 520, 585
CF_INVF2, CF_PHS, CF_PHC, CF_ZERO = 517, 518, 519, 584
CW1 = 6.28125
CW2 = float(2.0 * np.pi - 6.28125)
CB_ID, CB_ONES, CB_MS, CB_MS0, NCB = 0, 128, 256, 512, 768
P_C, P_ADAB, P_LNG, P_LNB, P_LBR, P_NG, P_SINK, P_PSC, P_CW, P_CB = 0, 16, 208, 272, 336, 344, 346, 362, 378, 642
NPAR = 730


def stream_plan(n_layers=DEPTH, n_seq=2):
    plan = [("ada", 0, n) for n in range(48)]
    for s in range(n_seq):
        for l in range(n_layers):
            if l % 2 == 0:
                e = l // 2
                for h in range(4):
                    for base in (0, 512, 1024, 1536):
                        plan.append(("win", e, base + h * 128))
                for c in range(4):
                    plan.append(("win", e, 2048 + c * 128))
                for g in range(2):
                    plan.append(("bkdup", e, g))
                plan.append(("bv", e, 0))
                for n in range(8):
                    plan.append(("wout", e, n))
            else:
                o = l // 2
                plan.append(("pool", o, 0))
                plan.append(("pool", o, 1))
            for gidx, grp in enumerate(FF_GROUPS):
                for j in grp:
                    plan.append(("upu", l, j))
                    plan.append(("upv", l, j))
                    plan.append(("down", l, j))
                if s == 0 and l + 1 < n_layers:
                    plan += [("ada", l + 1, n) for n in range(gidx * 8, gidx * 8 + 8)]
    return plan


def _k1024(w):
    return np.ascontiguousarray(w.reshape(8, 128, 128).transpose(1, 0, 2)).reshape(128, 1024)


def materialize_stream(plan, inp):
    out = np.zeros((len(plan), 128, 1024), np.float32)
    cache = {}
    for i, u in enumerate(plan):
        if u in cache:
            out[i] = out[cache[u]]
            continue
        cache[u] = i
        kind, a, b = u
        if kind == "ada":
            out[i] = _k1024(inp["ada_w"][a][:, b * 128:(b + 1) * 128])
        elif kind == "win":
            out[i] = _k1024(inp["ab_w_in"][a][:, b:b + 128])
        elif kind == "bkdup":
            w = inp["ab_w_in"][a][:, 2560 + b * 64:2560 + (b + 1) * 64]
            out[i] = _k1024(np.concatenate([w, w], axis=1))
        elif kind == "bv":
            out[i] = _k1024(inp["ab_w_in"][a][:, 2688:2816])
        elif kind == "wout":
            out[i] = _k1024(inp["ab_w_out"][a][:, b * 128:(b + 1) * 128])
        elif kind == "pool":
            blk = np.zeros((128, 8, 128), np.float32)
            for gl in range(2):
                for m in range(2):
                    for kc in range(2):
                        blk[:, (gl * 2 + m) * 2 + kc, :] = inp["pool_w"][a][b * 2 + gl][kc * 128:(kc + 1) * 128, m * 128:(m + 1) * 128]
            out[i] = blk.reshape(128, 1024)
        elif kind == "upu":
            out[i] = _k1024(inp["ffn_w_up"][a][:, b * 128:(b + 1) * 128])
        elif kind == "upv":
            out[i] = _k1024(inp["ffn_w_up"][a][:, 2816 + b * 128:2816 + (b + 1) * 128])
        elif kind == "down":
            out[i] = inp["ffn_w_down"][a][b * 128:(b + 1) * 128, :]
    return out


def make_consts():
    cf = np.zeros((128, NCF), np.float32)
    cb = np.zeros((128, NCB), np.float32)
    p = np.arange(128)
    d = p % 64
    perm = np.zeros((128, 128), np.float32)
    for m in range(128):
        dm = m % 64
        if dm < 8:
            perm[m + 8, m] = 1.0
        elif dm < 16:
            perm[m - 8, m] = 1.0
    cf[:, CF_PERM:CF_PERM + 128] = perm
    s = p[:, None]
    t = p[None, :]
    cf[:, CF_MBD:CF_MBD + 128] = ((s // 32 == t // 32) & (s <= t)).astype(np.float32)
    cm = np.ones(256, np.float32)
    cm[::32] = 0.0
    cf[:, CF_CMASK:CF_CMASK + 256] = cm[None, :]
    invf = 500000.0 ** (-np.arange(0, 16, 2, dtype=np.float32) / 16.0)
    iv = np.zeros(128, np.float32)
    sg = np.zeros(128, np.float32)
    for r in range(128):
        dd = r % 64
        if dd < 8:
            iv[r] = invf[dd]
            sg[r] = -1.0
        elif dd < 16:
            iv[r] = invf[dd - 8]
            sg[r] = 1.0
    cf[:, CF_INVF] = iv
    cf[:, CF_SGN] = sg
    cf[:, CF_PI] = np.float32(np.pi)
    cf[:, CF_HPI] = np.float32(np.pi / 2)
    cf[:, CF_2PI] = np.float32(2 * np.pi)
    cf[:, CF_INVF2] = (iv.astype(np.float64) / (2 * np.pi)).astype(np.float32)
    cf[:, CF_PHS] = 0.5
    cf[:, CF_PHC] = 0.75
    for gi, w in enumerate((2, 4, 8, 16)):
        cnt = np.minimum(np.arange(16) + 1, w).astype(np.float32)
        cf[:, CF_PINV + gi * 16:CF_PINV + (gi + 1) * 16] = (1.0 / cnt)[None, :]
    cb[:, CB_ID:CB_ID + 128] = np.eye(128, dtype=np.float32)
    cb[:, CB_ONES:CB_ONES + 128] = 1.0
    q = p[:, None]
    ki = np.arange(256)[None, :]
    rel = q + 128 - ki
    allowed = (rel >= 0) & (rel < 128)
    cb[:, CB_MS:CB_MS + 256] = np.where(allowed, 0.0, -30000.0)
    cb[:, CB_MS0:CB_MS0 + 256] = np.where(allowed & (ki >= 128), 0.0, -30000.0)
    return cf, cb


def make_params(inp, core):
    pr = np.zeros((128, NPAR), np.float32)

    def fm(v):
        v = np.asarray(v)
        n = v.shape[-1] // 128
        v = v.reshape(v.shape[:-1] + (n, 128))
        return np.moveaxis(v, -1, 0)

    pr[:, P_C:P_C + 16] = fm(inp["c"][2 * core:2 * core + 2]).transpose(0, 2, 1).reshape(128, 16)
    pr[:, P_ADAB:P_ADAB + 192] = fm(inp["ada_b"]).reshape(128, 192)
    pr[:, P_LNG:P_LNG + 64] = fm(inp["ln_g"]).reshape(128, 64)
    pr[:, P_LNB:P_LNB + 64] = fm(inp["ln_b"]).reshape(128, 64)
    pr[:, P_LBR:P_LBR + 8] = fm(inp["hgrn_lb_raw"]).reshape(128, 8)
    pr[:, P_NG:P_NG + 2] = fm(inp["hgrn_norm_g"]).reshape(128, 2)
    pr[:, P_SINK:P_SINK + 16] = np.asarray(inp["attn_sinks"]).reshape(1, 16)
    pr[:, P_PSC:P_PSC + 16] = fm(inp["pool_scale"]).reshape(128, 16)
    pr[:, P_CW:P_CW + 264] = fm(inp["ffn_conv_w"]).reshape(128, 264)
    pr[:, P_CB:P_CB + 88] = fm(inp["ffn_conv_b"]).reshape(128, 88)
    return pr


class Tk:
    __slots__ = ("ap", "buf", "lo", "hi", "w", "r", "sem", "cnt", "ov", "ovv")

    def __init__(self, ap, buf, lo=0, hi=1):
        self.ap, self.buf, self.lo, self.hi = ap, buf, lo, hi
        self.w = None
        self.r = {}
        self.sem = None
        self.cnt = 0
        self.ov = None
        self.ovv = -1


class Eng:
    def __init__(self, k, eng, name, skip_own=False):
        self.k, self.eng, self.name, self.skip_own = k, eng, name, skip_own
        self.seen = {}
        self.pend = []
        self.sems = set()
        self.newsem()

    def newsem(self):
        self.sem = self.k.nc.alloc_semaphore(name="%s_s%d" % (self.name, self.k.nsem))
        self.k.nsem += 1
        self.cnt = 0
        self.sems.add(self.sem.num)


class Prog:
    def __init__(self, nc):
        self.nc = nc
        self.nsem = 0
        self.bufs = {}
        self.nbuf = 0
        self.sb_off = 16384
        self.PE = Eng(self, nc.tensor, "pe", skip_own=True)
        self.ACT = Eng(self, nc.scalar, "act")
        self.DVE = Eng(self, nc.vector, "dve")
        self.POOL = Eng(self, nc.gpsimd, "pool")
        self.SP = Eng(self, nc.sync, "sp")

    def sb(self, name, shape, dt, off=None):
        esz = 4 if dt in (F32, I32) else 2
        nbytes = int(np.prod(shape[1:])) * esz
        if off is None:
            off = self.sb_off
            self.sb_off += (nbytes + 31) // 32 * 32
        h = self.nc.alloc_sbuf_tensor_at(name, list(shape), dt, offset=off)
        return h, off, nbytes

    def tile(self, name, shape, dt):
        h, off, nb = self.sb(name, shape, dt)
        return self.reg(Tk(h[:], "sb_" + name, off, off + nb))

    def tile_at(self, name, shape, dt, off):
        h, off, nb = self.sb(name, shape, dt, off)
        return self.reg(Tk(h[:], "arena", off, off + nb))

    def reg(self, t):
        self.bufs.setdefault(t.buf, []).append(t)
        self.ver = getattr(self, "ver", 0) + 1
        return t

    def sub(self, parent_ap, buf, lo, hi):
        return self.reg(Tk(parent_ap, buf, lo, hi))

    def over(self, t):
        if t.ovv != self.ver:
            t.ov = [t2 for t2 in self.bufs[t.buf] if t2.lo < t.hi and t.lo < t2.hi]
            t.ovv = self.ver
        return t.ov

    def _need(self, reads, writes):
        need = {}

        def add(tok):
            if tok is None:
                return
            sem, val = tok
            if need.get(sem.num, (None, 0))[1] < val:
                need[sem.num] = (sem, val)
        for t in reads:
            psum = t.buf.startswith("pb")
            for t2 in self.over(t):
                add(t2.w)
                if psum:
                    for tok in t2.r.values():
                        add(tok)
        for t in writes:
            for t2 in self.over(t):
                add(t2.w)
                for tok in t2.r.values():
                    add(tok)
        return need

    def _wait(self, e, need):
        for key, (sem, val) in need.items():
            if e.skip_own and key in e.sems:
                continue
            if e.seen.get(key, 0) < val:
                e.eng.wait_ge(sem, val)
                e.seen[key] = val

    def _commit(self, tok, reads, writes):
        for t in writes:
            t.w = tok
            t.r = {}
        for t in reads:
            t.r[tok[0].num] = tok

    def op(self, e, fn, reads=(), writes=(), inc=True):
        self._wait(e, self._need(reads, writes))
        ins = fn()
        if not inc:
            e.pend.append((reads, writes))
            return
        if e.cnt >= 40000:
            e.newsem()
        e.cnt += 1
        ins.then_inc(e.sem, 1)
        tok = (e.sem, e.cnt)
        for (rs, ws) in e.pend:
            self._commit(tok, rs, ws)
        e.pend = []
        self._commit(tok, reads, writes)

    def dma(self, e, out_t, out_ap, in_ap, reads=(), semt=None):
        st = semt if semt is not None else (out_t[0] if isinstance(out_t, (list, tuple)) else out_t)
        if st.sem is None:
            st.sem = self.nc.alloc_semaphore(name="d%d" % self.nsem)
            self.nsem += 1
        writes = list(out_t) if isinstance(out_t, (list, tuple)) else ([out_t] if out_t is not None else [])
        if semt is None:
            st = writes[0]
            if st.sem is None:
                st.sem = self.nc.alloc_semaphore(name="d%d" % self.nsem)
                self.nsem += 1
        self._wait(e, self._need(reads, writes))
        e.eng.dma_start(out=out_ap, in_=in_ap).then_inc(st.sem, 16)
        st.cnt += 16
        self._commit((st.sem, st.cnt), reads, writes)


def build_program(n_layers=DEPTH, n_seq=2, nunits=None):
    nc = bass.Bass("TRN2", target_bir_lowering=False)
    plan = stream_plan(n_layers, n_seq)
    U = len(plan)
    x_d = nc.dram_tensor("xfm", [2, 128, 8, T], F32, kind="ExternalInput").ap()
    pos_d = nc.dram_tensor("pos", [2, T], I32, kind="ExternalInput").ap()
    cf_d = nc.dram_tensor("cf", [128, NCF], F32, kind="ExternalInput").ap()
    cb_d = nc.dram_tensor("cb", [128, NCB], F32, kind="ExternalInput").ap()
    par_d = nc.dram_tensor("par", [128, NPAR], F32, kind="ExternalInput").ap()
    ws_d = nc.dram_tensor("wstream", [U, 128, 1024], F32, kind="ExternalInput").ap()
    out_d = nc.dram_tensor("ofm", [2, 128, 8, T], F32, kind="ExternalOutput").ap()

    k = Prog(nc)
    PE, ACT, DVE, POOL, SP = k.PE, k.ACT, k.DVE, k.POOL, k.SP
    pe, act, dve, pool = nc.tensor, nc.scalar, nc.vector, nc.gpsimd
    LNB = {"b": 6}

    xa_h, xa_off, _ = k.sb("xa", [128, 8, T], F32)
    H_h, H_off, _ = k.sb("H", [128, 8, T], BF16)
    xa = [[k.reg(Tk(xa_h[:, kc, tt * TW:(tt + 1) * TW], "xa%d_%d" % (kc, tt))) for tt in range(NT)] for kc in range(8)]
    Hh = [[k.reg(Tk(H_h[:, kc, tt * TW:(tt + 1) * TW], "H%d_%d" % (kc, tt))) for tt in range(NT)] for kc in range(8)]
    ring = [k.tile("ring%d" % i, [128, 1024], BF16) for i in range(R)]
    cf = k.tile("cf", [128, NCF], F32)
    cb = k.tile("cb", [128, NCB], BF16)
    par = k.tile("par", [128, NPAR], F32)
    modT = k.tile("modT", [128, DEPTH, 48, 2], F32)
    coef = k.tile("coef", [128, 8, 8], F32)
    ctmp = k.tile("ctmp", [128, 8], F32)
    cact = k.tile("cact", [128, 8, 2], BF16)
    lbt = k.tile("lbt", [128, 2, 2, 4], F32)
    ngs = k.tile("ngs", [128, 2], F32)
    h1 = [k.tile("h1_%d" % kc, [128, TW], BF16) for kc in range(8)]
    cat = [k.tile("cat_%d" % kc, [128, TW], BF16) for kc in range(8)]
    S32 = [k.tile("S32_%d" % h, [128, 128], F32) for h in range(4)]
    Sbf = [k.tile("Sbf_%d" % h, [128, 128], BF16) for h in range(4)]
    kTc = [k.tile("kTc_%d" % g, [128, 128 + TW], BF16) for g in range(2)]
    vpad = [k.tile("vpad_%d" % g, [128, 3, 192], BF16) for g in range(2)]
    phalo = k.tile("phalo", [128, 8, 16], F32)
    arena0 = k.sb_off
    ARENA = 21 * 1024
    k.sb_off += ARENA
    assert k.sb_off <= 16384 + 212800, k.sb_off
    ar = {"off": arena0}
    acache = {}
    pcache = {}

    def a_reset():
        ar["off"] = arena0

    def a_tile(name, shape, dt):
        esz = 4 if dt in (F32, I32) else 2
        nb = int(np.prod(shape[1:])) * esz
        off = ar["off"]
        ar["off"] += (nb + 31) // 32 * 32
        assert ar["off"] <= arena0 + ARENA, (name, ar["off"] - arena0)
        key = (off, tuple(shape), str(dt))
        if key not in acache:
            acache[key] = k.tile_at("ar_%s_%d" % (name, k.nbuf_inc()), shape, dt, off)
        return acache[key]

    def nbuf_inc():
        k.nbuf += 1
        return k.nbuf
    k.nbuf_inc = nbuf_inc

    Bk = []
    for i in range(7):
        ph = nc.alloc_psum_tensor("pb%d" % i, [128, 512], F32)
        Bk.append(ph)
    B7h = nc.alloc_psum_tensor("pb7", [128, 1024], BF16)

    def pt(i, lo, hi):
        key = (i, lo, hi)
        if key not in pcache:
            if i == 7:
                pcache[key] = k.sub(B7h[:, lo:hi], "pb7", 0, 1)
            else:
                pcache[key] = k.sub(Bk[i][:, lo:hi], "pb%d" % i, 0, 1)
        return pcache[key]

    ws = {"cur": 0, "emitted": 0, "done": 0}

    def emit_dma(u):
        slot = ring[u % R]
        k.dma(POOL, slot, slot.ap, ws_d[u])

    def ws_prime():
        while ws["emitted"] < min(R, U):
            emit_dma(ws["emitted"])
            ws["emitted"] += 1

    def ws_get(n, kinds=None):
        res = []
        for i in range(n):
            u = ws["cur"]
            assert u < ws["emitted"], ("ring deadlock", u)
            if kinds is not None:
                assert plan[u][0] == kinds[i], (plan[u], kinds[i])
            res.append(ring[u % R])
            ws["cur"] += 1
        return res

    def ws_done(n):
        for i in range(n):
            ws["done"] += 1
            if ws["emitted"] < U:
                assert ws["emitted"] - ws["done"] < R
                emit_dma(ws["emitted"])
                ws["emitted"] += 1

    def w3(slot):
        return slot.ap.rearrange("p (k c) -> p k c", c=128)

    k.dma(SP, cf, cf.ap, cf_d)
    k.dma(SP, par, par.ap, par_d)
    k.dma(POOL, cb, cb.ap, cb_d)
    ws_prime()
    ident = cb.ap[:, CB_ID:CB_ID + 128]
    ones = cb.ap[:, CB_ONES:CB_ONES + 128]
    pc = lambda c0, n=1: par.ap[:, c0:c0 + n]
    cfc = lambda c0: cf.ap[:, c0:c0 + 1]

    k.op(ACT, lambda: act.activation(cact.ap.rearrange("p k b -> p (k b)"), par.ap[:, P_C:P_C + 16], AF.Silu), [par], [cact])
    lb2 = lbt.ap
    k.op(DVE, lambda: dve.memset(lb2[:, 0, 0, :], 0.0), [], [lbt])
    k.op(DVE, lambda: dve.tensor_tensor(out=lb2[:, 1, 1, :], in0=par.ap[:, P_LBR + 4:P_LBR + 8], in1=par.ap[:, P_LBR:P_LBR + 4], op=OP.subtract), [par], [lbt])
    k.op(ACT, lambda: act.activation(lb2[:, 1, 0, :], lb2[:, 1, 1, :], AF.Sigmoid), [lbt], [lbt])
    k.op(DVE, lambda: dve.memset(lb2[:, 0, 1, :], 1.0), [], [lbt])
    k.op(DVE, lambda: dve.tensor_scalar(out=lb2[:, 1, 1, :], in0=lb2[:, 1, 0, :], scalar1=-1.0, scalar2=1.0, op0=OP.mult, op1=OP.add), [lbt], [lbt])
    k.op(DVE, lambda: dve.tensor_scalar(out=lb2[:, :, 1, :], in0=lb2[:, :, 1, :], scalar1=0.5, scalar2=None, op0=OP.mult), [lbt], [lbt])
    k.op(DVE, lambda: dve.tensor_tensor(out=lb2[:, :, 0, :], in0=lb2[:, :, 0, :], in1=lb2[:, :, 1, :], op=OP.add), [lbt], [lbt])
    k.op(DVE, lambda: dve.tensor_scalar(out=ngs.ap, in0=par.ap[:, P_NG:P_NG + 2], scalar1=float(np.sqrt(128.0)), scalar2=None, op0=OP.mult), [par], [ngs])

    def compute_mod(l, n0=0, n1=48):
        nn = n1 - n0
        mps = pt(6, 0, 96)
        for n in range(n0, n1):
            (u,) = ws_get(1, ["ada"])
            assert plan[ws["cur"] - 1] == ("ada", l, n)
            c0 = (n - n0) * 2
            for kc in range(8):
                k.op(PE, lambda kc=kc: pe.matmul(mps.ap[:, c0:c0 + 2], w3(u)[:, kc, :], cact.ap[:, kc, :], start=(kc == 0), stop=(kc == 7)),
                     [u, cact], [mps], inc=(kc == 7))
            ws_done(1)
        k.op(DVE, lambda: dve.tensor_tensor(out=modT.ap[:, l, n0:n1, :], in0=mps.ap[:, 0:2 * nn].rearrange("p (n b) -> p n b", b=2),
                                            in1=par.ap[:, P_ADAB + l * 48 + n0:P_ADAB + l * 48 + n1].rearrange("p (n o) -> p n o", o=1).to_broadcast([128, nn, 2]),
                                            op=OP.add), [mps, par], [modT])

    compute_mod(0)

    CO_S1, CO_G1, CO_A1, CO_A0, CO_B1, CO_B0, CO_C1, CO_C0 = range(8)

    def compute_coefs(l, b, last):
        m = modT.ap[:, l]
        md = lambda i: m[:, i * 8:(i + 1) * 8, b]
        co = lambda i: coef.ap[:, i, :]
        lng = lambda j: par.ap[:, P_LNG + (l * 2 + j) * 8:P_LNG + (l * 2 + j) * 8 + 8]
        lnb = lambda j: par.ap[:, P_LNB + (l * 2 + j) * 8:P_LNB + (l * 2 + j) * 8 + 8]
        rw = [modT, par, coef, ctmp]
        k.op(DVE, lambda: dve.tensor_scalar(out=co(CO_S1), in0=md(1), scalar1=1.0 / ALPHA, scalar2=1.0 / ALPHA, op0=OP.mult, op1=OP.add), rw, [coef])
        if l % 2 == 1:
            o = l // 2
            k.op(DVE, lambda: dve.tensor_tensor(out=co(CO_G1), in0=md(2), in1=par.ap[:, P_PSC + o * 8:P_PSC + o * 8 + 8], op=OP.mult), rw, [coef])
        else:
            k.op(DVE, lambda: dve.tensor_copy(co(CO_G1), md(2)), rw, [coef])
        k.op(DVE, lambda: dve.tensor_scalar(out=co(CO_A1), in0=lng(0), scalar1=ALPHA, scalar2=None, op0=OP.mult), rw, [coef])
        k.op(DVE, lambda: dve.tensor_scalar(out=co(CO_A0), in0=lnb(0), scalar1=ALPHA, scalar2=None, op0=OP.mult), rw, [coef])
        k.op(DVE, lambda: dve.tensor_scalar(out=ctmp.ap, in0=md(4), scalar1=1.0, scalar2=None, op0=OP.add), rw, [ctmp])
        k.op(DVE, lambda: dve.tensor_tensor(out=co(CO_B1), in0=lng(0), in1=ctmp.ap, op=OP.mult), rw, [coef])
        k.op(DVE, lambda: dve.tensor_tensor(out=co(CO_B0), in0=lnb(0), in1=ctmp.ap, op=OP.mult), rw, [coef])
        k.op(DVE, lambda: dve.tensor_tensor(out=co(CO_B0), in0=co(CO_B0), in1=md(3), op=OP.add), rw, [coef])
        a2 = 1.0 if last else ALPHA
        k.op(DVE, lambda: dve.tensor_scalar(out=co(CO_C1), in0=lng(1), scalar1=a2, scalar2=None, op0=OP.mult), rw, [coef])
        k.op(DVE, lambda: dve.tensor_scalar(out=co(CO_C0), in0=lnb(1), scalar1=a2, scalar2=None, op0=OP.mult), rw, [coef])

    cc = lambda i, kc: coef.ap[:, i, kc:kc + 1]

    def layer_norm(tt, c1, c0, hb=None, out_dma=None):
        a_reset()
        rb = [a_tile("rb", [128, TW], BF16) for _ in range(3)]
        rsq = [a_tile("rsq", [128, TW], BF16) for _ in range(3)]
        mu = a_tile("mu", [128, TW], F32)
        msq = a_tile("msq", [128, TW], F32)
        rstd = a_tile("rstd", [128, TW], F32)
        nrm = [a_tile("nrm", [128, TW], F32) for _ in range(3)]
        psum_s = pt(LNB["b"], 0, TW)
        psum_q = pt(LNB["b"], TW, 2 * TW)
        for n in range(8):
            x_t = xa[n][tt]
            k.op(DVE, lambda: dve.tensor_copy(rb[n % 3].ap, x_t.ap), [x_t], [rb[n % 3]])
            k.op(ACT, lambda: act.activation(rsq[n % 3].ap, x_t.ap, AF.Square), [x_t], [rsq[n % 3]])
            k.op(PE, lambda: pe.matmul(psum_s.ap, ones, rb[n % 3].ap, start=(n == 0), stop=False), [cb, rb[n % 3]], [psum_s], inc=True)
            k.op(PE, lambda: pe.matmul(psum_q.ap, ones, rsq[n % 3].ap, start=False, stop=(n == 7)), [cb, rsq[n % 3]], [psum_q], inc=True)
        k.op(ACT, lambda: act.activation(mu.ap, psum_s.ap, AF.Identity, scale=1.0 / D), [psum_s], [mu])
        k.op(DVE, lambda: dve.tensor_tensor(out=msq.ap, in0=mu.ap, in1=mu.ap, op=OP.mult), [mu], [msq])
        k.op(DVE, lambda: dve.scalar_tensor_tensor(out=rstd.ap, in0=psum_q.ap, scalar=1.0 / D, in1=msq.ap, op0=OP.mult, op1=OP.subtract), [psum_q, msq], [rstd])
        k.op(DVE, lambda: dve.tensor_scalar(out=rstd.ap, in0=rstd.ap, scalar1=LN_EPS, scalar2=None, op0=OP.add), [rstd], [rstd])
        k.op(ACT, lambda: act.activation(rstd.ap, rstd.ap, AF.Ln), [rstd], [rstd])
        k.op(ACT, lambda: act.activation(rstd.ap, rstd.ap, AF.Exp, scale=-0.5), [rstd], [rstd])
        for n in range(8):
            x_t = xa[n][tt]
            nm = nrm[n % 3]
            k.op(DVE, lambda: dve.tensor_tensor(out=nm.ap, in0=x_t.ap, in1=mu.ap, op=OP.subtract), [x_t, mu], [nm])
            k.op(DVE, lambda: dve.tensor_tensor(out=nm.ap, in0=nm.ap, in1=rstd.ap, op=OP.mult), [nm, rstd], [nm])
            k.op(POOL, lambda: pool.tensor_scalar(out=x_t.ap, in0=nm.ap, scalar1=cc(c1, n), scalar2=cc(c0, n), op0=OP.mult, op1=OP.add), [nm, coef], [x_t])
            if hb is not None:
                k.op(ACT, lambda: act.activation(Hh[n][tt].ap, nm.ap, AF.Identity, scale=cc(hb[0], n), bias=cc(hb[1], n)), [nm, coef], [Hh[n][tt]])
        if out_dma is not None:
            b = out_dma
            semt = xa[0][tt]
            k.dma(SP, None, out_d[b, :, :, tt * TW:(tt + 1) * TW], xa_h[:, :, tt * TW:(tt + 1) * TW], reads=[xa[n][tt] for n in range(8)], semt=semt)
            out_waits.append((semt.sem, semt.cnt))

    out_waits = []

    def proj_head(uA, hd):
        pq, pf = pt(0, 0, TW), pt(0, TW, 2 * TW)
        pg, pv = pt(1, 0, TW), pt(1, TW, 2 * TW)
        uq, uf, ui, ug = uA[hd * 4:hd * 4 + 4]
        for (u, po) in ((uf, pf), (ug, pg), (uq, pq)):
            for kc in range(8):
                k.op(PE, lambda: pe.matmul(po.ap, w3(u)[:, kc, :], h1[kc].ap, start=(kc == 0), stop=(kc == 7)), [u, h1[kc]], [po], inc=(kc == 7))
        for j in range(2):
            for kc in range(8):
                k.op(PE, lambda: pe.matmul(pv.ap[:, j * 128:(j + 1) * 128], h1[kc].ap[:, j * 128:(j + 1) * 128], w3(ui)[:, kc, :], start=(kc == 0), stop=(kc == 7)),
                     [ui, h1[kc]], [pv], inc=(kc == 7))

    def even_prologue(tt, units, b):
        for kc in range(8):
            k.op(ACT, lambda: act.activation(h1[kc].ap, xa[kc][tt].ap, AF.Identity, scale=cc(CO_S1, kc), bias=modT.ap[:, lcur["l"], kc, b:b + 1]),
                 [xa[kc][tt], coef, modT], [h1[kc]])
        proj_head(units[0:16], 0)

    def even_mixer_tile(e, tt, units, b):
        uA = units[0:16]
        uBq = units[16:20]
        uBk = units[20:22]
        uBv = units[22]
        uWo = units[23:31]
        a_reset()
        sig = a_tile("sig", [128, TW], F32)
        lf = a_tile("lf", [128, TW], F32)
        kk = a_tile("kk", [128, TW], F32)
        bb = a_tile("bb", [128, TW], F32)
        bcl = a_tile("bcl", [128, TW], F32)
        eb = a_tile("eb", [128, TW], F32)
        ebe = [a_tile("ebe", [128, 8], F32) for _ in range(4)]
        sg = [a_tile("sg", [128, TW], F32) for _ in range(4)]
        osb, rs = sig, lf
        Qt = [a_tile("Qt", [128, TW], BF16) for _ in range(4)]
        Kt = a_tile("Kt", [128, TW], BF16)
        Kh = [a_tile("Kh", [128, TW], BF16) for _ in range(4)]
        vsb = [a_tile("vsb", [128, 2, 128], BF16) for _ in range(4)]
        KhT = [a_tile("KhT", [128, 128], BF16) for _ in range(4)]
        osq = Kt
        pq, pf = pt(0, 0, TW), pt(0, TW, 2 * TW)
        pg, pv = pt(1, 0, TW), pt(1, TW, 2 * TW)
        patt = pt(2, 0, 128)
        pss = pt(2, 256, 512)
        pkh = pt(7, 0, 128)
        pS = [pt(3 + h % 2, (h // 2) * 128, (h // 2 + 1) * 128) for h in range(4)]
        pO = [pt(5 + h // 2, (h % 2) * TW, (h % 2 + 1) * TW) for h in range(4)]
        qs = a_tile("qs", [128, TW], F32)
        xtra = a_tile("xtra", [128, TW], F32)

        def proj(hd):
            proj_head(uA, hd)

        def chain(hd):
            k.op(ACT, lambda: act.activation(sig.ap, pf.ap, AF.Tanh, scale=0.5), [pf], [sig])
            k.op(ACT, lambda: act.activation(sg[hd].ap, pg.ap, AF.Silu), [pg], [sg[hd]])
            k.op(ACT, lambda: act.activation(qs.ap, pq.ap, AF.Copy), [pq], [qs])
            k.op(ACT, lambda: act.activation(vsb[hd].ap.rearrange("p j d -> p (j d)"), pv.ap, AF.Copy), [pv], [vsb[hd]])
            k.op(DVE, lambda: dve.tensor_scalar(out=sig.ap, in0=sig.ap, scalar1=lbt.ap[:, e, 1, hd:hd + 1], scalar2=lbt.ap[:, e, 0, hd:hd + 1], op0=OP.mult, op1=OP.add), [sig, lbt], [sig])
            k.op(ACT, lambda: act.activation(lf.ap, sig.ap, AF.Ln), [sig], [lf])
            k.op(DVE, lambda: dve.tensor_scalar(out=kk.ap, in0=sig.ap, scalar1=-1.0, scalar2=1.0, op0=OP.mult, op1=OP.add), [sig], [kk])
            k.op(DVE, lambda: dve.tensor_tensor_scan(bb.ap, cf.ap[:, CF_CMASK:CF_CMASK + TW], lf.ap, 0.0, OP.mult, OP.add), [cf, lf], [bb])
            k.op(DVE, lambda: dve.tensor_scalar(out=bcl.ap, in0=bb.ap, scalar1=-80.0, scalar2=None, op0=OP.max), [bb], [bcl])
            k.op(ACT, lambda: act.activation(eb.ap, bb.ap, AF.Exp), [bb], [eb])
            k.op(ACT, lambda: act.activation(bcl.ap, bcl.ap, AF.Exp, scale=-1.0), [bcl], [bcl])
            ebv = eb.ap.rearrange("p (c k) -> p c k", k=32)
            k.op(DVE, lambda: dve.tensor_tensor(out=Qt[hd].ap, in0=qs.ap, in1=eb.ap, op=OP.mult), [qs, eb], [Qt[hd]])
            k.op(DVE, lambda: dve.tensor_copy(ebe[hd].ap.rearrange("p (c o) -> p c o", o=1), ebv[:, :, 31:32]), [eb], [ebe[hd]])
            k.op(DVE, lambda: dve.tensor_tensor(out=Kt.ap, in0=kk.ap, in1=bcl.ap, op=OP.mult), [kk, bcl], [Kt])
            k.op(DVE, lambda: dve.tensor_tensor(out=Kh[hd].ap.rearrange("p (c k) -> p c k", k=32), in0=Kt.ap.rearrange("p (c k) -> p c k", k=32),
                                                in1=ebv[:, :, 31:32].to_broadcast([128, 8, 32]), op=OP.mult), [Kt, eb], [Kh[hd]])

        def scores(hd):
            for j in range(2):
                k.op(PE, lambda: pe.matmul(patt.ap, Kt.ap[:, j * 128:(j + 1) * 128], Qt[hd].ap[:, j * 128:(j + 1) * 128], start=True, stop=True), [Kt, Qt[hd]], [patt])
                att_j = attT2[hd][j]
                k.op(DVE, lambda: dve.tensor_tensor(out=att_j.ap, in0=patt.ap, in1=cf.ap[:, CF_MBD:CF_MBD + 128], op=OP.mult), [patt, cf], [att_j])

        for hd in range(4):
            chain(hd)
            if hd + 1 < 4:
                proj(hd + 1)
            scores(hd)
        for j in range(2):
            for hd in range(4):
                k.op(PE, lambda: pe.transpose(pkh.ap, Kh[hd].ap[:, j * 128:(j + 1) * 128], ident), [Kh[hd], cb], [pkh])
                k.op(ACT, lambda: act.activation(KhT[hd].ap, pkh.ap, AF.Copy), [pkh], [KhT[hd]])
                k.op(PE, lambda: pe.matmul(pO[hd].ap[:, j * 128:(j + 1) * 128], vsb[hd].ap[:, j, :], attT2[hd][j].ap, start=(j == 0 and hd % 2 == 0), stop=False), [vsb[hd], attT2[hd][j]], [pO[hd]], inc=False)
            for c in range(4):
                col = j * 128 + c * 32
                ci = j * 4 + c
                for hd in range(4):
                    k.op(PE, lambda: pe.matmul(pO[hd].ap[:, col:col + 32], Sbf[hd].ap, Qt[hd].ap[:, col:col + 32], start=False, stop=(c == 3 and j == 1 and hd % 2 == 1)), [Sbf[hd], Qt[hd]], [pO[hd]], inc=True)
                    k.op(PE, lambda: pe.matmul(pS[hd].ap, KhT[hd].ap[c * 32:(c + 1) * 32, :], vsb[hd].ap[c * 32:(c + 1) * 32, j, :], start=True, stop=True, tile_position=(c * 32, 0)),
                         [KhT[hd], vsb[hd]], [pS[hd]])
                    k.op(DVE, lambda: dve.scalar_tensor_tensor(out=S32[hd].ap, in0=S32[hd].ap, scalar=ebe[hd].ap[:, ci:ci + 1], in1=pS[hd].ap, op0=OP.mult, op1=OP.add), [S32[hd], ebe[hd], pS[hd]], [S32[hd]])
                    k.op(ACT, lambda: act.activation(Sbf[hd].ap, S32[hd].ap, AF.Copy), [S32[hd]], [Sbf[hd]])
        osbs = [sig, lf, kk, bb]
        rss = [bcl, eb, qs, xtra]
        osqs = Kh
        psss = [pt(h // 2, (h % 2) * TW, (h % 2 + 1) * TW) for h in range(4)]
        for hd in range(4):
            k.op(ACT, lambda: act.activation(osbs[hd].ap, pO[hd].ap, AF.Copy), [pO[hd]], [osbs[hd]])
            k.op(ACT, lambda: act.activation(osqs[hd].ap, pO[hd].ap, AF.Square), [pO[hd]], [osqs[hd]])
            k.op(PE, lambda: pe.matmul(psss[hd].ap, ones, osqs[hd].ap, start=True, stop=True), [cb, osqs[hd]], [psss[hd]])
        for hd in range(4):
            k.op(DVE, lambda: dve.tensor_scalar(out=rss[hd].ap, in0=psss[hd].ap, scalar1=128.0 * RMS_EPS, scalar2=None, op0=OP.add), [psss[hd]], [rss[hd]])
        for hd in range(4):
            k.op(ACT, lambda: act.activation(rss[hd].ap, rss[hd].ap, AF.Ln), [rss[hd]], [rss[hd]])
        for hd in range(4):
            k.op(ACT, lambda: act.activation(rss[hd].ap, rss[hd].ap, AF.Exp, scale=-0.5), [rss[hd]], [rss[hd]])
        for hd in range(4):
            k.op(DVE, lambda: dve.tensor_tensor(out=osbs[hd].ap, in0=osbs[hd].ap, in1=rss[hd].ap, op=OP.mult), [osbs[hd], rss[hd]], [osbs[hd]])
            k.op(DVE, lambda: dve.scalar_tensor_tensor(out=cat[hd].ap, in0=osbs[hd].ap, scalar=ngs.ap[:, e:e + 1], in1=sg[hd].ap, op0=OP.mult, op1=OP.mult), [osbs[hd], ngs, sg[hd]], [cat[hd]])

        a_reset()
        qr = [[a_tile("qr", [128, TW], BF16) for _ in range(2)] for _ in range(2)]
        b_mark = ar["off"]
        xs = [a_tile("xs", [128, TW], F32) for _ in range(2)]
        cosF = a_tile("cosF", [128, TW], F32)
        sinF = a_tile("sinF", [128, TW], F32)
        posf = a_tile("posf", [128, TW], F32)
        posi = a_tile("posi", [128, TW], I32)
        t1 = a_tile("t1", [128, TW], F32)
        t2 = a_tile("t2", [128, TW], F32)
        nint = a_tile("nint", [128, TW], I32)
        k.dma(SP, posi, posi.ap, pos_d[b:b + 1, tt * TW:(tt + 1) * TW].to_broadcast([128, TW]))
        k.op(DVE, lambda: dve.tensor_copy(posf.ap, posi.ap), [posi], [posf])

        def sin_table(dst, phcol, addcol):
            PI = float(np.pi)
            k.op(DVE, lambda: dve.tensor_scalar(out=t1.ap, in0=posf.ap, scalar1=cfc(CF_INVF2), scalar2=cfc(phcol), op0=OP.mult, op1=OP.add), [posf, cf], [t1])
            k.op(DVE, lambda: dve.tensor_copy(nint.ap, t1.ap), [t1], [nint])
            k.op(DVE, lambda: dve.tensor_copy(t1.ap, nint.ap), [nint], [t1])
            k.op(DVE, lambda: dve.tensor_scalar(out=t2.ap, in0=posf.ap, scalar1=cfc(CF_INVF), scalar2=cfc(addcol), op0=OP.mult, op1=OP.add), [posf, cf], [t2])
            k.op(DVE, lambda: dve.scalar_tensor_tensor(out=t2.ap, in0=t1.ap, scalar=-CW1, in1=t2.ap, op0=OP.mult, op1=OP.add), [t1, t2], [t2])
            k.op(DVE, lambda: dve.scalar_tensor_tensor(out=t2.ap, in0=t1.ap, scalar=-CW2, in1=t2.ap, op0=OP.mult, op1=OP.add), [t1, t2], [t2])
            k.op(DVE, lambda: dve.tensor_scalar(out=t1.ap, in0=t2.ap, scalar1=-PI, scalar2=2.0 * PI, op0=OP.is_lt, op1=OP.mult), [t2], [t1])
            k.op(DVE, lambda: dve.tensor_tensor(out=t2.ap, in0=t2.ap, in1=t1.ap, op=OP.add), [t1, t2], [t2])
            k.op(DVE, lambda: dve.tensor_scalar(out=t1.ap, in0=t2.ap, scalar1=PI, scalar2=-2.0 * PI, op0=OP.is_gt, op1=OP.mult), [t2], [t1])
            k.op(DVE, lambda: dve.tensor_tensor(out=t2.ap, in0=t2.ap, in1=t1.ap, op=OP.add), [t1, t2], [t2])
            k.op(DVE, lambda: dve.tensor_scalar(out=t2.ap, in0=t2.ap, scalar1=-PI, scalar2=PI, op0=OP.max, op1=OP.min), [t2], [t2])
            k.op(ACT, lambda: act.activation(dst.ap, t2.ap, AF.Sin), [t2], [dst])

        sin_table(sinF, CF_PHS, CF_ZERO)
        k.op(DVE, lambda: dve.tensor_scalar(out=sinF.ap, in0=sinF.ap, scalar1=cfc(CF_SGN), scalar2=None, op0=OP.mult), [sinF, cf], [sinF])
        sin_table(cosF, CF_PHC, CF_HPI)
        prot = pt(4, 0, TW)

        def rope(u, pacc, xsb, dst_t, dst_ap, scale):
            for kc in range(8):
                k.op(PE, lambda: pe.matmul(pacc.ap, w3(u)[:, kc, :], h1[kc].ap, start=(kc == 0), stop=(kc == 7)), [u, h1[kc]], [pacc], inc=(kc == 7))
            k.op(ACT, lambda: act.activation(xsb.ap, pacc.ap, AF.Identity, scale=scale), [pacc], [xsb])
            k.op(PE, lambda: pe.matmul(prot.ap, cf.ap[:, CF_PERM:CF_PERM + 128], xsb.ap, start=True, stop=True), [cf, xsb], [prot])
            k.op(DVE, lambda: dve.tensor_tensor(out=t1.ap, in0=xsb.ap, in1=cosF.ap, op=OP.mult), [xsb, cosF], [t1])
            k.op(DVE, lambda: dve.tensor_tensor(out=t2.ap, in0=prot.ap, in1=sinF.ap, op=OP.mult), [prot, sinF], [t2])
            k.op(DVE, lambda: dve.tensor_tensor(out=dst_ap, in0=t1.ap, in1=t2.ap, op=OP.add), [t1, t2], [dst_t])

        pacc = [pt(0, 0, TW), pt(2, 0, TW)]
        pvv = pt(1, 0, 128)
        pob = [pt(1, 256, 384), pt(1, 384, 512)]
        ps4s = [[pt(5, 0, 512), pt(6, 0, 512)], [pt(2, 0, 512), pt(3, 0, 512)]]
        ppt = pt(7, 0, 1024)
        for g in range(2):
            if tt == 0:
                k.op(DVE, lambda: dve.memset(kTc[g].ap[:, 0:128], 0.0), [], [kTc[g]])
                k.op(DVE, lambda: dve.memset(vpad[g].ap, 0.0), [], [vpad[g]])
            else:
                k.op(DVE, lambda: dve.tensor_copy(kTc[g].ap[:, 0:128], kTc[g].ap[:, TW:TW + 128]), [kTc[g]], [kTc[g]])
                k.op(DVE, lambda: dve.tensor_copy(vpad[g].ap[:, 0, 64:128], vpad[g].ap[:, 2, 64:128]), [vpad[g]], [vpad[g]])
            rope(uBk[g], pacc[0], xs[0], kTc[g], kTc[g].ap[:, 128:128 + TW], 1.0)
            for c in range(2):
                rope(uBq[g * 2 + c], pacc[(c + 1) % 2], xs[(c + 1) % 2], qr[g][c], qr[g][c].ap, 0.125)
            for j in range(2):
                for kc in range(8):
                    k.op(PE, lambda: pe.matmul(pvv.ap[:, j * 64:(j + 1) * 64], h1[kc].ap[:, j * 128:(j + 1) * 128], w3(uBv)[:, kc, g * 64:(g + 1) * 64], start=(kc == 0), stop=(kc == 7)),
                         [uBv, h1[kc]], [pvv], inc=(kc == 7))
            k.op(ACT, lambda: act.activation(vpad[g].ap[:, 1:3, 64:128], pvv.ap.rearrange("p (j d) -> p j d", d=64), AF.Copy), [pvv], [vpad[g]])
        ar["off"] = b_mark
        pexs = [a_tile("pex", [128, 4, 256], F32) for _ in range(2)]
        pns = [a_tile("pn", [128, 4, 256], BF16) for _ in range(2)]
        pT = a_tile("pT", [128, 8, 128], BF16)
        smalls = [[a_tile("sml", [128, 4], F32) for _ in range(4)] for _ in range(2)]
        blks = [(g, j) for g in range(2) for j in range(2)]

        def stageA(bi):
            g, j = blks[bi]
            ps4 = ps4s[bi % 2]
            pex = pexs[bi % 2]
            mx, ngm, sm, es = smalls[bi % 2]
            gblk = tt * 2 + j
            msk = cb.ap[:, CB_MS0:CB_MS0 + 256] if gblk == 0 else cb.ap[:, CB_MS:CB_MS + 256]
            for hl in range(4):
                c, rbp = hl // 2, (hl % 2) * 64
                so = ps4[hl // 2].ap[:, (hl % 2) * 256:(hl % 2 + 1) * 256]
                k.op(PE, lambda: pe.matmul(so, qr[g][c].ap[rbp:rbp + 64, j * 128:(j + 1) * 128], kTc[g].ap[rbp:rbp + 64, j * 128:j * 128 + 256], start=True, stop=False),
                     [qr[g][c], kTc[g]], [ps4[hl // 2]], inc=False)
                k.op(PE, lambda: pe.matmul(so, ident, msk, start=False, stop=True), [cb], [ps4[hl // 2]], inc=True)
            for pr in range(2):
                k.op(DVE, lambda: dve.tensor_reduce(out=mx.ap[:, pr * 2:pr * 2 + 2], in_=ps4[pr].ap.rearrange("p (h k) -> p h k", k=256), axis=AX.X, op=OP.max), [ps4[pr]], [mx])
            snk = par.ap[:, P_SINK + e * 8 + g * 4:P_SINK + e * 8 + g * 4 + 4]
            k.op(DVE, lambda: dve.tensor_tensor(out=mx.ap, in0=mx.ap, in1=snk, op=OP.max), [mx, par], [mx])
            k.op(DVE, lambda: dve.tensor_scalar(out=ngm.ap, in0=mx.ap, scalar1=-1.0, scalar2=None, op0=OP.mult), [mx], [ngm])
            k.op(DVE, lambda: dve.memset(sm.ap, 0.0), [], [sm])
            for hl in range(4):
                so = ps4[hl // 2].ap[:, (hl % 2) * 256:(hl % 2 + 1) * 256]
                k.op(ACT, lambda: act.activation(pex.ap[:, hl, :], so, AF.Exp, bias=ngm.ap[:, hl:hl + 1], accum_out=sm.ap[:, hl:hl + 1]), [ps4[hl // 2], ngm], [pex, sm])
            k.op(DVE, lambda: dve.tensor_tensor(out=es.ap, in0=snk, in1=ngm.ap, op=OP.add), [par, ngm], [es])
            k.op(ACT, lambda: act.activation(es.ap, es.ap, AF.Exp), [es], [es])

        def stageB(bi):
            g, j = blks[bi]
            pex, pn = pexs[bi % 2], pns[bi % 2]
            mx, ngm, sm, es = smalls[bi % 2]
            k.op(DVE, lambda: dve.tensor_tensor(out=es.ap, in0=es.ap, in1=sm.ap, op=OP.add), [es, sm], [es])
            k.op(DVE, lambda: dve.reciprocal(es.ap, es.ap), [es], [es])
            k.op(DVE, lambda: dve.tensor_tensor(out=pn.ap, in0=pex.ap, in1=es.ap.rearrange("p (h o) -> p h o", o=1).to_broadcast([128, 4, 256]), op=OP.mult), [pex, es], [pn])
            for hl in range(4):
                for hf in range(2):
                    i8 = hl * 2 + hf
                    k.op(PE, lambda: pe.transpose(ppt.ap[:, i8 * 128:(i8 + 1) * 128], pn.ap[:, hl, hf * 128:(hf + 1) * 128], ident), [pn, cb], [ppt], inc=(i8 == 7))
            k.op(ACT, lambda: act.activation(pT.ap.rearrange("p a q -> p (a q)"), ppt.ap, AF.Copy), [ppt], [pT])
            for pr in range(2):
                n4 = 0
                for hl in (2 * pr, 2 * pr + 1):
                    off = 64 if hl % 2 == 0 else 0
                    for hf in range(2):
                        k.op(PE, lambda: pe.matmul(pob[pr].ap, vpad[g].ap[:, j + hf, off:off + 128], pT.ap[:, hl * 2 + hf, :], start=(n4 == 0), stop=(n4 == 3)),
                             [vpad[g], pT], [pob[pr]], inc=(n4 == 3))
                        n4 += 1
                ct = cat[4 + g * 2 + pr]
                k.op(ACT, lambda: act.activation(ct.ap[:, j * 128:(j + 1) * 128], pob[pr].ap, AF.Copy), [pob[pr]], [ct])

        stageA(0)
        for bi in range(4):
            if bi + 1 < 4:
                stageA(bi + 1)
            stageB(bi)
        py = [pt(2, 0, TW), pt(3, 0, TW)]
        for n in range(8):
            po = py[n % 2]
            for kc in range(8):
                k.op(PE, lambda: pe.matmul(po.ap, w3(uWo[n])[:, kc, :], cat[kc].ap, start=(kc == 0), stop=(kc == 7)), [uWo[n], cat[kc]], [po], inc=(kc == 7))
            x_t = xa[n][tt]
            k.op(DVE, lambda: dve.scalar_tensor_tensor(out=x_t.ap, in0=po.ap, scalar=cc(CO_G1, n), in1=x_t.ap, op0=OP.mult, op1=OP.add), [po, coef, x_t], [x_t])

    attT2 = [[k.tile("attT2_%d_%d" % (h, j), [128, 128], BF16) for j in range(2)] for h in range(4)]
    assert k.sb_off - 16384 <= 210784, k.sb_off - 16384
    print("sbuf bytes used", k.sb_off - 16384, "of 212863")

    def odd_mixer_tile(o, tt, units, b):
        a_reset()
        hf = [a_tile("hf", [128, TW], F32) for _ in range(2)]
        Pb = [a_tile("Pb", [128, 16 + TW], F32) for _ in range(2)]
        wsm = [a_tile("wsm", [128, TW], F32) for _ in range(2)]
        pl = [a_tile("pl", [128, TW], F32) for _ in range(2)]
        fx = a_tile("fx", [128, 16], F32)
        onesb = cf.ap[:, CF_CMASK + 1:CF_CMASK + 2].to_broadcast([128, TW])
        def st1(kc):
            hh = hf[kc % 2]
            k.op(ACT, lambda: act.activation(hh.ap, xa[kc][tt].ap, AF.Identity, scale=cc(CO_S1, kc), bias=modT.ap[:, lcur["l"], kc, b:b + 1]),
                 [xa[kc][tt], coef, modT], [hh])

        def st2(kc):
            gi = kc // 2
            w = (2, 4, 8, 16)[gi]
            hh, P, wv, pp = hf[kc % 2], Pb[kc % 2], wsm[kc % 2], pl[kc % 2]
            if tt == 0:
                k.op(DVE, lambda: dve.memset(P.ap[:, 0:16], 0.0), [], [P])
            else:
                k.op(DVE, lambda: dve.tensor_copy(P.ap[:, 0:16], phalo.ap[:, kc, :]), [phalo], [P])
            init = 0.0 if tt == 0 else phalo.ap[:, kc, 15:16]
            k.op(DVE, lambda: dve.tensor_tensor_scan(P.ap[:, 16:16 + TW], onesb, hh.ap, init, OP.mult, OP.add), [cf, hh, phalo], [P])
            k.op(DVE, lambda: dve.tensor_copy(phalo.ap[:, kc, :], P.ap[:, TW:TW + 16]), [P], [phalo])
            k.op(POOL, lambda: pool.tensor_tensor(out=wv.ap, in0=P.ap[:, 16:16 + TW], in1=P.ap[:, 16 - w:16 - w + TW], op=OP.subtract), [P], [wv])
            k.op(DVE, lambda: dve.scalar_tensor_tensor(out=pp.ap, in0=wv.ap, scalar=1.0 / w, in1=hh.ap, op0=OP.mult, op1=OP.subtract), [wv, hh], [pp])
            if tt == 0:
                k.op(DVE, lambda: dve.tensor_tensor(out=fx.ap, in0=wv.ap[:, 0:16], in1=cf.ap[:, CF_PINV + gi * 16:CF_PINV + (gi + 1) * 16], op=OP.mult), [wv, cf], [fx])
                k.op(DVE, lambda: dve.tensor_tensor(out=pp.ap[:, 0:16], in0=fx.ap, in1=hh.ap[:, 0:16], op=OP.subtract), [fx, hh], [pp])

        def st3(kc):
            pp = pl[kc % 2]
            k.op(ACT, lambda: act.activation(h1[kc].ap, pp.ap, AF.Copy), [pp], [h1[kc]])

        st1(0)
        for kc in range(8):
            st2(kc)
            if kc + 1 < 8:
                st1(kc + 1)
            st3(kc)
        py = [pt(2, 0, TW), pt(3, 0, TW)]
        for n in range(8):
            gi, mm_ = n // 2, n % 2
            u = units[gi // 2]
            gl = gi % 2
            po = py[n % 2]
            for kc in range(2):
                k.op(PE, lambda: pe.matmul(po.ap, w3(u)[:, (gl * 2 + mm_) * 2 + kc, :], h1[gi * 2 + kc].ap, start=(kc == 0), stop=(kc == 1)), [u, h1[gi * 2 + kc]], [po], inc=(kc == 1))
            x_t = xa[n][tt]
            k.op(DVE, lambda: dve.scalar_tensor_tensor(out=x_t.ap, in0=po.ap, scalar=cc(CO_G1, n), in1=x_t.ap, op0=OP.mult, op1=OP.add), [po, coef, x_t], [x_t])

    def ffn_phase(l, b, last):
        for gidx, grp in enumerate(FF_GROUPS):
            ng = len(grp)
            units = ws_get(3 * ng, ["upu", "upv", "down"] * ng)
            lastg = gidx == len(FF_GROUPS) - 1
            a_reset()
            if lastg:
                ar["off"] = arena0 + 9216
            DP = 2 if lastg else 3
            ub = [[a_tile("ub", [128, 2 + TW], F32) for _ in range(2)] for _ in range(ng)]
            acc = [a_tile("acc", [128, TW], F32) for _ in range(DP)]
            gt = [a_tile("gt", [128, TW], BF16) for _ in range(DP)]
            puv = [(pt(4 + d, 0, TW), pt(4 + d, TW, 2 * TW)) for d in range(DP)]
            py = [pt(m // 2, (m % 2) * TW, (m % 2 + 1) * TW) for m in range(8)]
            items = [(tt, jl) for tt in range(NT) for jl in range(ng)]

            def up(i):
                tt, jl = items[i]
                uu, uv = units[3 * jl], units[3 * jl + 1]
                pu, pvv = puv[i % DP]
                for (u, po) in ((uu, pu), (uv, pvv)):
                    for kc in range(8):
                        k.op(PE, lambda: pe.matmul(po.ap, w3(u)[:, kc, :], Hh[kc][tt].ap, start=(kc == 0), stop=(kc == 7)), [u, Hh[kc][tt]], [po], inc=(kc == 7))

            def mid(i, stage):
                tt, jl = items[i]
                j = grp[jl]
                pu, pvv = puv[i % DP]
                u_c = ub[jl][tt % 2]
                u_n = ub[jl][(tt + 1) % 2]
                a_ = acc[i % DP]
                cw = lambda t3: par.ap[:, P_CW + (l * 3 + t3) * 22 + j:P_CW + (l * 3 + t3) * 22 + j + 1]
                cbias = par.ap[:, P_CB + l * 22 + j:P_CB + l * 22 + j + 1]
                if stage == 1:
                    if tt == 0:
                        k.op(DVE, lambda: dve.memset(u_c.ap[:, 0:2], 0.0), [], [u_c])
                    k.op(ACT, lambda: act.activation(u_c.ap[:, 2:2 + TW], pu.ap, AF.Copy), [pu], [u_c])
                    k.op(ACT, lambda: act.activation(a_.ap, pu.ap, AF.Identity, scale=cw(2), bias=cbias), [pu, par], [a_])
                elif stage == 2:
                    k.op(DVE, lambda: dve.scalar_tensor_tensor(out=a_.ap, in0=u_c.ap[:, 0:TW], scalar=cw(0), in1=a_.ap, op0=OP.mult, op1=OP.add), [u_c, par, a_], [a_])
                    k.op(DVE, lambda: dve.scalar_tensor_tensor(out=a_.ap, in0=u_c.ap[:, 1:1 + TW], scalar=cw(1), in1=a_.ap, op0=OP.mult, op1=OP.add), [u_c, par, a_], [a_])
                else:
                    k.op(ACT, lambda: act.activation(u_n.ap[:, 0:2], u_c.ap[:, TW:TW + 2], AF.Copy), [u_c], [u_n])

            def midB(i):
                pu, pvv = puv[i % DP]
                a_ = acc[i % DP]
                k.op(ACT, lambda: act.activation(a_.ap, a_.ap, AF.Silu), [a_], [a_])
                k.op(DVE, lambda: dve.tensor_tensor(out=gt[i % DP].ap, in0=a_.ap, in1=pvv.ap, op=OP.mult), [a_, pvv], [gt[i % DP]])

            def down(i):
                tt, jl = items[i]
                ud = units[3 * jl + 2]
                for m in range(8):
                    k.op(PE, lambda: pe.matmul(py[m].ap, ud.ap[:, m * 128:(m + 1) * 128], gt[i % DP].ap, start=(jl == 0 and m % 2 == 0), stop=(jl == ng - 1 and m % 2 == 1)), [ud, gt[i % DP]], [py[m]], inc=(m == 7 or (jl == ng - 1 and m % 2 == 1)))
                if jl == ng - 1:
                    def evac(tt=tt):
                        for m in range(8):
                            x_t = xa[m][tt]
                            g2c = modT.ap[:, l, 40 + m, b:b + 1]
                            if m < 4 and OPT_EVAC:
                                tm = ytmp[m]
                                k.op(ACT, lambda: act.activation(tm.ap, py[m].ap, AF.Identity, scale=g2c, bias=cfc(CF_ZERO)), [py[m], modT, cf], [tm])
                                pend.append((x_t, tm))
                            else:
                                k.op(DVE, lambda: dve.scalar_tensor_tensor(out=x_t.ap, in0=py[m].ap, scalar=g2c, in1=x_t.ap, op0=OP.mult, op1=OP.add),
                                     [py[m], modT, x_t], [x_t])
                    if lastg:
                        evac()
                        flush_pend()
                    else:
                        evq.append(evac)
                    if lastg:
                        save = ar["off"]
                        layer_norm(tt, CO_C1, CO_C0, hb=None, out_dma=(outb["b"] if last else None))
                        ar["off"] = save

            ytmp = [a_tile("ytmp", [128, TW], F32) for _ in range(4)]
            pend = []
            evq = []

            def flush_pend():
                for (x_t, tm) in pend:
                    if OPT_POOLADD:
                        k.op(POOL, lambda: pool.tensor_tensor(out=x_t.ap, in0=x_t.ap, in1=tm.ap, op=OP.add), [x_t, tm], [x_t])
                    else:
                        k.op(DVE, lambda: dve.tensor_tensor(out=x_t.ap, in0=x_t.ap, in1=tm.ap, op=OP.add), [x_t, tm], [x_t])
                del pend[:]

            n_it = len(items)
            for d in range(DP - 1):
                if d < n_it:
                    up(d)
            HOIST = DP >= 3
            if HOIST:
                mid(0, 1)
            LAG = 2 if HOIST else 1
            for i in range(n_it + LAG):
                if i < n_it:
                    if not HOIST:
                        mid(i, 1)
                    mid(i, 2)
                    if HOIST and i + 1 < n_it:
                        mid(i + 1, 1)
                    mid(i, 3)
                    midB(i)
                for f in evq:
                    f()
                del evq[:]
                flush_pend()
                if i + DP - 1 < n_it:
                    up(i + DP - 1)
                if i >= LAG:
                    down(i - LAG)
            for f in evq:
                f()
            del evq[:]
            flush_pend()
            ws_done(3 * ng)
            if b == 0 and l + 1 < n_layers:
                compute_mod(l + 1, gidx * 8, gidx * 8 + 8)

    outb = {"b": 0}
    lcur = {"l": 0}

    for s in range(n_seq):
        outb["b"] = s
        for tt in range(NT):
            k.dma(SP, [xa[kc][tt] for kc in range(8)], xa_h[:, :, tt * TW:(tt + 1) * TW], x_d[s, :, :, tt * TW:(tt + 1) * TW], reads=[], semt=xa[0][tt])
        for tt in range(NT):
            for kc in range(8):
                k.op(ACT if kc % 2 else DVE,
                     (lambda kc=kc, tt=tt: act.activation(xa[kc][tt].ap, xa[kc][tt].ap, AF.Identity, scale=ALPHA)) if kc % 2 else
                     (lambda kc=kc, tt=tt: dve.tensor_scalar(out=xa[kc][tt].ap, in0=xa[kc][tt].ap, scalar1=ALPHA, scalar2=None, op0=OP.mult)),
                     [xa[kc][tt]], [xa[kc][tt]])
        for l in range(n_layers):
            lcur["l"] = l
            last = l == n_layers - 1
            compute_coefs(l, s, last)
            if l % 2 == 0:
                e = l // 2
                for hd in range(4):
                    k.op(DVE, lambda: dve.memset(S32[hd].ap, 0.0), [], [S32[hd]])
                    k.op(DVE, lambda: dve.memset(Sbf[hd].ap, 0.0), [], [Sbf[hd]])
                units = ws_get(31)
                even_prologue(0, units, s)
                for tt in range(NT):
                    even_mixer_tile(e, tt, units, s)
                    if tt + 1 < NT:
                        even_prologue(tt + 1, units, s)
                    layer_norm(tt, CO_A1, CO_A0, hb=(CO_B1, CO_B0))
                ws_done(31)
            else:
                o = l // 2
                units = ws_get(2, ["pool", "pool"])
                for tt in range(NT):
                    odd_mixer_tile(o, tt, units, s)
                    layer_norm(tt, CO_A1, CO_A0, hb=(CO_B1, CO_B0))
                ws_done(2)
            ffn_phase(l, s, last)
    for (sem, val) in out_waits:
        nc.sync.wait_ge(sem, val)
    assert ws["cur"] == U, (ws["cur"], U)
    return nc, plan


_CACHE = {}


def kernel(**inputs):
    inp = {kk: np.asarray(v) for kk, v in inputs.items()}
    if "prog" not in _CACHE:
        _CACHE["prog"] = build_program()
    nc, plan = _CACHE["prog"]
    wstream = materialize_stream(plan, inp)
    cf, cb = make_consts()
    x = inp["x"].astype(np.float32)
    pos = inp["positions"].astype(np.int32)
    in_maps = []
    for c in range(NCORES):
        xc = x[2 * c:2 * c + 2]
        xfm = np.ascontiguousarray(xc.transpose(0, 2, 1).reshape(2, 8, 128, T).transpose(0, 2, 1, 3))
        in_maps.append({"xfm": xfm, "pos": np.ascontiguousarray(pos[2 * c:2 * c + 2]), "cf": cf, "cb": cb,
                        "par": make_params(inp, c), "wstream": wstream})
    res = run_bass_kernel_spmd(nc, in_maps, core_ids=list(range(NCORES)))
    out = np.zeros((16, T, D), np.float32)
    for c in range(NCORES):
        o = np.asarray(res.results[c]["ofm"])
        out[2 * c:2 * c + 2] = o.transpose(0, 2, 1, 3).reshape(2, D, T).transpose(0, 2, 1)
    return out
```

```python
import numpy as np
import concourse.bass as bass
import concourse.mybir as mybir
from concourse.bass_utils import run_bass_kernel_spmd

F32 = mybir.dt.float32
BF16 = mybir.dt.bfloat16
I32 = mybir.dt.int32
AF = mybir.ActivationFunctionType
OP = mybir.AluOpType
AX = mybir.AxisListType

D = 1024
T = 2048
DEPTH = 4
TW = 256
NT = T // TW
NFF = 22
ALPHA = (2.0 * DEPTH) ** 0.25
LN_EPS = 1e-5
RMS_EPS = 1e-6
R = 31
FF_GROUPS = [list(range(0, 4)), list(range(4, 8)), list(range(8, 12)), list(range(12, 16)),
             list(range(16, 20)), [20, 21]]
NCORES = 8
import os
OPT_LNPOOL = int(os.environ.get('K_LNPOOL', '1'))
OPT_EVAC = int(os.environ.get('K_EVAC', '1'))
OPT_POOLADD = int(os.environ.get('K_POOLADD', '1'))

CF_PERM, CF_MBD, CF_CMASK, CF_INVF, CF_SGN, CF_PI, CF_HPI, CF_2PI, CF_PINV, NCF = 0, 128, 256, 512, 513, 514, 515, 516, 520, 585
CF_INVF2, CF_PHS, CF_PHC, CF_ZERO = 517, 518, 519, 584
CW1 = 6.28125
CW2 = float(2.0 * np.pi - 6.28125)
CB_ID, CB_ONES, CB_MS, CB_MS0, NCB = 0, 128, 256, 512, 768
P_C, P_ADAB, P_LNG, P_LNB, P_LBR, P_NG, P_SINK, P_PSC, P_CW, P_CB = 0, 16, 208, 272, 336, 344, 346, 362, 378, 642
NPAR = 730


def stream_plan(n_layers=DEPTH, n_seq=2):
    plan = [("ada", 0, n) for n in range(48)]
    for s in range(n_seq):
        for l in range(n_layers):
            if l % 2 == 0:
                e = l // 2
                for h in range(4):
                    for base in (0, 512, 1024, 1536):
                        plan.append(("win", e, base + h * 128))
                for c in range(4):
                    plan.append(("win", e, 2048 + c * 128))
                for g in range(2):
                    plan.append(("bkdup", e, g))
                plan.append(("bv", e, 0))
                for n in range(8):
                    plan.append(("wout", e, n))
            else:
                o = l // 2
                plan.append(("pool", o, 0))
                plan.append(("pool", o, 1))
            for gidx, grp in enumerate(FF_GROUPS):
                for j in grp:
                    plan.append(("upu", l, j))
                    plan.append(("upv", l, j))
                    plan.append(("down", l, j))
                if s == 0 and l + 1 < n_layers:
                    plan += [("ada", l + 1, n) for n in range(gidx * 8, gidx * 8 + 8)]
    return plan


def _k1024(w):
    return np.ascontiguousarray(w.reshape(8, 128, 128).transpose(1, 0, 2)).reshape(128, 1024)


def materialize_stream(plan, inp):
    out = np.zeros((len(plan), 128, 1024), np.float32)
    cache = {}
    for i, u in enumerate(plan):
        if u in cache:
            out[i] = out[cache[u]]
            continue
        cache[u] = i
        kind, a, b = u
        if kind == "ada":
            out[i] = _k1024(inp["ada_w"][a][:, b * 128:(b + 1) * 128])
        elif kind == "win":
            out[i] = _k1024(inp["ab_w_in"][a][:, b:b + 128])
        elif kind == "bkdup":
            w = inp["ab_w_in"][a][:, 2560 + b * 64:2560 + (b + 1) * 64]
            out[i] = _k1024(np.concatenate([w, w], axis=1))
        elif kind == "bv":
            out[i] = _k1024(inp["ab_w_in"][a][:, 2688:2816])
        elif kind == "wout":
            out[i] = _k1024(inp["ab_w_out"][a][:, b * 128:(b + 1) * 128])
        elif kind == "pool":
            blk = np.zeros((128, 8, 128), np.float32)
            for gl in range(2):
                for m in range(2):
                    for kc in range(2):
                        blk[:, (gl * 2 + m) * 2 + kc, :] = inp["pool_w"][a][b * 2 + gl][kc * 128:(kc + 1) * 128, m * 128:(m + 1) * 128]
            out[i] = blk.reshape(128, 1024)
        elif kind == "upu":
            out[i] = _k1024(inp["ffn_w_up"][a][:, b * 128:(b + 1) * 128])
        elif kind == "upv":
            out[i] = _k1024(inp["ffn_w_up"][a][:, 2816 + b * 128:2816 + (b + 1) * 128])
        elif kind == "down":
            out[i] = inp["ffn_w_down"][a][b * 128:(b + 1) * 128, :]
    return out


def make_consts():
    cf = np.zeros((128, NCF), np.float32)
    cb = np.zeros((128, NCB), np.float32)
    p = np.arange(128)
    d = p % 64
    perm = np.zeros((128, 128), np.float32)
    for m in range(128):
        dm = m % 64
        if dm < 8:
            perm[m + 8, m] = 1.0
        elif dm < 16:
            perm[m - 8, m] = 1.0
    cf[:, CF_PERM:CF_PERM + 128] = perm
    s = p[:, None]
    t = p[None, :]
    cf[:, CF_MBD:CF_MBD + 128] = ((s // 32 == t // 32) & (s <= t)).astype(np.float32)
    cm = np.ones(256, np.float32)
    cm[::32] = 0.0
    cf[:, CF_CMASK:CF_CMASK + 256] = cm[None, :]
    invf = 500000.0 ** (-np.arange(0, 16, 2, dtype=np.float32) / 16.0)
    iv = np.zeros(128, np.float32)
    sg = np.zeros(128, np.float32)
    for r in range(128):
        dd = r % 64
        if dd < 8:
            iv[r] = invf[dd]
            sg[r] = -1.0
        elif dd < 16:
            iv[r] = invf[dd - 8]
            sg[r] = 1.0
    cf[:, CF_INVF] = iv
    cf[:, CF_SGN] = sg
    cf[:, CF_PI] = np.float32(np.pi)
    cf[:, CF_HPI] = np.float32(np.pi / 2)
    cf[:, CF_2PI] = np.float32(2 * np.pi)
    cf[:, CF_INVF2] = (iv.astype(np.float64) / (2 * np.pi)).astype(np.float32)
    cf[:, CF_PHS] = 0.5
    cf[:, CF_PHC] = 0.75
    for gi, w in enumerate((2, 4, 8, 16)):
        cnt = np.minimum(np.arange(16) + 1, w).astype(np.float32)
        cf[:, CF_PINV + gi * 16:CF_PINV + (gi + 1) * 16] = (1.0 / cnt)[None, :]
    cb[:, CB_ID:CB_ID + 128] = np.eye(128, dtype=np.float32)
    cb[:, CB_ONES:CB_ONES + 128] = 1.0
    q = p[:, None]
    ki = np.arange(256)[None, :]
    rel = q + 128 - ki
    allowed = (rel >= 0) & (rel < 128)
    cb[:, CB_MS:CB_MS + 256] = np.where(allowed, 0.0, -30000.0)
    cb[:, CB_MS0:CB_MS0 + 256] = np.where(allowed & (ki >= 128), 0.0, -30000.0)
    return cf, cb


def make_params(inp, core):
    pr = np.zeros((128, NPAR), np.float32)

    def fm(v):
        v = np.asarray(v)
        n = v.shape[-1] // 128
        v = v.reshape(v.shape[:-1] + (n, 128))
        return np.moveaxis(v, -1, 0)

    pr[:, P_C:P_C + 16] = fm(inp["c"][2 * core:2 * core + 2]).transpose(0, 2, 1).reshape(128, 16)
    pr[:, P_ADAB:P_ADAB + 192] = fm(inp["ada_b"]).reshape(128, 192)
    pr[:, P_LNG:P_LNG + 64] = fm(inp["ln_g"]).reshape(128, 64)
    pr[:, P_LNB:P_LNB + 64] = fm(inp["ln_b"]).reshape(128, 64)
    pr[:, P_LBR:P_LBR + 8] = fm(inp["hgrn_lb_raw"]).reshape(128, 8)
    pr[:, P_NG:P_NG + 2] = fm(inp["hgrn_norm_g"]).reshape(128, 2)
    pr[:, P_SINK:P_SINK + 16] = np.asarray(inp["attn_sinks"]).reshape(1, 16)
    pr[:, P_PSC:P_PSC + 16] = fm(inp["pool_scale"]).reshape(128, 16)
    pr[:, P_CW:P_CW + 264] = fm(inp["ffn_conv_w"]).reshape(128, 264)
    pr[:, P_CB:P_CB + 88] = fm(inp["ffn_conv_b"]).reshape(128, 88)
    return pr


class Tk:
    __slots__ = ("ap", "buf", "lo", "hi", "w", "r", "sem", "cnt", "ov", "ovv")

    def __init__(self, ap, buf, lo=0, hi=1):
        self.ap, self.buf, self.lo, self.hi = ap, buf, lo, hi
        self.w = None
        self.r = {}
        self.sem = None
        self.cnt = 0
        self.ov = None
        self.ovv = -1


class Eng:
    def __init__(self, k, eng, name, skip_own=False):
        self.k, self.eng, self.name, self.skip_own = k, eng, name, skip_own
        self.seen = {}
        self.pend = []
        self.sems = set()
        self.newsem()

    def newsem(self):
        self.sem = self.k.nc.alloc_semaphore(name="%s_s%d" % (self.name, self.k.nsem))
        self.k.nsem += 1
        self.cnt = 0
        self.sems.add(self.sem.num)


class Prog:
    def __init__(self, nc):
        self.nc = nc
        self.nsem = 0
        self.bufs = {}
        self.nbuf = 0
        self.sb_off = 16384
        self.PE = Eng(self, nc.tensor, "pe", skip_own=True)
        self.ACT = Eng(self, nc.scalar, "act")
        self.DVE = Eng(self, nc.vector, "dve")
        self.POOL = Eng(self, nc.gpsimd, "pool")
        self.SP = Eng(self, nc.sync, "sp")

    def sb(self, name, shape, dt, off=None):
        esz = 4 if dt in (F32, I32) else 2
        nbytes = int(np.prod(shape[1:])) * esz
        if off is None:
            off = self.sb_off
            self.sb_off += (nbytes + 31) // 32 * 32
        h = self.nc.alloc_sbuf_tensor_at(name, list(shape), dt, offset=off)
        return h, off, nbytes

    def tile(self, name, shape, dt):
        h, off, nb = self.sb(name, shape, dt)
        return self.reg(Tk(h[:], "sb_" + name, off, off + nb))

    def tile_at(self, name, shape, dt, off):
        h, off, nb = self.sb(name, shape, dt, off)
        return self.reg(Tk(h[:], "arena", off, off + nb))

    def reg(self, t):
        self.bufs.setdefault(t.buf, []).append(t)
        self.ver = getattr(self, "ver", 0) + 1
        return t

    def sub(self, parent_ap, buf, lo, hi):
        return self.reg(Tk(parent_ap, buf, lo, hi))

    def over(self, t):
        if t.ovv != self.ver:
            t.ov = [t2 for t2 in self.bufs[t.buf] if t2.lo < t.hi and t.lo < t2.hi]
            t.ovv = self.ver
        return t.ov

    def _need(self, reads, writes):
        need = {}

        def add(tok):
            if tok is None:
                return
            sem, val = tok
            if need.get(sem.num, (None, 0))[1] < val:
                need[sem.num] = (sem, val)
        for t in reads:
            psum = t.buf.startswith("pb")
            for t2 in self.over(t):
                add(t2.w)
                if psum:
                    for tok in t2.r.values():
                        add(tok)
        for t in writes:
            for t2 in self.over(t):
                add(t2.w)
                for tok in t2.r.values():
                    add(tok)
        return need

    def _wait(self, e, need):
        for key, (sem, val) in need.items():
            if e.skip_own and key in e.sems:
                continue
            if e.seen.get(key, 0) < val:
                e.eng.wait_ge(sem, val)
                e.seen[key] = val

    def _commit(self, tok, reads, writes):
        for t in writes:
            t.w = tok
            t.r = {}
        for t in reads:
            t.r[tok[0].num] = tok

    def op(self, e, fn, reads=(), writes=(), inc=True):
        self._wait(e, self._need(reads, writes))
        ins = fn()
        if not inc:
            e.pend.append((reads, writes))
            return
        if e.cnt >= 40000:
            e.newsem()
        e.cnt += 1
        ins.then_inc(e.sem, 1)
        tok = (e.sem, e.cnt)
        for (rs, ws) in e.pend:
            self._commit(tok, rs, ws)
        e.pend = []
        self._commit(tok, reads, writes)

    def dma(self, e, out_t, out_ap, in_ap, reads=(), semt=None):
        st = semt if semt is not None else (out_t[0] if isinstance(out_t, (list, tuple)) else out_t)
        if st.sem is None:
            st.sem = self.nc.alloc_semaphore(name="d%d" % self.nsem)
            self.nsem += 1
        writes = list(out_t) if isinstance(out_t, (list, tuple)) else ([out_t] if out_t is not None else [])
        if semt is None:
            st = writes[0]
            if st.sem is None:
                st.sem = self.nc.alloc_semaphore(name="d%d" % self.nsem)
                self.nsem += 1
        self._wait(e, self._need(reads, writes))
        e.eng.dma_start(out=out_ap, in_=in_ap).then_inc(st.sem, 16)
        st.cnt += 16
        self._commit((st.sem, st.cnt), reads, writes)


def build_program(n_layers=DEPTH, n_seq=2, nunits=None):
    nc = bass.Bass("TRN2", target_bir_lowering=False)
    plan = stream_plan(n_layers, n_seq)
    U = len(plan)
    x_d = nc.dram_tensor("xfm", [2, 128, 8, T], F32, kind="ExternalInput").ap()
    pos_d = nc.dram_tensor("pos", [2, T], I32, kind="ExternalInput").ap()
    cf_d = nc.dram_tensor("cf", [128, NCF], F32, kind="ExternalInput").ap()
    cb_d = nc.dram_tensor("cb", [128, NCB], F32, kind="ExternalInput").ap()
    par_d = nc.dram_tensor("par", [128, NPAR], F32, kind="ExternalInput").ap()
    ws_d = nc.dram_tensor("wstream", [U, 128, 1024], F32, kind="ExternalInput").ap()
    out_d = nc.dram_tensor("ofm", [2, 128, 8, T], F32, kind="ExternalOutput").ap()

    k = Prog(nc)
    PE, ACT, DVE, POOL, SP = k.PE, k.ACT, k.DVE, k.POOL, k.SP
    pe, act, dve, pool = nc.tensor, nc.scalar, nc.vector, nc.gpsimd
    LNB = {"b": 6}

    xa_h, xa_off, _ = k.sb("xa", [128, 8, T], F32)
    H_h, H_off, _ = k.sb("H", [128, 8, T], BF16)
    xa = [[k.reg(Tk(xa_h[:, kc, tt * TW:(tt + 1) * TW], "xa%d_%d" % (kc, tt))) for tt in range(NT)] for kc in range(8)]
    Hh = [[k.reg(Tk(H_h[:, kc, tt * TW:(tt + 1) * TW], "H%d_%d" % (kc, tt))) for tt in range(NT)] for kc in range(8)]
    ring = [k.tile("ring%d" % i, [128, 1024], BF16) for i in range(R)]
    cf = k.tile("cf", [128, NCF], F32)
    cb = k.tile("cb", [128, NCB], BF16)
    par = k.tile("par", [128, NPAR], F32)
    modT = k.tile("modT", [128, DEPTH, 48, 2], F32)
    coef = k.tile("coef", [128, 8, 8], F32)
    ctmp = k.tile("ctmp", [128, 8], F32)
    cact = k.tile("cact", [128, 8, 2], BF16)
    lbt = k.tile("lbt", [128, 2, 2, 4], F32)
    ngs = k.tile("ngs", [128, 2], F32)
    h1 = [k.tile("h1_%d" % kc, [128, TW], BF16) for kc in range(8)]
    cat = [k.tile("cat_%d" % kc, [128, TW], BF16) for kc in range(8)]
    S32 = [k.tile("S32_%d" % h, [128, 128], F32) for h in range(4)]
    Sbf = [k.tile("Sbf_%d" % h, [128, 128], BF16) for h in range(4)]
    kTc = [k.tile("kTc_%d" % g, [128, 128 + TW], BF16) for g in range(2)]
    vpad = [k.tile("vpad_%d" % g, [128, 3, 192], BF16) for g in range(2)]
    phalo = k.tile("phalo", [128, 8, 16], F32)
    arena0 = k.sb_off
    ARENA = 21 * 1024
    k.sb_off += ARENA
    assert k.sb_off <= 16384 + 212800, k.sb_off
    ar = {"off": arena0}
    acache = {}
    pcache = {}

    def a_reset():
        ar["off"] = arena0

    def a_tile(name, shape, dt):
        esz = 4 if dt in (F32, I32) else 2
        nb = int(np.prod(shape[1:])) * esz
        off = ar["off"]
        ar["off"] += (nb + 31) // 32 * 32
        assert ar["off"] <= arena0 + ARENA, (name, ar["off"] - arena0)
        key = (off, tuple(shape), str(dt))
        if key not in acache:
            acache[key] = k.tile_at("ar_%s_%d" % (name, k.nbuf_inc()), shape, dt, off)
        return acache[key]

    def nbuf_inc():
        k.nbuf += 1
        return k.nbuf
    k.nbuf_inc = nbuf_inc

    Bk = []
    for i in range(7):
        ph = nc.alloc_psum_tensor("pb%d" % i, [128, 512], F32)
        Bk.append(ph)
    B7h = nc.alloc_psum_tensor("pb7", [128, 1024], BF16)

    def pt(i, lo, hi):
        key = (i, lo, hi)
        if key not in pcache:
            if i == 7:
                pcache[key] = k.sub(B7h[:, lo:hi], "pb7", 0, 1)
            else:
                pcache[key] = k.sub(Bk[i][:, lo:hi], "pb%d" % i, 0, 1)
        return pcache[key]

    ws = {"cur": 0, "emitted": 0, "done": 0}

    def emit_dma(u):
        slot = ring[u % R]
        k.dma(POOL, slot, slot.ap, ws_d[u])

    def ws_prime():
        while ws["emitted"] < min(R, U):
            emit_dma(ws["emitted"])
            ws["emitted"] += 1

    def ws_get(n, kinds=None):
        res = []
        for i in range(n):
            u = ws["cur"]
            assert u < ws["emitted"], ("ring deadlock", u)
            if kinds is not None:
                assert plan[u][0] == kinds[i], (plan[u], kinds[i])
            res.append(ring[u % R])
            ws["cur"] += 1
        return res

    def ws_done(n):
        for i in range(n):
            ws["done"] += 1
            if ws["emitted"] < U:
                assert ws["emitted"] - ws["done"] < R
                emit_dma(ws["emitted"])
                ws["emitted"] += 1

    def w3(slot):
        return slot.ap.rearrange("p (k c) -> p k c", c=128)

    k.dma(SP, cf, cf.ap, cf_d)
    k.dma(SP, par, par.ap, par_d)
    k.dma(POOL, cb, cb.ap, cb_d)
    ws_prime()
    ident = cb.ap[:, CB_ID:CB_ID + 128]
    ones = cb.ap[:, CB_ONES:CB_ONES + 128]
    pc = lambda c0, n=1: par.ap[:, c0:c0 + n]
    cfc = lambda c0: cf.ap[:, c0:c0 + 1]

    k.op(ACT, lambda: act.activation(cact.ap.rearrange("p k b -> p (k b)"), par.ap[:, P_C:P_C + 16], AF.Silu), [par], [cact])
    lb2 = lbt.ap
    k.op(DVE, lambda: dve.memset(lb2[:, 0, 0, :], 0.0), [], [lbt])
    k.op(DVE, lambda: dve.tensor_tensor(out=lb2[:, 1, 1, :], in0=par.ap[:, P_LBR + 4:P_LBR + 8], in1=par.ap[:, P_LBR:P_LBR + 4], op=OP.subtract), [par], [lbt])
    k.op(ACT, lambda: act.activation(lb2[:, 1, 0, :], lb2[:, 1, 1, :], AF.Sigmoid), [lbt], [lbt])
    k.op(DVE, lambda: dve.memset(lb2[:, 0, 1, :], 1.0), [], [lbt])
    k.op(DVE, lambda: dve.tensor_scalar(out=lb2[:, 1, 1, :], in0=lb2[:, 1, 0, :], scalar1=-1.0, scalar2=1.0, op0=OP.mult, op1=OP.add), [lbt], [lbt])
    k.op(DVE, lambda: dve.tensor_scalar(out=lb2[:, :, 1, :], in0=lb2[:, :, 1, :], scalar1=0.5, scalar2=None, op0=OP.mult), [lbt], [lbt])
    k.op(DVE, lambda: dve.tensor_tensor(out=lb2[:, :, 0, :], in0=lb2[:, :, 0, :], in1=lb2[:, :, 1, :], op=OP.add), [lbt], [lbt])
    k.op(DVE, lambda: dve.tensor_scalar(out=ngs.ap, in0=par.ap[:, P_NG:P_NG + 2], scalar1=float(np.sqrt(128.0)), scalar2=None, op0=OP.mult), [par], [ngs])

    def compute_mod(l, n0=0, n1=48):
        nn = n1 - n0
        mps = pt(6, 0, 96)
        for n in range(n0, n1):
            (u,) = ws_get(1, ["ada"])
            assert plan[ws["cur"] - 1] == ("ada", l, n)
            c0 = (n - n0) * 2
            for kc in range(8):
                k.op(PE, lambda kc=kc: pe.matmul(mps.ap[:, c0:c0 + 2], w3(u)[:, kc, :], cact.ap[:, kc, :], start=(kc == 0), stop=(kc == 7)),
                     [u, cact], [mps], inc=(kc == 7))
            ws_done(1)
        k.op(DVE, lambda: dve.tensor_tensor(out=modT.ap[:, l, n0:n1, :], in0=mps.ap[:, 0:2 * nn].rearrange("p (n b) -> p n b", b=2),
                                            in1=par.ap[:, P_ADAB + l * 48 + n0:P_ADAB + l * 48 + n1].rearrange("p (n o) -> p n o", o=1).to_broadcast([128, nn, 2]),
                                            op=OP.add), [mps, par], [modT])

    compute_mod(0)

    CO_S1, CO_G1, CO_A1, CO_A0, CO_B1, CO_B0, CO_C1, CO_C0 = range(8)

    def compute_coefs(l, b, last):
        m = modT.ap[:, l]
        md = lambda i: m[:, i * 8:(i + 1) * 8, b]
        co = lambda i: coef.ap[:, i, :]
        lng = lambda j: par.ap[:, P_LNG + (l * 2 + j) * 8:P_LNG + (l * 2 + j) * 8 + 8]
        lnb = lambda j: par.ap[:, P_LNB + (l * 2 + j) * 8:P_LNB + (l * 2 + j) * 8 + 8]
        rw = [modT, par, coef, ctmp]
        k.op(DVE, lambda: dve.tensor_scalar(out=co(CO_S1), in0=md(1), scalar1=1.0 / ALPHA, scalar2=1.0 / ALPHA, op0=OP.mult, op1=OP.add), rw, [coef])
        if l % 2 == 1:
            o = l // 2
            k.op(DVE, lambda: dve.tensor_tensor(out=co(CO_G1), in0=md(2), in1=par.ap[:, P_PSC + o * 8:P_PSC + o * 8 + 8], op=OP.mult), rw, [coef])
        else:
            k.op(DVE, lambda: dve.tensor_copy(co(CO_G1), md(2)), rw, [coef])
        k.op(DVE, lambda: dve.tensor_scalar(out=co(CO_A1), in0=lng(0), scalar1=ALPHA, scalar2=None, op0=OP.mult), rw, [coef])
        k.op(DVE, lambda: dve.tensor_scalar(out=co(CO_A0), in0=lnb(0), scalar1=ALPHA, scalar2=None, op0=OP.mult), rw, [coef])
        k.op(DVE, lambda: dve.tensor_scalar(out=ctmp.ap, in0=md(4), scalar1=1.0, scalar2=None, op0=OP.add), rw, [ctmp])
        k.op(DVE, lambda: dve.tensor_tensor(out=co(CO_B1), in0=lng(0), in1=ctmp.ap, op=OP.mult), rw, [coef])
        k.op(DVE, lambda: dve.tensor_tensor(out=co(CO_B0), in0=lnb(0), in1=ctmp.ap, op=OP.mult), rw, [coef])
        k.op(DVE, lambda: dve.tensor_tensor(out=co(CO_B0), in0=co(CO_B0), in1=md(3), op=OP.add), rw, [coef])
        a2 = 1.0 if last else ALPHA
        k.op(DVE, lambda: dve.tensor_scalar(out=co(CO_C1), in0=lng(1), scalar1=a2, scalar2=None, op0=OP.mult), rw, [coef])
        k.op(DVE, lambda: dve.tensor_scalar(out=co(CO_C0), in0=lnb(1), scalar1=a2, scalar2=None, op0=OP.mult), rw, [coef])

    cc = lambda i, kc: coef.ap[:, i, kc:kc + 1]

    def layer_norm(tt, c1, c0, hb=None, out_dma=None):
        a_reset()
        rb = [a_tile("rb", [128, TW], BF16) for _ in range(3)]
        rsq = [a_tile("rsq", [128, TW], BF16) for _ in range(3)]
        mu = a_tile("mu", [128, TW], F32)
        msq = a_tile("msq", [128, TW], F32)
        rstd = a_tile("rstd", [128, TW], F32)
        nrm = [a_tile("nrm", [128, TW], F32) for _ in range(3)]
        psum_s = pt(LNB["b"], 0, TW)
        psum_q = pt(LNB["b"], TW, 2 * TW)
        for n in range(8):
            x_t = xa[n][tt]
            k.op(DVE, lambda: dve.tensor_copy(rb[n % 3].ap, x_t.ap), [x_t], [rb[n % 3]])
            k.op(ACT, lambda: act.activation(rsq[n % 3].ap, x_t.ap, AF.Square), [x_t], [rsq[n % 3]])
            k.op(PE, lambda: pe.matmul(psum_s.ap, ones, rb[n % 3].ap, start=(n == 0), stop=False), [cb, rb[n % 3]], [psum_s], inc=True)
            k.op(PE, lambda: pe.matmul(psum_q.ap, ones, rsq[n % 3].ap, start=False, stop=(n == 7)), [cb, rsq[n % 3]], [psum_q], inc=True)
        k.op(ACT, lambda: act.activation(mu.ap, psum_s.ap, AF.Identity, scale=1.0 / D), [psum_s], [mu])
        k.op(DVE, lambda: dve.tensor_tensor(out=msq.ap, in0=mu.ap, in1=mu.ap, op=OP.mult), [mu], [msq])
        k.op(DVE, lambda: dve.scalar_tensor_tensor(out=rstd.ap, in0=psum_q.ap, scalar=1.0 / D, in1=msq.ap, op0=OP.mult, op1=OP.subtract), [psum_q, msq], [rstd])
        k.op(DVE, lambda: dve.tensor_scalar(out=rstd.ap, in0=rstd.ap, scalar1=LN_EPS, scalar2=None, op0=OP.add), [rstd], [rstd])
        k.op(ACT, lambda: act.activation(rstd.ap, rstd.ap, AF.Ln), [rstd], [rstd])
        k.op(ACT, lambda: act.activation(rstd.ap, rstd.ap, AF.Exp, scale=-0.5), [rstd], [rstd])
        for n in range(8):
            x_t = xa[n][tt]
            nm = nrm[n % 3]
            k.op(DVE, lambda: dve.tensor_tensor(out=nm.ap, in0=x_t.ap, in1=mu.ap, op=OP.subtract), [x_t, mu], [nm])
            k.op(DVE, lambda: dve.tensor_tensor(out=nm.ap, in0=nm.ap, in1=rstd.ap, op=OP.mult), [nm, rstd], [nm])
            k.op(POOL, lambda: pool.tensor_scalar(out=x_t.ap, in0=nm.ap, scalar1=cc(c1, n), scalar2=cc(c0, n), op0=OP.mult, op1=OP.add), [nm, coef], [x_t])
            if hb is not None:
                k.op(ACT, lambda: act.activation(Hh[n][tt].ap, nm.ap, AF.Identity, scale=cc(hb[0], n), bias=cc(hb[1], n)), [nm, coef], [Hh[n][tt]])
        if out_dma is not None:
            b = out_dma
            semt = xa[0][tt]
            k.dma(SP, None, out_d[b, :, :, tt * TW:(tt + 1) * TW], xa_h[:, :, tt * TW:(tt + 1) * TW], reads=[xa[n][tt] for n in range(8)], semt=semt)
            out_waits.append((semt.sem, semt.cnt))

    out_waits = []

    def proj_head(uA, hd):
        pq, pf = pt(0, 0, TW), pt(0, TW, 2 * TW)
        pg, pv = pt(1, 0, TW), pt(1, TW, 2 * TW)
        uq, uf, ui, ug = uA[hd * 4:hd * 4 + 4]
        for (u, po) in ((uf, pf), (ug, pg), (uq, pq)):
            for kc in range(8):
                k.op(PE, lambda: pe.matmul(po.ap, w3(u)[:, kc, :], h1[kc].ap, start=(kc == 0), stop=(kc == 7)), [u, h1[kc]], [po], inc=(kc == 7))
        for j in range(2):
            for kc in range(8):
                k.op(PE, lambda: pe.matmul(pv.ap[:, j * 128:(j + 1) * 128], h1[kc].ap[:, j * 128:(j + 1) * 128], w3(ui)[:, kc, :], start=(kc == 0), stop=(kc == 7)),
                     [ui, h1[kc]], [pv], inc=(kc == 7))

    def even_prologue(tt, units, b):
        for kc in range(8):
            k.op(ACT, lambda: act.activation(h1[kc].ap, xa[kc][tt].ap, AF.Identity, scale=cc(CO_S1, kc), bias=modT.ap[:, lcur["l"], kc, b:b + 1]),
                 [xa[kc][tt], coef, modT], [h1[kc]])
        proj_head(units[0:16], 0)

    def even_mixer_tile(e, tt, units, b):
        uA = units[0:16]
        uBq = units[16:20]
        uBk = units[20:22]
        uBv = units[22]
        uWo = units[23:31]
        a_reset()
        sig = a_tile("sig", [128, TW], F32)
        lf = a_tile("lf", [128, TW], F32)
        kk = a_tile("kk", [128, TW], F32)
        bb = a_tile("bb", [128, TW], F32)
        bcl = a_tile("bcl", [128, TW], F32)
        eb = a_tile("eb", [128, TW], F32)
        ebe = [a_tile("ebe", [128, 8], F32) for _ in range(4)]
        sg = [a_tile("sg", [128, TW], F32) for _ in range(4)]
        osb, rs = sig, lf
        Qt = [a_tile("Qt", [128, TW], BF16) for _ in range(4)]
        Kt = a_tile("Kt", [128, TW], BF16)
        Kh = [a_tile("Kh", [128, TW], BF16) for _ in range(4)]
        vsb = [a_tile("vsb", [128, 2, 128], BF16) for _ in range(4)]
        KhT = [a_tile("KhT", [128, 128], BF16) for _ in range(4)]
        osq = Kt
        pq, pf = pt(0, 0, TW), pt(0, TW, 2 * TW)
        pg, pv = pt(1, 0, TW), pt(1, TW, 2 * TW)
        patt = pt(2, 0, 128)
        pss = pt(2, 256, 512)
        pkh = pt(7, 0, 128)
        pS = [pt(3 + h % 2, (h // 2) * 128, (h // 2 + 1) * 128) for h in range(4)]
        pO = [pt(5 + h // 2, (h % 2) * TW, (h % 2 + 1) * TW) for h in range(4)]
        qs = a_tile("qs", [128, TW], F32)
        xtra = a_tile("xtra", [128, TW], F32)

        def proj(hd):
            proj_head(uA, hd)

        def chain(hd):
            k.op(ACT, lambda: act.activation(sig.ap, pf.ap, AF.Tanh, scale=0.5), [pf], [sig])
            k.op(ACT, lambda: act.activation(sg[hd].ap, pg.ap, AF.Silu), [pg], [sg[hd]])
            k.op(ACT, lambda: act.activation(qs.ap, pq.ap, AF.Copy), [pq], [qs])
            k.op(ACT, lambda: act.activation(vsb[hd].ap.rearrange("p j d -> p (j d)"), pv.ap, AF.Copy), [pv], [vsb[hd]])
            k.op(DVE, lambda: dve.tensor_scalar(out=sig.ap, in0=sig.ap, scalar1=lbt.ap[:, e, 1, hd:hd + 1], scalar2=lbt.ap[:, e, 0, hd:hd + 1], op0=OP.mult, op1=OP.add), [sig, lbt], [sig])
            k.op(ACT, lambda: act.activation(lf.ap, sig.ap, AF.Ln), [sig], [lf])
            k.op(DVE, lambda: dve.tensor_scalar(out=kk.ap, in0=sig.ap, scalar1=-1.0, scalar2=1.0, op0=OP.mult, op1=OP.add), [sig], [kk])
            k.op(DVE, lambda: dve.tensor_tensor_scan(bb.ap, cf.ap[:, CF_CMASK:CF_CMASK + TW], lf.ap, 0.0, OP.mult, OP.add), [cf, lf], [bb])
            k.op(DVE, lambda: dve.tensor_scalar(out=bcl.ap, in0=bb.ap, scalar1=-80.0, scalar2=None, op0=OP.max), [bb], [bcl])
            k.op(ACT, lambda: act.activation(eb.ap, bb.ap, AF.Exp), [bb], [eb])
            k.op(ACT, lambda: act.activation(bcl.ap, bcl.ap, AF.Exp, scale=-1.0), [bcl], [bcl])
            ebv = eb.ap.rearrange("p (c k) -> p c k", k=32)
            k.op(DVE, lambda: dve.tensor_tensor(out=Qt[hd].ap, in0=qs.ap, in1=eb.ap, op=OP.mult), [qs, eb], [Qt[hd]])
            k.op(DVE, lambda: dve.tensor_copy(ebe[hd].ap.rearrange("p (c o) -> p c o", o=1), ebv[:, :, 31:32]), [eb], [ebe[hd]])
            k.op(DVE, lambda: dve.tensor_tensor(out=Kt.ap, in0=kk.ap, in1=bcl.ap, op=OP.mult), [kk, bcl], [Kt])
            k.op(DVE, lambda: dve.tensor_tensor(out=Kh[hd].ap.rearrange("p (c k) -> p c k", k=32), in0=Kt.ap.rearrange("p (c k) -> p c k", k=32),
                                                in1=ebv[:, :, 31:32].to_broadcast([128, 8, 32]), op=OP.mult), [Kt, eb], [Kh[hd]])

        def scores(hd):
            for j in range(2):
                k.op(PE, lambda: pe.matmul(patt.ap, Kt.ap[:, j * 128:(j + 1) * 128], Qt[hd].ap[:, j * 128:(j + 1) * 128], start=True, stop=True), [Kt, Qt[hd]], [patt])
                att_j = attT2[hd][j]
                k.op(DVE, lambda: dve.tensor_tensor(out=att_j.ap, in0=patt.ap, in1=cf.ap[:, CF_MBD:CF_MBD + 128], op=OP.mult), [patt, cf], [att_j])

        for hd in range(4):
            chain(hd)
            if hd + 1 < 4:
                proj(hd + 1)
            scores(hd)
        for j in range(2):
            for hd in range(4):
                k.op(PE, lambda: pe.transpose(pkh.ap, Kh[hd].ap[:, j * 128:(j + 1) * 128], ident), [Kh[hd], cb], [pkh])
                k.op(ACT, lambda: act.activation(KhT[hd].ap, pkh.ap, AF.Copy), [pkh], [KhT[hd]])
                k.op(PE, lambda: pe.matmul(pO[hd].ap[:, j * 128:(j + 1) * 128], vsb[hd].ap[:, j, :], attT2[hd][j].ap, start=(j == 0 and hd % 2 == 0), stop=False), [vsb[hd], attT2[hd][j]], [pO[hd]], inc=False)
            for c in range(4):
                col = j * 128 + c * 32
                ci = j * 4 + c
                for hd in range(4):
                    k.op(PE, lambda: pe.matmul(pO[hd].ap[:, col:col + 32], Sbf[hd].ap, Qt[hd].ap[:, col:col + 32], start=False, stop=(c == 3 and j == 1 and hd % 2 == 1)), [Sbf[hd], Qt[hd]], [pO[hd]], inc=True)
                    k.op(PE, lambda: pe.matmul(pS[hd].ap, KhT[hd].ap[c * 32:(c + 1) * 32, :], vsb[hd].ap[c * 32:(c + 1) * 32, j, :], start=True, stop=True, tile_position=(c * 32, 0)),
                         [KhT[hd], vsb[hd]], [pS[hd]])
                    k.op(DVE, lambda: dve.scalar_tensor_tensor(out=S32[hd].ap, in0=S32[hd].ap, scalar=ebe[hd].ap[:, ci:ci + 1], in1=pS[hd].ap, op0=OP.mult, op1=OP.add), [S32[hd], ebe[hd], pS[hd]], [S32[hd]])
                    k.op(ACT, lambda: act.activation(Sbf[hd].ap, S32[hd].ap, AF.Copy), [S32[hd]], [Sbf[hd]])
        osbs = [sig, lf, kk, bb]
        rss = [bcl, eb, qs, xtra]
        osqs = Kh
        psss = [pt(h // 2, (h % 2) * TW, (h % 2 + 1) * TW) for h in range(4)]
        for hd in range(4):
            k.op(ACT, lambda: act.activation(osbs[hd].ap, pO[hd].ap, AF.Copy), [pO[hd]], [osbs[hd]])
            k.op(ACT, lambda: act.activation(osqs[hd].ap, pO[hd].ap, AF.Square), [pO[hd]], [osqs[hd]])
            k.op(PE, lambda: pe.matmul(psss[hd].ap, ones, osqs[hd].ap, start=True, stop=True), [cb, osqs[hd]], [psss[hd]])
        for hd in range(4):
            k.op(DVE, lambda: dve.tensor_scalar(out=rss[hd].ap, in0=psss[hd].ap, scalar1=128.0 * RMS_EPS, scalar2=None, op0=OP.add), [psss[hd]], [rss[hd]])
        for hd in range(4):
            k.op(ACT, lambda: act.activation(rss[hd].ap, rss[hd].ap, AF.Ln), [rss[hd]], [rss[hd]])
        for hd in range(4):
            k.op(ACT, lambda: act.activation(rss[hd].ap, rss[hd].ap, AF.Exp, scale=-0.5), [rss[hd]], [rss[hd]])
        for hd in range(4):
            k.op(DVE, lambda: dve.tensor_tensor(out=osbs[hd].ap, in0=osbs[hd].ap, in1=rss[hd].ap, op=OP.mult), [osbs[hd], rss[hd]], [osbs[hd]])
            k.op(DVE, lambda: dve.scalar_tensor_tensor(out=cat[hd].ap, in0=osbs[hd].ap, scalar=ngs.ap[:, e:e + 1], in1=sg[hd].ap, op0=OP.mult, op1=OP.mult), [osbs[hd], ngs, sg[hd]], [cat[hd]])

        a_reset()
        qr = [[a_tile("qr", [128, TW], BF16) for _ in range(2)] for _ in range(2)]
        b_mark = ar["off"]
        xs = [a_tile("xs", [128, TW], F32) for _ in range(2)]
        cosF = a_tile("cosF", [128, TW], F32)
        sinF = a_tile("sinF", [128, TW], F32)
        posf = a_tile("posf", [128, TW], F32)
        posi = a_tile("posi", [128, TW], I32)
        t1 = a_tile("t1", [128, TW], F32)
        t2 = a_tile("t2", [128, TW], F32)
        nint = a_tile("nint", [128, TW], I32)
        k.dma(SP, posi, posi.ap, pos_d[b:b + 1, tt * TW:(tt + 1) * TW].to_broadcast([128, TW]))
        k.op(DVE, lambda: dve.tensor_copy(posf.ap, posi.ap), [posi], [posf])

        def sin_table(dst, phcol, addcol):
            PI = float(np.pi)
            k.op(DVE, lambda: dve.tensor_scalar(out=t1.ap, in0=posf.ap, scalar1=cfc(CF_INVF2), scalar2=cfc(phcol), op0=OP.mult, op1=OP.add), [posf, cf], [t1])
            k.op(DVE, lambda: dve.tensor_copy(nint.ap, t1.ap), [t1], [nint])
            k.op(DVE, lambda: dve.tensor_copy(t1.ap, nint.ap), [nint], [t1])
            k.op(DVE, lambda: dve.tensor_scalar(out=t2.ap, in0=posf.ap, scalar1=cfc(CF_INVF), scalar2=cfc(addcol), op0=OP.mult, op1=OP.add), [posf, cf], [t2])
            k.op(DVE, lambda: dve.scalar_tensor_tensor(out=t2.ap, in0=t1.ap, scalar=-CW1, in1=t2.ap, op0=OP.mult, op1=OP.add), [t1, t2], [t2])
            k.op(DVE, lambda: dve.scalar_tensor_tensor(out=t2.ap, in0=t1.ap, scalar=-CW2, in1=t2.ap, op0=OP.mult, op1=OP.add), [t1, t2], [t2])
            k.op(DVE, lambda: dve.tensor_scalar(out=t1.ap, in0=t2.ap, scalar1=-PI, scalar2=2.0 * PI, op0=OP.is_lt, op1=OP.mult), [t2], [t1])
            k.op(DVE, lambda: dve.tensor_tensor(out=t2.ap, in0=t2.ap, in1=t1.ap, op=OP.add), [t1, t2], [t2])
            k.op(DVE, lambda: dve.tensor_scalar(out=t1.ap, in0=t2.ap, scalar1=PI, scalar2=-2.0 * PI, op0=OP.is_gt, op1=OP.mult), [t2], [t1])
            k.op(DVE, lambda: dve.tensor_tensor(out=t2.ap, in0=t2.ap, in1=t1.ap, op=OP.add), [t1, t2], [t2])
            k.op(DVE, lambda: dve.tensor_scalar(out=t2.ap, in0=t2.ap, scalar1=-PI, scalar2=PI, op0=OP.max, op1=OP.min), [t2], [t2])
            k.op(ACT, lambda: act.activation(dst.ap, t2.ap, AF.Sin), [t2], [dst])

        sin_table(sinF, CF_PHS, CF_ZERO)
        k.op(DVE, lambda: dve.tensor_scalar(out=sinF.ap, in0=sinF.ap, scalar1=cfc(CF_SGN), scalar2=None, op0=OP.mult), [sinF, cf], [sinF])
        sin_table(cosF, CF_PHC, CF_HPI)
        prot = pt(4, 0, TW)

        def rope_proj(u, pacc):
            for kc in range(8):
                k.op(PE, lambda: pe.matmul(pacc.ap, w3(u)[:, kc, :], h1[kc].ap, start=(kc == 0), stop=(kc == 7)), [u, h1[kc]], [pacc], inc=(kc == 7))

        def rope(u, pacc, xsb, dst_t, dst_ap, scale):
            k.op(ACT, lambda: act.activation(xsb.ap, pacc.ap, AF.Identity, scale=scale), [pacc], [xsb])
            k.op(PE, lambda: pe.matmul(prot.ap, cf.ap[:, CF_PERM:CF_PERM + 128], xsb.ap, start=True, stop=True), [cf, xsb], [prot])
            k.op(DVE, lambda: dve.tensor_tensor(out=t1.ap, in0=xsb.ap, in1=cosF.ap, op=OP.mult), [xsb, cosF], [t1])
            k.op(DVE, lambda: dve.tensor_tensor(out=t2.ap, in0=prot.ap, in1=sinF.ap, op=OP.mult), [prot, sinF], [t2])
            k.op(DVE, lambda: dve.tensor_tensor(out=dst_ap, in0=t1.ap, in1=t2.ap, op=OP.add), [t1, t2], [dst_t])

        pacc = [pt(0, 0, TW), pt(2, 0, TW)]
        pvv = pt(1, 0, 128)
        pob = [pt(1, 256, 384), pt(1, 384, 512)]
        ps4s = [[pt(5, 0, 512), pt(6, 0, 512)], [pt(2, 0, 512), pt(3, 0, 512)]]
        ppt = pt(7, 0, 1024)
        for g in range(2):
            if tt == 0:
                k.op(DVE, lambda: dve.memset(kTc[g].ap[:, 0:128], 0.0), [], [kTc[g]])
                k.op(DVE, lambda: dve.memset(vpad[g].ap, 0.0), [], [vpad[g]])
            else:
                k.op(DVE, lambda: dve.tensor_copy(kTc[g].ap[:, 0:128], kTc[g].ap[:, TW:TW + 128]), [kTc[g]], [kTc[g]])
                k.op(DVE, lambda: dve.tensor_copy(vpad[g].ap[:, 0, 64:128], vpad[g].ap[:, 2, 64:128]), [vpad[g]], [vpad[g]])
            rope_proj(uBk[g], pacc[0])
            rope_proj(uBq[g * 2], pacc[1])
            rope(uBk[g], pacc[0], xs[0], kTc[g], kTc[g].ap[:, 128:128 + TW], 1.0)
            rope_proj(uBq[g * 2 + 1], pacc[0])
            for j in range(2):
                for kc in range(8):
                    k.op(PE, lambda: pe.matmul(pvv.ap[:, j * 64:(j + 1) * 64], h1[kc].ap[:, j * 128:(j + 1) * 128], w3(uBv)[:, kc, g * 64:(g + 1) * 64], start=(kc == 0), stop=(kc == 7)),
                         [uBv, h1[kc]], [pvv], inc=(kc == 7))
            rope(uBq[g * 2], pacc[1], xs[1], qr[g][0], qr[g][0].ap, 0.125)
            rope(uBq[g * 2 + 1], pacc[0], xs[0], qr[g][1], qr[g][1].ap, 0.125)
            k.op(ACT, lambda: act.activation(vpad[g].ap[:, 1:3, 64:128], pvv.ap.rearrange("p (j d) -> p j d", d=64), AF.Copy), [pvv], [vpad[g]])
        ar["off"] = b_mark
        pexs = [a_tile("pex", [128, 4, 256], F32) for _ in range(2)]
        pns = [a_tile("pn", [128, 4, 256], BF16) for _ in range(2)]
        pT = a_tile("pT", [128, 8, 128], BF16)
        smalls = [[a_tile("sml", [128, 4], F32) for _ in range(4)] for _ in range(2)]
        blks = [(g, j) for g in range(2) for j in range(2)]

        def stageA(bi):
            g, j = blks[bi]
            ps4 = ps4s[bi % 2]
            pex = pexs[bi % 2]
            mx, ngm, sm, es = smalls[bi % 2]
            gblk = tt * 2 + j
            msk = cb.ap[:, CB_MS0:CB_MS0 + 256] if gblk == 0 else cb.ap[:, CB_MS:CB_MS + 256]
            for hl in range(4):
                c, rbp = hl // 2, (hl % 2) * 64
                so = ps4[hl // 2].ap[:, (hl % 2) * 256:(hl % 2 + 1) * 256]
                k.op(PE, lambda: pe.matmul(so, qr[g][c].ap[rbp:rbp + 64, j * 128:(j + 1) * 128], kTc[g].ap[rbp:rbp + 64, j * 128:j * 128 + 256], start=True, stop=False),
                     [qr[g][c], kTc[g]], [ps4[hl // 2]], inc=False)
                k.op(PE, lambda: pe.matmul(so, ident, msk, start=False, stop=True), [cb], [ps4[hl // 2]], inc=True)
            for pr in range(2):
                k.op(DVE, lambda: dve.tensor_reduce(out=mx.ap[:, pr * 2:pr * 2 + 2], in_=ps4[pr].ap.rearrange("p (h k) -> p h k", k=256), axis=AX.X, op=OP.max), [ps4[pr]], [mx])
            snk = par.ap[:, P_SINK + e * 8 + g * 4:P_SINK + e * 8 + g * 4 + 4]
            k.op(DVE, lambda: dve.tensor_tensor(out=mx.ap, in0=mx.ap, in1=snk, op=OP.max), [mx, par], [mx])
            k.op(DVE, lambda: dve.tensor_scalar(out=ngm.ap, in0=mx.ap, scalar1=-1.0, scalar2=None, op0=OP.mult), [mx], [ngm])
            k.op(DVE, lambda: dve.memset(sm.ap, 0.0), [], [sm])
            for hl in range(4):
                so = ps4[hl // 2].ap[:, (hl % 2) * 256:(hl % 2 + 1) * 256]
                k.op(ACT, lambda: act.activation(pex.ap[:, hl, :], so, AF.Exp, bias=ngm.ap[:, hl:hl + 1], accum_out=sm.ap[:, hl:hl + 1]), [ps4[hl // 2], ngm], [pex, sm])
            k.op(DVE, lambda: dve.tensor_tensor(out=es.ap, in0=snk, in1=ngm.ap, op=OP.add), [par, ngm], [es])
            k.op(ACT, lambda: act.activation(es.ap, es.ap, AF.Exp), [es], [es])

        def stageB(bi):
            g, j = blks[bi]
            pex, pn = pexs[bi % 2], pns[bi % 2]
            mx, ngm, sm, es = smalls[bi % 2]
            k.op(DVE, lambda: dve.tensor_tensor(out=es.ap, in0=es.ap, in1=sm.ap, op=OP.add), [es, sm], [es])
            k.op(DVE, lambda: dve.reciprocal(es.ap, es.ap), [es], [es])
            k.op(DVE, lambda: dve.tensor_tensor(out=pn.ap, in0=pex.ap, in1=es.ap.rearrange("p (h o) -> p h o", o=1).to_broadcast([128, 4, 256]), op=OP.mult), [pex, es], [pn])
            for hl in range(4):
                for hf in range(2):
                    i8 = hl * 2 + hf
                    k.op(PE, lambda: pe.transpose(ppt.ap[:, i8 * 128:(i8 + 1) * 128], pn.ap[:, hl, hf * 128:(hf + 1) * 128], ident), [pn, cb], [ppt], inc=(i8 == 7))
            k.op(ACT, lambda: act.activation(pT.ap.rearrange("p a q -> p (a q)"), ppt.ap, AF.Copy), [ppt], [pT])
            for pr in range(2):
                n4 = 0
                for hl in (2 * pr, 2 * pr + 1):
                    off = 64 if hl % 2 == 0 else 0
                    for hf in range(2):
                        k.op(PE, lambda: pe.matmul(pob[pr].ap, vpad[g].ap[:, j + hf, off:off + 128], pT.ap[:, hl * 2 + hf, :], start=(n4 == 0), stop=(n4 == 3)),
                             [vpad[g], pT], [pob[pr]], inc=(n4 == 3))
                        n4 += 1
                ct = cat[4 + g * 2 + pr]
                k.op(ACT, lambda: act.activation(ct.ap[:, j * 128:(j + 1) * 128], pob[pr].ap, AF.Copy), [pob[pr]], [ct])

        stageA(0)
        for bi in range(4):
            if bi + 1 < 4:
                stageA(bi + 1)
            stageB(bi)
        py = [pt(2, 0, TW), pt(3, 0, TW)]
        for n in range(8):
            po = py[n % 2]
            for kc in range(8):
                k.op(PE, lambda: pe.matmul(po.ap, w3(uWo[n])[:, kc, :], cat[kc].ap, start=(kc == 0), stop=(kc == 7)), [uWo[n], cat[kc]], [po], inc=(kc == 7))
            x_t = xa[n][tt]
            k.op(DVE, lambda: dve.scalar_tensor_tensor(out=x_t.ap, in0=po.ap, scalar=cc(CO_G1, n), in1=x_t.ap, op0=OP.mult, op1=OP.add), [po, coef, x_t], [x_t])

    attT2 = [[k.tile("attT2_%d_%d" % (h, j), [128, 128], BF16) for j in range(2)] for h in range(4)]
    assert k.sb_off - 16384 <= 210784, k.sb_off - 16384
    print("sbuf bytes used", k.sb_off - 16384, "of 212863")

    def odd_mixer_tile(o, tt, units, b):
        a_reset()
        hf = [a_tile("hf", [128, TW], F32) for _ in range(2)]
        Pb = [a_tile("Pb", [128, 16 + TW], F32) for _ in range(2)]
        wsm = [a_tile("wsm", [128, TW], F32) for _ in range(2)]
        pl = [a_tile("pl", [128, TW], F32) for _ in range(2)]
        fx = a_tile("fx", [128, 16], F32)
        onesb = cf.ap[:, CF_CMASK + 1:CF_CMASK + 2].to_broadcast([128, TW])
        def st1(kc):
            hh = hf[kc % 2]
            k.op(ACT, lambda: act.activation(hh.ap, xa[kc][tt].ap, AF.Identity, scale=cc(CO_S1, kc), bias=modT.ap[:, lcur["l"], kc, b:b + 1]),
                 [xa[kc][tt], coef, modT], [hh])

        def st2(kc):
            gi = kc // 2
            w = (2, 4, 8, 16)[gi]
            hh, P, wv, pp = hf[kc % 2], Pb[kc % 2], wsm[kc % 2], pl[kc % 2]
            if tt == 0:
                k.op(DVE, lambda: dve.memset(P.ap[:, 0:16], 0.0), [], [P])
            else:
                k.op(DVE, lambda: dve.tensor_copy(P.ap[:, 0:16], phalo.ap[:, kc, :]), [phalo], [P])
            init = 0.0 if tt == 0 else phalo.ap[:, kc, 15:16]
            k.op(DVE, lambda: dve.tensor_tensor_scan(P.ap[:, 16:16 + TW], onesb, hh.ap, init, OP.mult, OP.add), [cf, hh, phalo], [P])
            k.op(DVE, lambda: dve.tensor_copy(phalo.ap[:, kc, :], P.ap[:, TW:TW + 16]), [P], [phalo])
            k.op(POOL, lambda: pool.tensor_tensor(out=wv.ap, in0=P.ap[:, 16:16 + TW], in1=P.ap[:, 16 - w:16 - w + TW], op=OP.subtract), [P], [wv])
            k.op(DVE, lambda: dve.scalar_tensor_tensor(out=pp.ap, in0=wv.ap, scalar=1.0 / w, in1=hh.ap, op0=OP.mult, op1=OP.subtract), [wv, hh], [pp])
            if tt == 0:
                k.op(DVE, lambda: dve.tensor_tensor(out=fx.ap, in0=wv.ap[:, 0:16], in1=cf.ap[:, CF_PINV + gi * 16:CF_PINV + (gi + 1) * 16], op=OP.mult), [wv, cf], [fx])
                k.op(DVE, lambda: dve.tensor_tensor(out=pp.ap[:, 0:16], in0=fx.ap, in1=hh.ap[:, 0:16], op=OP.subtract), [fx, hh], [pp])

        def st3(kc):
            pp = pl[kc % 2]
            k.op(ACT, lambda: act.activation(h1[kc].ap, pp.ap, AF.Copy), [pp], [h1[kc]])

        st1(0)
        for kc in range(8):
            st2(kc)
            if kc + 1 < 8:
                st1(kc + 1)
            st3(kc)
        py = [pt(2, 0, TW), pt(3, 0, TW)]
        for n in range(8):
            gi, mm_ = n // 2, n % 2
            u = units[gi // 2]
            gl = gi % 2
            po = py[n % 2]
            for kc in range(2):
                k.op(PE, lambda: pe.matmul(po.ap, w3(u)[:, (gl * 2 + mm_) * 2 + kc, :], h1[gi * 2 + kc].ap, start=(kc == 0), stop=(kc == 1)), [u, h1[gi * 2 + kc]], [po], inc=(kc == 1))
            x_t = xa[n][tt]
            k.op(DVE, lambda: dve.scalar_tensor_tensor(out=x_t.ap, in0=po.ap, scalar=cc(CO_G1, n), in1=x_t.ap, op0=OP.mult, op1=OP.add), [po, coef, x_t], [x_t])

    def ffn_phase(l, b, last):
        for gidx, grp in enumerate(FF_GROUPS):
            ng = len(grp)
            units = ws_get(3 * ng, ["upu", "upv", "down"] * ng)
            lastg = gidx == len(FF_GROUPS) - 1
            a_reset()
            if lastg:
                ar["off"] = arena0 + 9216
            DP = 2 if lastg else 3
            ub = [[a_tile("ub", [128, 2 + TW], F32) for _ in range(2)] for _ in range(ng)]
            acc = [a_tile("acc", [128, TW], F32) for _ in range(DP)]
            gt = [a_tile("gt", [128, TW], BF16) for _ in range(DP)]
            puv = [(pt(4 + d, 0, TW), pt(4 + d, TW, 2 * TW)) for d in range(DP)]
            py = [pt(m // 2, (m % 2) * TW, (m % 2 + 1) * TW) for m in range(8)]
            items = [(tt, jl) for tt in range(NT) for jl in range(ng)]

            def up(i):
                tt, jl = items[i]
                uu, uv = units[3 * jl], units[3 * jl + 1]
                pu, pvv = puv[i % DP]
                for (u, po) in ((uu, pu), (uv, pvv)):
                    for kc in range(8):
                        k.op(PE, lambda: pe.matmul(po.ap, w3(u)[:, kc, :], Hh[kc][tt].ap, start=(kc == 0), stop=(kc == 7)), [u, Hh[kc][tt]], [po], inc=(kc == 7))

            def mid(i, stage):
                tt, jl = items[i]
                j = grp[jl]
                pu, pvv = puv[i % DP]
                u_c = ub[jl][tt % 2]
                u_n = ub[jl][(tt + 1) % 2]
                a_ = acc[i % DP]
                cw = lambda t3: par.ap[:, P_CW + (l * 3 + t3) * 22 + j:P_CW + (l * 3 + t3) * 22 + j + 1]
                cbias = par.ap[:, P_CB + l * 22 + j:P_CB + l * 22 + j + 1]
                if stage == 1:
                    if tt == 0:
                        k.op(DVE, lambda: dve.memset(u_c.ap[:, 0:2], 0.0), [], [u_c])
                    k.op(ACT, lambda: act.activation(u_c.ap[:, 2:2 + TW], pu.ap, AF.Copy), [pu], [u_c])
                    k.op(ACT, lambda: act.activation(a_.ap, pu.ap, AF.Identity, scale=cw(2), bias=cbias), [pu, par], [a_])
                elif stage == 2:
                    k.op(DVE, lambda: dve.scalar_tensor_tensor(out=a_.ap, in0=u_c.ap[:, 0:TW], scalar=cw(0), in1=a_.ap, op0=OP.mult, op1=OP.add), [u_c, par, a_], [a_])
                    k.op(DVE, lambda: dve.scalar_tensor_tensor(out=a_.ap, in0=u_c.ap[:, 1:1 + TW], scalar=cw(1), in1=a_.ap, op0=OP.mult, op1=OP.add), [u_c, par, a_], [a_])
                else:
                    k.op(ACT, lambda: act.activation(u_n.ap[:, 0:2], u_c.ap[:, TW:TW + 2], AF.Copy), [u_c], [u_n])

            def midB(i):
                pu, pvv = puv[i % DP]
                a_ = acc[i % DP]
                k.op(ACT, lambda: act.activation(a_.ap, a_.ap, AF.Silu), [a_], [a_])
                k.op(DVE, lambda: dve.tensor_tensor(out=gt[i % DP].ap, in0=a_.ap, in1=pvv.ap, op=OP.mult), [a_, pvv], [gt[i % DP]])

            def down(i):
                tt, jl = items[i]
                ud = units[3 * jl + 2]
                for m in range(8):
                    k.op(PE, lambda: pe.matmul(py[m].ap, ud.ap[:, m * 128:(m + 1) * 128], gt[i % DP].ap, start=(jl == 0 and m % 2 == 0), stop=(jl == ng - 1 and m % 2 == 1)), [ud, gt[i % DP]], [py[m]], inc=(m == 7 or (jl == ng - 1 and m % 2 == 1)))
                if jl == ng - 1:
                    def evac(tt=tt):
                        for m in range(8):
                            x_t = xa[m][tt]
                            g2c = modT.ap[:, l, 40 + m, b:b + 1]
                            if m < 4 and OPT_EVAC:
                                tm = ytmp[m]
                                k.op(ACT, lambda: act.activation(tm.ap, py[m].ap, AF.Identity, scale=g2c, bias=cfc(CF_ZERO)), [py[m], modT, cf], [tm])
                                pend.append((x_t, tm))
                            else:
                                k.op(DVE, lambda: dve.scalar_tensor_tensor(out=x_t.ap, in0=py[m].ap, scalar=g2c, in1=x_t.ap, op0=OP.mult, op1=OP.add),
                                     [py[m], modT, x_t], [x_t])
                    if lastg:
                        evac()
                        flush_pend()
                    else:
                        evq.append(evac)
                    if lastg:
                        save = ar["off"]
                        layer_norm(tt, CO_C1, CO_C0, hb=None, out_dma=(outb["b"] if last else None))
                        ar["off"] = save

            ytmp = [a_tile("ytmp", [128, TW], F32) for _ in range(4)]
            pend = []
            evq = []

            def flush_pend():
                for (x_t, tm) in pend:
                    if OPT_POOLADD:
                        k.op(POOL, lambda: pool.tensor_tensor(out=x_t.ap, in0=x_t.ap, in1=tm.ap, op=OP.add), [x_t, tm], [x_t])
                    else:
                        k.op(DVE, lambda: dve.tensor_tensor(out=x_t.ap, in0=x_t.ap, in1=tm.ap, op=OP.add), [x_t, tm], [x_t])
                del pend[:]

            n_it = len(items)
            for d in range(DP - 1):
                if d < n_it:
                    up(d)
            HOIST = DP >= 3
            if HOIST:
                mid(0, 1)
            LAG = 2 if HOIST else 1
            for i in range(n_it + LAG):
                if i < n_it:
                    if not HOIST:
                        mid(i, 1)
                    mid(i, 2)
                    if HOIST and i + 1 < n_it:
                        mid(i + 1, 1)
                    mid(i, 3)
                    midB(i)
                for f in evq:
                    f()
                del evq[:]
                flush_pend()
                if i + DP - 1 < n_it:
                    up(i + DP - 1)
                if i >= LAG:
                    down(i - LAG)
            for f in evq:
                f()
            del evq[:]
            flush_pend()
            ws_done(3 * ng)
            if b == 0 and l + 1 < n_layers:
                compute_mod(l + 1, gidx * 8, gidx * 8 + 8)

    outb = {"b": 0}
    lcur = {"l": 0}

    for s in range(n_seq):
        outb["b"] = s
        for tt in range(NT):
            k.dma(SP, [xa[kc][tt] for kc in range(8)], xa_h[:, :, tt * TW:(tt + 1) * TW], x_d[s, :, :, tt * TW:(tt + 1) * TW], reads=[], semt=xa[0][tt])
        for tt in range(NT):
            for kc in range(8):
                k.op(ACT if kc % 2 else DVE,
                     (lambda kc=kc, tt=tt: act.activation(xa[kc][tt].ap, xa[kc][tt].ap, AF.Identity, scale=ALPHA)) if kc % 2 else
                     (lambda kc=kc, tt=tt: dve.tensor_scalar(out=xa[kc][tt].ap, in0=xa[kc][tt].ap, scalar1=ALPHA, scalar2=None, op0=OP.mult)),
                     [xa[kc][tt]], [xa[kc][tt]])
        for l in range(n_layers):
            lcur["l"] = l
            last = l == n_layers - 1
            compute_coefs(l, s, last)
            if l % 2 == 0:
                e = l // 2
                for hd in range(4):
                    k.op(DVE, lambda: dve.memset(S32[hd].ap, 0.0), [], [S32[hd]])
                    k.op(DVE, lambda: dve.memset(Sbf[hd].ap, 0.0), [], [Sbf[hd]])
                units = ws_get(31)
                even_prologue(0, units, s)
                for tt in range(NT):
                    even_mixer_tile(e, tt, units, s)
                    if tt + 1 < NT:
                        even_prologue(tt + 1, units, s)
                    layer_norm(tt, CO_A1, CO_A0, hb=(CO_B1, CO_B0))
                ws_done(31)
            else:
                o = l // 2
                units = ws_get(2, ["pool", "pool"])
                for tt in range(NT):
                    odd_mixer_tile(o, tt, units, s)
                    layer_norm(tt, CO_A1, CO_A0, hb=(CO_B1, CO_B0))
                ws_done(2)
            ffn_phase(l, s, last)
    for (sem, val) in out_waits:
        nc.sync.wait_ge(sem, val)
    assert ws["cur"] == U, (ws["cur"], U)
    return nc, plan


_CACHE = {}


def kernel(**inputs):
    inp = {kk: np.asarray(v) for kk, v in inputs.items()}
    if "prog" not in _CACHE:
        _CACHE["prog"] = build_program()
    nc, plan = _CACHE["prog"]
    wstream = materialize_stream(plan, inp)
    cf, cb = make_consts()
    x = inp["x"].astype(np.float32)
    pos = inp["positions"].astype(np.int32)
    in_maps = []
    for c in range(NCORES):
        xc = x[2 * c:2 * c + 2]
        xfm = np.ascontiguousarray(xc.transpose(0, 2, 1).reshape(2, 8, 128, T).transpose(0, 2, 1, 3))
        in_maps.append({"xfm": xfm, "pos": np.ascontiguousarray(pos[2 * c:2 * c + 2]), "cf": cf, "cb": cb,
                        "par": make_params(inp, c), "wstream": wstream})
    res = run_bass_kernel_spmd(nc, in_maps, core_ids=list(range(NCORES)))
    out = np.zeros((16, T, D), np.float32)
    for c in range(NCORES):
        o = np.asarray(res.results[c]["ofm"])
        out[2 * c:2 * c + 2] = o.transpose(0, 2, 1, 3).reshape(2, D, T).transpose(0, 2, 1)
    return out
```
